# Optimizing a Trainium2 kernel written in Bass

```python
import math
import jax, jax.numpy as jnp
from jax import lax
import numpy as np

D_MODEL = 1024
BATCH = 16
SEQ = 4096
DEPTH = 1

CHUNK = 64
D_MIX = D_MODEL
S5_WIDTH = D_MIX // 2
S5_GROUP = 16
S5_GROUPS = S5_WIDTH // S5_GROUP
S5_STATE = 64
S5_DT_MIN = 0.001
S5_DT_MAX = 0.1
RET_WIDTH = D_MIX - S5_WIDTH
RET_HEADS = 4
RET_DV = RET_WIDTH // RET_HEADS
RET_DK = RET_DV // 2
D_IN = S5_WIDTH + 2 * RET_HEADS * RET_DK + 2 * RET_WIDTH
D_FF = 2816
CONV_W = 3
ROPE_BASE = 10000.0
LN_EPS = 1e-5
DEEPNORM_ALPHA = (2.0 * DEPTH) ** 0.25
DEEPNORM_BETA = (8.0 * DEPTH) ** -0.25

kernel_name = "hymba_s5_retnet_convffn_deepnorm"


def layer_norm(x, g, b):
    x32 = x.astype(jnp.float32)
    mu = jnp.mean(x32, axis=-1, keepdims=True)
    xc = x32 - mu
    var = jnp.mean(jnp.square(xc), axis=-1, keepdims=True)
    y = xc * lax.rsqrt(var + LN_EPS) * g.astype(jnp.float32) + b.astype(jnp.float32)
    return y.astype(x.dtype)


def rms_norm(x, g):
    x32 = x.astype(jnp.float32)
    y = x32 * lax.rsqrt(jnp.mean(jnp.square(x32), axis=-1, keepdims=True) + LN_EPS)
    return y * g.astype(jnp.float32)


def rotary(x, pos):
    half = x.shape[-1] // 2
    freqs = ROPE_BASE ** (-jnp.arange(half, dtype=jnp.float32) / half)
    ang = pos[:, None] * freqs[None, :]
    cos = jnp.cos(ang)[None, :, None, :]
    sin = jnp.sin(ang)[None, :, None, :]
    x1, x2 = x[..., :half], x[..., half:]
    return jnp.concatenate([x1 * cos - x2 * sin, x1 * sin + x2 * cos], axis=-1)


def _ssm_combine(left, right):
    a_l, b_l = left
    a_r, b_r = right
    return a_r * a_l, a_r * b_l + b_r


def s5_mixer(u, lam_re, lam_im, b_re, b_im, c_re, c_im, d, log_step, glu_w, glu_b):
    bsz, seq_len, _ = u.shape
    f32 = jnp.float32
    u32 = u.astype(f32).reshape(bsz, seq_len, S5_GROUPS, S5_GROUP)
    lam = lax.complex(lam_re.astype(f32), lam_im.astype(f32))
    dt = jnp.exp(log_step.astype(f32))[:, None]
    lam_bar = jnp.exp(lam * dt)
    b = lax.complex(b_re.astype(f32), b_im.astype(f32))
    b_bar = ((lam_bar - 1.0) / lam)[..., None] * b
    bu = jnp.einsum('blgc,gpc->blgp', u32.astype(jnp.complex64), b_bar)
    a_elems = jnp.broadcast_to(lam_bar[None, None], (1, seq_len, S5_GROUPS, S5_STATE))
    _, states = lax.associative_scan(_ssm_combine, (a_elems, bu), axis=1)
    c = lax.complex(c_re.astype(f32), c_im.astype(f32))
    y = jnp.real(jnp.einsum('blgp,gcp->blgc', states, c))
    y = y + d.astype(f32).reshape(S5_GROUPS, S5_GROUP) * u32
    y = jax.nn.gelu(y.reshape(bsz, seq_len, S5_WIDTH))
    return y * jax.nn.sigmoid(y @ glu_w.astype(f32) + glu_b.astype(f32))


def retention(q, k, v, gn_gain):
    bsz, seq_len = q.shape[0], q.shape[1]
    n_chunks = seq_len // CHUNK
    f32 = jnp.float32
    log_gamma = jnp.log1p(-(2.0 ** (-5.0 - jnp.arange(RET_HEADS, dtype=f32))))
    k = k * (RET_DK ** -0.5)
    qc = q.reshape(bsz, n_chunks, CHUNK, RET_HEADS, RET_DK)
    kc = k.reshape(bsz, n_chunks, CHUNK, RET_HEADS, RET_DK)
    vc = v.reshape(bsz, n_chunks, CHUNK, RET_HEADS, RET_DV)
    pos = jnp.arange(CHUNK, dtype=f32)
    intra_decay = jnp.exp(jnp.abs(pos[:, None] - pos[None, :])[None] * log_gamma[:, None, None])
    scores = jnp.einsum('bncht,bnmht->bhncm', qc, kc) * intra_decay[:, None]
    intra = jnp.einsum('bhncm,bnmhe->bnche', scores, vc)
    zeta = jnp.exp((CHUNK - 1.0 - pos)[None, :] * log_gamma[:, None])
    xi = jnp.exp((pos + 1.0)[None, :] * log_gamma[:, None])
    kv = jnp.einsum('bnmht,hm,bnmhe->nbhte', kc, zeta, vc)
    chunk_decay = jnp.exp(CHUNK * log_gamma)[:, None, None]

    def step(state, kv_n):
        return state * chunk_decay + kv_n, state

    init = jnp.zeros((bsz, RET_HEADS, RET_DK, RET_DV), f32)
    _, states_before = lax.scan(step, init, kv)
    cross = jnp.einsum('bncht,hc,nbhte->bnche', qc, xi, states_before)
    out = intra + cross
    mu = jnp.mean(out, axis=-1, keepdims=True)
    var = jnp.mean(jnp.square(out - mu), axis=-1, keepdims=True)
    out = (out - mu) * lax.rsqrt(var + LN_EPS) * gn_gain.astype(f32).reshape(RET_HEADS, RET_DV)
    return out.reshape(bsz, seq_len, RET_WIDTH)


def hybrid_mixer(x, w_in, lam_re, lam_im, b_re, b_im, c_re, c_im, d, log_step,
                 glu_w, glu_b, s5_gain, gn_gain, w_out):
    bsz, seq_len, _ = x.shape
    h = x @ w_in
    o1 = S5_WIDTH
    o2 = o1 + RET_HEADS * RET_DK
    o3 = o2 + RET_HEADS * RET_DK
    o4 = o3 + RET_WIDTH
    u_s5 = h[..., :o1]
    pos = jnp.arange(seq_len, dtype=jnp.float32)
    q = rotary(h[..., o1:o2].astype(jnp.float32).reshape(bsz, seq_len, RET_HEADS, RET_DK), pos)
    k = rotary(h[..., o2:o3].astype(jnp.float32).reshape(bsz, seq_len, RET_HEADS, RET_DK), pos)
    v = h[..., o3:o4].astype(jnp.float32).reshape(bsz, seq_len, RET_HEADS, RET_DV)
    gate = h[..., o4:].astype(jnp.float32)
    y_s5 = rms_norm(s5_mixer(u_s5, lam_re, lam_im, b_re, b_im, c_re, c_im, d, log_step, glu_w, glu_b), s5_gain)
    y_ret = retention(q, k, v, gn_gain) * jax.nn.silu(gate)
    y = jnp.concatenate([y_s5, y_ret], axis=-1).astype(x.dtype)
    return y @ w_out


def conv_ffn(x, w_up, conv_w, conv_b, w_down):
    seq_len = x.shape[1]
    ug = x @ w_up
    a, g = ug[..., :D_FF], ug[..., D_FF:]
    a_pad = jnp.pad(a, ((0, 0), (CONV_W - 1, 0), (0, 0)))
    a = sum(conv_w[i] * a_pad[:, i:i + seq_len] for i in range(CONV_W)) + conv_b
    return (jax.nn.silu(a) * g) @ w_down


def setup_inputs(seed: int = 0) -> dict:
    key = jax.random.key(seed)
    ks = jax.random.split(key, 24)
    f32 = jnp.float32
    nrm = lambda k, shape, s: jax.random.normal(k, shape, f32) * s
    x = jax.random.normal(ks[0], (BATCH, SEQ, D_MODEL), f32)
    col_scale = jnp.concatenate([
        jnp.ones((S5_WIDTH + 2 * RET_HEADS * RET_DK,), f32),
        jnp.full((RET_WIDTH,), DEEPNORM_BETA, f32),
        jnp.ones((RET_WIDTH,), f32)])
    w_in = nrm(ks[1], (DEPTH, D_MODEL, D_IN), D_MODEL ** -0.5) * col_scale
    s5_lambda_re = -0.5 + nrm(ks[2], (DEPTH, S5_GROUPS, S5_STATE), 0.01)
    s5_lambda_im = math.pi * jnp.arange(S5_STATE, dtype=f32)[None, None, :] + nrm(ks[3], (DEPTH, S5_GROUPS, S5_STATE), 0.01)
    s5_b_re = nrm(ks[4], (DEPTH, S5_GROUPS, S5_STATE, S5_GROUP), (2.0 * S5_GROUP) ** -0.5)
    s5_b_im = nrm(ks[5], (DEPTH, S5_GROUPS, S5_STATE, S5_GROUP), (2.0 * S5_GROUP) ** -0.5)
    s5_c_re = nrm(ks[6], (DEPTH, S5_GROUPS, S5_GROUP, S5_STATE), (2.0 * S5_STATE) ** -0.5)
    s5_c_im = nrm(ks[7], (DEPTH, S5_GROUPS, S5_GROUP, S5_STATE), (2.0 * S5_STATE) ** -0.5)
    s5_d = nrm(ks[8], (DEPTH, S5_WIDTH), 1.0)
    s5_log_step = jax.random.uniform(ks[9], (DEPTH, S5_GROUPS), f32, math.log(S5_DT_MIN), math.log(S5_DT_MAX))
    s5_glu_w = nrm(ks[10], (DEPTH, S5_WIDTH, S5_WIDTH), S5_WIDTH ** -0.5)
    s5_glu_b = nrm(ks[11], (DEPTH, S5_WIDTH), 0.01)
    s5_out_gain = 1.0 + nrm(ks[12], (DEPTH, S5_WIDTH), 0.01)
    ret_gn_gain = 1.0 + nrm(ks[13], (DEPTH, RET_WIDTH), 0.01)
    w_out = nrm(ks[14], (DEPTH, D_MIX, D_MODEL), D_MIX ** -0.5 * DEEPNORM_BETA)
    ln1_g = 1.0 + nrm(ks[15], (DEPTH, D_MODEL), 0.01)
    ln1_b = nrm(ks[16], (DEPTH, D_MODEL), 0.01)
    ffn_w_up = nrm(ks[17], (DEPTH, D_MODEL, 2 * D_FF), D_MODEL ** -0.5)
    ffn_conv_w = nrm(ks[18], (DEPTH, CONV_W, D_FF), CONV_W ** -0.5)
    ffn_conv_b = nrm(ks[19], (DEPTH, D_FF), 0.01)
    ffn_w_down = nrm(ks[20], (DEPTH, D_FF, D_MODEL), D_FF ** -0.5 * DEEPNORM_BETA)
    ln2_g = 1.0 + nrm(ks[21], (DEPTH, D_MODEL), 0.01)
    ln2_b = nrm(ks[22], (DEPTH, D_MODEL), 0.01)
    return {"x": x, "w_in": w_in, "s5_lambda_re": s5_lambda_re, "s5_lambda_im": s5_lambda_im,
            "s5_b_re": s5_b_re, "s5_b_im": s5_b_im, "s5_c_re": s5_c_re, "s5_c_im": s5_c_im,
            "s5_d": s5_d, "s5_log_step": s5_log_step, "s5_glu_w": s5_glu_w, "s5_glu_b": s5_glu_b,
            "s5_out_gain": s5_out_gain, "ret_gn_gain": ret_gn_gain, "w_out": w_out,
            "ln1_g": ln1_g, "ln1_b": ln1_b, "ffn_w_up": ffn_w_up, "ffn_conv_w": ffn_conv_w,
            "ffn_conv_b": ffn_conv_b, "ffn_w_down": ffn_w_down, "ln2_g": ln2_g, "ln2_b": ln2_b}


def reference(x, w_in, s5_lambda_re, s5_lambda_im, s5_b_re, s5_b_im, s5_c_re, s5_c_im,
              s5_d, s5_log_step, s5_glu_w, s5_glu_b, s5_out_gain, ret_gn_gain, w_out,
              ln1_g, ln1_b, ffn_w_up, ffn_conv_w, ffn_conv_b, ffn_w_down, ln2_g, ln2_b):
    for l in range(DEPTH):
        mix = hybrid_mixer(x, w_in[l], s5_lambda_re[l], s5_lambda_im[l], s5_b_re[l], s5_b_im[l],
                           s5_c_re[l], s5_c_im[l], s5_d[l], s5_log_step[l], s5_glu_w[l], s5_glu_b[l],
                           s5_out_gain[l], ret_gn_gain[l], w_out[l])
        x = layer_norm(DEEPNORM_ALPHA * x + mix, ln1_g[l], ln1_b[l])
        ffn = conv_ffn(x, ffn_w_up[l], ffn_conv_w[l], ffn_conv_b[l], ffn_w_down[l])
        x = layer_norm(DEEPNORM_ALPHA * x + ffn, ln2_g[l], ln2_b[l])
    return x
```

```python
import math
from contextlib import ExitStack
import numpy as np
import ml_dtypes
import concourse.bass as bass
import concourse.mybir as mybir
from concourse.bass_utils import run_bass_kernel_spmd

F32 = mybir.dt.float32
BF16 = mybir.dt.bfloat16
AF = mybir.ActivationFunctionType
ALU = mybir.AluOpType
NPBF = ml_dtypes.bfloat16

ENGS = ("pe", "act", "dve", "pool", "sp")
EPOCH = 12000
NDMA = 8
ALPHA = 2.0 ** 0.25
EPS = 1e-5
L = 4096
NTILE = 8
NSEQ = 2
import os as _os
TSTOP = int(_os.environ.get('TSTOP', '0'))


class KB:
    def __init__(self, nc):
        self.nc = nc
        self.ops = {e: [] for e in ENGS}
        self.cnt = {e: 0 for e in ENGS}
        self.esems = {e: [] for e in ENGS}
        self.seen = {e: {} for e in ENGS}
        self.reg = {}
        self.dsem = {}
        self.dptr = {e: 0 for e in ENGS}
        self.bulk = []
        self.keep = ("wup_s", "wdn_s")

    def new_sem(self, name):
        return self.nc.alloc_semaphore(name=name)

    def _esem(self, e, idx):
        ep = idx // EPOCH
        while len(self.esems[e]) <= ep:
            self.esems[e].append(self.new_sem(f"p_{e}_{len(self.esems[e])}"))
        return self.esems[e][ep], idx - ep * EPOCH

    def _waits_for(self, e, reads, writes):
        need = {}

        def add(t, same_ok):
            if t is None:
                return
            s, v, src = t
            if src == e and same_ok:
                return
            k = id(s)
            if self.seen[e].get(k, 0) >= v:
                return
            if k not in need or need[k][1] < v:
                need[k] = (s, v)

        for r in reads:
            ent = self.reg.get(r)
            if ent:
                add(ent[0], same_ok=(e == "pe"))
        for w in writes:
            ent = self.reg.get(w)
            if ent:
                add(ent[0], same_ok=True)
                for t in ent[1]:
                    add(t, same_ok=True)
        out = []
        for k, (s, v) in need.items():
            self.seen[e][k] = v
            out.append((s, v))
        return out

    def _commit(self, ticket, reads, writes):
        for r in reads:
            self.reg.setdefault(r, [None, []])[1].append(ticket)
        for w in writes:
            self.reg[w] = [ticket, []]

    def op(self, e, fn, reads=(), writes=(), inc=True):
        waits = self._waits_for(e, reads, writes)
        s, local = self._esem(e, self.cnt[e])
        ticket = (s, local + 1, e)
        self._commit(ticket, reads, writes)
        if inc:
            self.cnt[e] += 1
            self.ops[e].append((waits, fn, (s, 1)))
        else:
            self.ops[e].append((waits, fn, None))

    def dma(self, q, out, in_, reads=(), writes=(), bulk=False, **kw):
        if bulk:
            ent = [self.new_sem(f"db_{len(self.bulk)}"), 0]
            self.bulk.append(ent)
        else:
            if q not in self.dsem:
                self.dsem[q] = [[self.new_sem(f"d_{q}_{i}"), 0] for i in range(NDMA)]
            i = self.dptr[q]
            self.dptr[q] = (i + 1) % NDMA
            ent = self.dsem[q][i]
        s, uses = ent
        waits = self._waits_for(q, reads, writes)
        k = id(s)
        if uses > 0 and self.seen[q].get(k, 0) < 16 * uses:
            waits.append((s, 16 * uses))
            self.seen[q][k] = 16 * uses
        ent[1] = uses + 1
        ticket = (s, 16 * (uses + 1), "dma")
        self._commit(ticket, reads, writes)

        def fn(eng, out=out, in_=in_, kw=kw):
            return eng.dma_start(out=out, in_=in_, **kw)

        self.ops[q].append((waits, fn, (s, 16)))

    def barrier(self, final=False):
        ticks = []
        if final:
            for s, uses in self.bulk:
                ticks.append((s, 16 * uses))
        for e in ENGS:
            if self.cnt[e] > 0:
                s, local = self._esem(e, self.cnt[e] - 1)
                ticks.append((s, local + 1))
        for q, lst in self.dsem.items():
            for s, uses in lst:
                if uses > 0:
                    ticks.append((s, 16 * uses))
        for e in ENGS:
            w = []
            for s, v in ticks:
                if self.seen[e].get(id(s), 0) < v:
                    w.append((s, v))
                    self.seen[e][id(s)] = v
            self.ops[e].append((w, None, None))
        self.reg = {k: v for k, v in self.reg.items() if isinstance(k, tuple) and k[0] in self.keep}

    def emit(self):
        nc = self.nc
        with nc.Block() as block:
            def run(e):
                def body(eng):
                    for waits, fn, inc in self.ops[e]:
                        for (s, v) in waits:
                            eng.wait_ge(s, v)
                        if fn is not None:
                            inst = fn(eng)
                            if inc is not None:
                                inst.then_inc(inc[0], inc[1])
                return body
            block.tensor(run("pe"))
            block.scalar(run("act"))
            block.vector(run("dve"))
            block.gpsimd(run("pool"))
            block.sync(run("sp"))
        self.ops = {e: [] for e in ENGS}


def host_consts():
    c = {}
    c["ident_f"] = np.eye(128, dtype=np.float32)
    c["ident_b"] = np.eye(128, dtype=np.float32).astype(NPBF)
    c["ones512"] = np.full((128, 128), 1.0 / 512, np.float32).astype(NPBF)
    c["ones128"] = np.full((128, 128), 1.0 / 128, np.float32).astype(NPBF)
    c["ev17"] = np.tile(np.arange(17, dtype=np.float32)[None], (128, 1))
    c["ev16r"] = np.tile((15 - np.arange(16)).astype(np.float32)[None], (128, 1))
    c["nv32"] = np.tile(np.arange(1, 33, dtype=np.float32)[None], (128, 1))
    freqs = (np.float32(10000.0) ** (-(np.arange(32, dtype=np.float32)) / np.float32(32))).astype(np.float32)
    pos = np.arange(L, dtype=np.float32)
    ang = (pos[None, :] * freqs[:, None]).astype(np.float32).astype(np.float64)
    rope = np.zeros((128, 2, L), np.float32)
    for p in range(128):
        f = p % 32
        hf = (p // 32) % 2
        rope[p, 0] = np.cos(ang[f])
        rope[p, 1] = np.sin(ang[f]) * (-1.0 if hf == 0 else 1.0)
    c["rope"] = rope
    lg = np.log1p(-(2.0 ** (-5.0 - np.arange(4, dtype=np.float64))))
    m = np.arange(128)
    mask = np.zeros((128, 4, 128), np.float64)
    same = (m[:, None] // 64) == (m[None, :] // 64)
    for h in range(4):
        mask[:, h, :] = np.where(same, np.exp(np.abs(m[:, None] - m[None, :]) * lg[h]) / 8.0, 0.0)
    c["maskT"] = mask.astype(np.float32)
    xi = np.zeros((128, 2, 64), np.float64)
    dec = np.zeros((128, 2), np.float64)
    for p in range(128):
        for qt in range(2):
            h = 2 * qt + p // 64
            xi[p, qt] = np.exp((np.arange(64) + 1.0) * lg[h])
            dec[p, qt] = np.exp(64.0 * lg[h])
    c["xi"] = xi.astype(np.float32)
    c["dec"] = dec.astype(np.float32)
    zeta = np.zeros((128, 4), np.float64)
    for h in range(4):
        zeta[:, h] = np.exp((63 - (m % 64)) * lg[h]) / 8.0
    c["zeta"] = zeta.astype(np.float32)
    return c


def host_weights(inp):
    f = np.float32
    w = {}
    w["w_in_f"] = np.ascontiguousarray(inp["w_in"][0].reshape(8, 128, 16, 128).transpose(2, 1, 0, 3)).reshape(16, 128, 1024).astype(f)
    w["w_out_h"] = np.ascontiguousarray(inp["w_out"][0].reshape(8, 128, 8, 128).transpose(2, 1, 0, 3)).reshape(8, 128, 1024).astype(f)
    w["glu_w_r"] = np.ascontiguousarray(inp["s5_glu_w"][0].reshape(4, 128, 512).transpose(1, 0, 2)).astype(f)
    wu = inp["ffn_w_up"][0].reshape(8, 128, 2, 22, 128)
    w["w_up_r"] = np.ascontiguousarray(wu.transpose(3, 1, 0, 2, 4)).reshape(22, 128, 2048).astype(f)
    w["w_down_r"] = np.ascontiguousarray(inp["ffn_w_down"][0].reshape(22, 128, 1024)).astype(f)
    lre, lim, ls = inp["s5_lambda_re"][0], inp["s5_lambda_im"][0], inp["s5_log_step"][0]
    lamc = np.zeros((128, 3, 16), f)
    bcol = np.zeros((128, 2, 16, 32), f)
    ccol = np.zeros((128, 2, 16, 32), f)
    bre, bim, cre, cim = inp["s5_b_re"][0], inp["s5_b_im"][0], inp["s5_c_re"][0], inp["s5_c_im"][0]
    for q in range(16):
        for gi in range(2):
            g = 2 * q + gi
            sl = slice(64 * gi, 64 * gi + 64)
            lamc[sl, 0, q] = lre[g]
            lamc[sl, 1, q] = lim[g]
            lamc[sl, 2, q] = ls[g]
            bcol[sl, 0, q, 16 * gi:16 * gi + 16] = bre[g]
            bcol[sl, 1, q, 16 * gi:16 * gi + 16] = bim[g]
            ccol[sl, 0, q, 16 * gi:16 * gi + 16] = cre[g].T
            ccol[sl, 1, q, 16 * gi:16 * gi + 16] = cim[g].T
    w["lamc"], w["bcol"], w["ccol"] = lamc, bcol, ccol
    lamr = np.zeros((128, 3, 4, 128), f)
    brow = np.zeros((128, 2, 4, 128), f)
    for t in range(4):
        for qi in range(4):
            for gi2 in range(2):
                g2 = 8 * t + 2 * qi + gi2
                cs = slice(64 * gi2, 64 * gi2 + 64)
                rs = slice(32 * qi, 32 * qi + 32)
                lamr[rs, 0, t, cs] = lre[g2][None, :]
                lamr[rs, 1, t, cs] = lim[g2][None, :]
                lamr[rs, 2, t, cs] = ls[g2]
                rr = slice(32 * qi + 16 * gi2, 32 * qi + 16 * gi2 + 16)
                brow[rr, 0, t, cs] = bre[g2].T
                brow[rr, 1, t, cs] = bim[g2].T
    vec = lambda a, n: np.ascontiguousarray(a.reshape(n, 128).T).astype(f)
    w["s5d"] = vec(inp["s5_d"][0], 4)
    w["glub"] = vec(inp["s5_glu_b"][0], 4)
    w["s5gain"] = vec(inp["s5_out_gain"][0], 4)
    w["gngain"] = vec(inp["ret_gn_gain"][0], 4)
    w["convw"] = np.ascontiguousarray(inp["ffn_conv_w"][0].reshape(3, 22, 128).transpose(2, 1, 0)).astype(f)
    w["convb"] = vec(inp["ffn_conv_b"][0], 22)
    lnp = np.stack([inp["ln1_g"][0], inp["ln1_b"][0], inp["ln2_g"][0], inp["ln2_b"][0]], 0)
    w["lnp"] = np.ascontiguousarray(np.broadcast_to(lnp[None], (128, 4, 1024))).astype(f)
    return w


IN_SPECS = {
    "x": ([NSEQ, L, 1024], F32),
    "w_in_f": ([16, 128, 1024], F32), "w_out_h": ([8, 128, 1024], F32), "glu_w_r": ([128, 4, 512], F32),
    "w_up_r": ([22, 128, 2048], F32), "w_down_r": ([22, 128, 1024], F32),
    "lamc": ([128, 3, 16], F32), "bcol": ([128, 2, 16, 32], F32), "ccol": ([128, 2, 16, 32], F32),
    "s5d": ([128, 4], F32), "glub": ([128, 4], F32), "s5gain": ([128, 4], F32), "gngain": ([128, 4], F32),
    "convw": ([128, 22, 3], F32), "convb": ([128, 22], F32), "lnp": ([128, 4, 1024], F32),
    "ident_f": ([128, 128], F32), "ident_b": ([128, 128], BF16), "ones512": ([128, 128], BF16),
    "ones128": ([128, 128], BF16), "ev17": ([128, 17], F32), "ev16r": ([128, 16], F32), "nv32": ([128, 32], F32),
    "rope": ([128, 2, L], F32), "maskT": ([128, 4, 128], F32), "xi": ([128, 2, 64], F32),
    "dec": ([128, 2], F32), "zeta": ([128, 4], F32),
}


class Stream:
    def __init__(self, prog, q, bufs, name, src_fn, ncyc):
        self.p, self.q, self.bufs, self.name, self.src_fn, self.ncyc = prog, q, bufs, name, src_fn, ncyc
        self.issued = 0
        self.consumed = 0

    def _issue(self):
        m = self.issued
        self.issued += 1
        nb = len(self.bufs)
        self.p.load(self.q, self.bufs[m % nb][:], self.src_fn(m % self.ncyc), w=[(self.name, m % nb)])

    def get(self):
        n = self.consumed
        nb = len(self.bufs)
        while self.issued < n + nb:
            self._issue()
        self.consumed += 1
        return self.bufs[n % nb], (self.name, n % nb)


class Prog:
    def __init__(self, stages=("prep", "s5", "tile"), nseq=NSEQ, ntile=NTILE, debug=False):
        self.nc = nc = bass.Bass("TRN2", target_bir_lowering=False)
        self.kb = KB(nc)
        self.debug = debug
        self.I = {}
        for name, (shape, dt) in IN_SPECS.items():
            self.I[name] = nc.dram_tensor(name, shape, dt, kind="ExternalInput").ap()
        self.out = nc.dram_tensor("out", [NSEQ, L, 1024], F32, kind="ExternalOutput").ap()
        self.D = {}
        self.wup_s = nc.dram_tensor("wup_s", [22, 128, 2048], BF16, kind="Internal").ap()
        self.wdn_s = nc.dram_tensor("wdn_s", [22, 128, 1024], BF16, kind="Internal").ap()
        self.win_s = nc.dram_tensor("win_s", [16, 128, 1024], BF16, kind="Internal").ap()
        self.wout_s = nc.dram_tensor("wout_s", [8, 128, 1024], BF16, kind="Internal").ap()
        self.lnb_s = nc.dram_tensor("lnb_s", [128, 2048], BF16, kind="Internal").ap()
        self.kbd_s = nc.dram_tensor("kbd_s", [128, 16 * 4 * 128], BF16, kind="Internal").ap()
        self.wb_s = nc.dram_tensor("wb_s", [128, 16 * 4 * 2 * 128], BF16, kind="Internal").ap()
        self.wc_s = nc.dram_tensor("wc_s", [128, 2 * 16 * 17 * 32], BF16, kind="Internal").ap()
        self.sc_s = nc.dram_tensor("sc_s", [128, 16 * 32 * 3 + 16 * 2], F32, kind="Internal").ap()
        self.ps = [nc.alloc_psum_tensor(f"ps{i}", [128, 512], F32) for i in range(8)]
        self.uid = 0
        with ExitStack() as es:
            self.es_perm = es
            self.alloc_perm()
            if "prep" in stages:
                self.phase_prep()
            for s in range(nseq):
                if "s5" in stages:
                    self.phase_s5(s)
                if "tile" in stages:
                    self.phase_tiles(s, ntile)
            self.kb.barrier(final=True)
            self.kb.emit()

    def sb(self, es, name, shape, dt):
        self.uid += 1
        return es.enter_context(self.nc.sbuf_tensor(f"{name}_{self.uid}", shape, dt))

    def dbg(self, name, shape, dt=F32):
        if name not in self.D:
            self.D[name] = self.nc.dram_tensor("dbg_" + name, shape, dt, kind="ExternalOutput").ap()
        return self.D[name]

    def tt(self, e, out, a, b, op, r, w):
        self.kb.op(e, lambda eng: eng.tensor_tensor(out=out, in0=a, in1=b, op=op), reads=r, writes=w)

    def ts(self, e, out, a, s1, s2, op0, op1, r, w):
        if op1 is None:
            self.kb.op(e, lambda eng: eng.tensor_scalar(out=out, in0=a, scalar1=s1, scalar2=None, op0=op0), reads=r, writes=w)
        else:
            self.kb.op(e, lambda eng: eng.tensor_scalar(out=out, in0=a, scalar1=s1, scalar2=s2, op0=op0, op1=op1), reads=r, writes=w)

    def stt(self, out, a, sc, b, op0, op1, r, w):
        self.kb.op("dve", lambda eng: eng.scalar_tensor_tensor(out=out, in0=a, scalar=sc, in1=b, op0=op0, op1=op1), reads=r, writes=w)

    def act(self, out, in_, func, r, w, bias=None, scale=None):
        kw = {}
        if bias is not None:
            kw["bias"] = bias
        if scale is not None:
            kw["scale"] = scale
        self.kb.op("act", lambda eng: eng.activation(out=out, in_=in_, func=func, **kw), reads=r, writes=w)

    def cp(self, e, out, in_, r, w):
        if e == "act":
            self.kb.op(e, lambda eng: eng.activation(out=out, in_=in_, func=AF.Copy), reads=r, writes=w)
        else:
            self.kb.op(e, lambda eng: eng.tensor_copy(out=out, in_=in_), reads=r, writes=w)

    def mm(self, out, lhsT, rhs, start, stop, r, w, tp=None):
        kw = {}
        if tp is not None:
            kw["tile_position"] = tp
        self.kb.op("pe", lambda eng: eng.matmul(out, lhsT=lhsT, rhs=rhs, start=start, stop=stop, **kw),
                   reads=r, writes=w, inc=stop)

    def tr(self, out, in_, ident, r, w, inc=True):
        self.kb.op("pe", lambda eng: eng.transpose(out, in_, ident), reads=r, writes=w, inc=inc)

    def memset(self, e, ap, val, w):
        self.kb.op(e, lambda eng: eng.memset(ap, val), writes=w)

    def load(self, q, dst, src, w, r=(), bulk=False):
        self.kb.dma(q, dst, src, reads=r, writes=w, bulk=bulk)

    def alloc_perm(self):
        es = self.es_perm
        sb = lambda n, s, d: self.sb(es, n, s, d)
        self.identf = sb("identf", [128, 128], F32)
        self.identb = sb("identb", [128, 128], BF16)
        self.ones512 = sb("ones512", [128, 128], BF16)
        self.ones128 = sb("ones128", [128, 128], BF16)
        self.U = sb("U", [128, 4, L], BF16)
        self.vecs = sb("vecs", [128, 16], F32)
        self.convw = sb("convw", [128, 22, 3], F32)
        self.convb = sb("convb", [128, 22], F32)
        self._eps = sb("epsc", [128, 1], F32)
        self.memset("pool", self._eps[:], EPS, w=["epsc"])
        for dst, nm in ((self.identf, "ident_f"), (self.identb, "ident_b"), (self.ones512, "ones512"),
                        (self.ones128, "ones128"), (self.convw, "convw"), (self.convb, "convb")):
            self.load("sp", dst[:], self.I[nm], w=[nm])
        for i, nm in enumerate(("s5d", "glub", "s5gain", "gngain")):
            self.load("sp", self.vecs[:, 4 * i:4 * i + 4], self.I[nm], w=[nm])

    def g_wcast(self, st):
        if True:
            NB = len(st)
            n = 0
            for i in range(22):
                for (src, dst, cols, key) in ((self.I["w_up_r"], self.wup_s, 2048, "wup_s"), (self.I["w_down_r"], self.wdn_s, 1024, "wdn_s")):
                    b = st[n % NB]
                    bk = ("wst", n % NB)
                    n += 1
                    self.load("pool", b[:, 0:cols], src[i], w=[bk])
                    self.load("sp", dst[i], b[:, 0:cols], r=[bk], w=[(key, i)])
                    yield
            for f in (4, 5, 6, 7, 12, 13, 14, 15):
                b = st[n % NB]
                bk = ("wst", n % NB)
                n += 1
                self.load("pool", b[:, 0:1024], self.I["w_in_f"][f], w=[bk])
                self.load("sp", self.win_s[f], b[:, 0:1024], r=[bk], w=[("win_s", f)])
                yield
            for qn in range(8):
                b = st[n % NB]
                bk = ("wst", n % NB)
                n += 1
                self.load("pool", b[:, 0:1024], self.I["w_out_h"][qn], w=[bk])
                self.load("sp", self.wout_s[qn], b[:, 0:1024], r=[bk], w=[("wout_s", qn)])
                yield
            for j, row in enumerate((1, 3)):
                b = st[n % NB]
                bk = ("wst", n % NB)
                n += 1
                self.load("pool", b[:, 0:1024], self.I["lnp"][:, row, :], w=[bk])
                self.load("sp", self.lnb_s[:, j * 1024:(j + 1) * 1024], b[:, 0:1024], r=[bk], w=[("lnb_s", j)])
                yield

    def wc_pull(self, n):
        g = getattr(self, "_wcg", None)
        if g is None:
            return
        for _ in range(n):
            try:
                next(g)
            except StopIteration:
                self._wcg = None
                return

    def sincos(self, es, tag, ang, shape, s_out, c_out):
        k = self.sb(es, tag + "k", shape, F32)
        r = self.sb(es, tag + "r", shape, F32)
        kk, rr = k[:], r[:]
        M = 12582912.0
        TWO_PI = 2 * math.pi
        C1 = 6.28125
        C2 = float(np.float32(TWO_PI - C1))
        C3 = TWO_PI - C1 - C2
        PIS = 3.1415925
        A = tag + "ang"
        self.ts("dve", kk, ang, 1.0 / TWO_PI, M, ALU.mult, ALU.add, r=[A], w=[tag + "k"])
        self.ts("dve", kk, kk, M, None, ALU.subtract, None, r=[tag + "k"], w=[tag + "k"])
        self.stt(rr, kk, -C1, ang, ALU.mult, ALU.add, r=[A, tag + "k"], w=[tag + "r"])
        self.stt(rr, kk, -C2, rr, ALU.mult, ALU.add, r=[tag + "r", tag + "k"], w=[tag + "r"])
        self.stt(rr, kk, -C3, rr, ALU.mult, ALU.add, r=[tag + "r", tag + "k"], w=[tag + "r"])
        self.ts("dve", rr, rr, -PIS, PIS, ALU.max, ALU.min, r=[tag + "r"], w=[tag + "r"])
        self.act(s_out, rr, AF.Sin, r=[tag + "r"], w=[tag + "s"])
        self.stt(kk, rr, -1.0, rr, ALU.mult, ALU.max, r=[tag + "r"], w=[tag + "k"])
        self.ts("dve", kk, kk, -1.0, math.pi / 2, ALU.mult, ALU.add, r=[tag + "k"], w=[tag + "k"])
        self.act(c_out, kk, AF.Sin, r=[tag + "k"], w=[tag + "c"])

    def cmul(self, e, ore, oim, are, aim, bre, bim, t1, t2, r, w, tag):
        T1, T2 = tag + "t1", tag + "t2"
        self.tt(e, t1, are, bre, ALU.mult, r=r, w=[T1])
        self.tt(e, t2, aim, bim, ALU.mult, r=r, w=[T2])
        self.tt(e, ore, t1, t2, ALU.subtract, r=[T1, T2], w=[w + "re"])
        self.tt(e, t1, are, bim, ALU.mult, r=r, w=[T1])
        self.tt(e, t2, aim, bre, ALU.mult, r=r, w=[T2])
        self.tt(e, oim, t1, t2, ALU.add, r=[T1, T2], w=[w + "im"])

    def lam_basic(self, es, tag, lam3, X):
        sb = lambda n: self.sb(es, tag + n, [128, X], F32)
        dtv, a, th, cre, cim = sb("dtv"), sb("a"), sb("th"), sb("cre"), sb("cim")
        t1, t2, t3, t4 = sb("t1"), sb("t2"), sb("t3"), sb("t4")
        LAM = tag + "lam"
        self.act(dtv[:], lam3[:, 2, :], AF.Exp, r=[LAM], w=[tag + "dtv"])
        self.tt("dve", a[:], lam3[:, 0, :], dtv[:], ALU.mult, r=[LAM, tag + "dtv"], w=[tag + "a"])
        self.tt("dve", th[:], lam3[:, 1, :], dtv[:], ALU.mult, r=[LAM, tag + "dtv"], w=[tag + "th"])
        mag, s1, c1 = sb("mag"), sb("s1"), sb("c1")
        self.act(mag[:], a[:], AF.Exp, r=[tag + "a"], w=[tag + "mag"])
        with ExitStack() as es2:
            self.ts("dve", t1[:], th[:], 1.0, None, ALU.mult, None, r=[tag + "th"], w=[tag + "lbang"])
            self.sincos(es2, tag + "lb", t1[:], [128, X], s1[:], c1[:])
        self.tt("dve", t1[:], mag[:], c1[:], ALU.mult, r=[tag + "mag", tag + "lbc"], w=[tag + "nr"])
        self.ts("dve", t1[:], t1[:], -1.0, None, ALU.add, None, r=[tag + "nr"], w=[tag + "nr"])
        self.tt("dve", t2[:], mag[:], s1[:], ALU.mult, r=[tag + "mag", tag + "lbs"], w=[tag + "ni"])
        self.tt("dve", t3[:], lam3[:, 0, :], lam3[:, 0, :], ALU.mult, r=[LAM], w=[tag + "d1"])
        self.tt("dve", t4[:], lam3[:, 1, :], lam3[:, 1, :], ALU.mult, r=[LAM], w=[tag + "d2"])
        self.tt("dve", t3[:], t3[:], t4[:], ALU.add, r=[tag + "d1", tag + "d2"], w=[tag + "d1"])
        self.kb.op("dve", lambda eng: eng.reciprocal(out=t3[:], in_=t3[:]), reads=[tag + "d1"], writes=[tag + "d1"])
        self.tt("dve", cre[:], t1[:], lam3[:, 0, :], ALU.mult, r=[tag + "nr", LAM], w=[tag + "cre"])
        self.tt("dve", t4[:], t2[:], lam3[:, 1, :], ALU.mult, r=[tag + "ni", LAM, tag + "d1"], w=[tag + "d2"])
        self.tt("dve", cre[:], cre[:], t4[:], ALU.add, r=[tag + "cre", tag + "d2"], w=[tag + "cre"])
        self.tt("dve", cre[:], cre[:], t3[:], ALU.mult, r=[tag + "cre", tag + "d1"], w=[tag + "cre"])
        self.tt("dve", cim[:], t2[:], lam3[:, 0, :], ALU.mult, r=[tag + "ni", LAM], w=[tag + "cim"])
        self.tt("dve", t4[:], t1[:], lam3[:, 1, :], ALU.mult, r=[tag + "nr", LAM, tag + "cre"], w=[tag + "d2"])
        self.tt("dve", cim[:], cim[:], t4[:], ALU.subtract, r=[tag + "cim", tag + "d2"], w=[tag + "cim"])
        self.tt("dve", cim[:], cim[:], t3[:], ALU.mult, r=[tag + "cim", tag + "d1"], w=[tag + "cim"])
        return a, th, cre, cim

    def powers(self, es, tag, a_ap, th_ap, ev_ap, X, E, pwre, pwim, rin):
        shp = [128, X, E]
        arg = self.sb(es, tag + "arg", shp, F32)
        ang = self.sb(es, tag + "ang", shp, F32)
        sn = self.sb(es, tag + "sn", shp, F32)
        cs = self.sb(es, tag + "cs", shp, F32)
        abc = a_ap.unsqueeze(2).to_broadcast(shp)
        tbc = th_ap.unsqueeze(2).to_broadcast(shp)
        ebc = ev_ap.unsqueeze(1).to_broadcast(shp)
        self.tt("dve", arg[:], abc, ebc, ALU.mult, r=rin, w=[tag + "arg"])
        self.act(arg[:], arg[:], AF.Exp, r=[tag + "arg"], w=[tag + "arg"])
        self.tt("dve", ang[:], tbc, ebc, ALU.mult, r=rin, w=[tag + "ang"])
        self.sincos(es, tag, ang[:], shp, sn[:], cs[:])
        self.tt("dve", pwre, arg[:], cs[:], ALU.mult, r=[tag + "arg", tag + "c"], w=[tag + "pwre"])
        self.tt("dve", pwim, arg[:], sn[:], ALU.mult, r=[tag + "arg", tag + "s"], w=[tag + "pwim"])

    def phase_prep(self):
        kb = self.kb
        with ExitStack() as es:
            sb = lambda n, s, d=F32: self.sb(es, n, s, d)
            ev17, ev16r, nv32 = sb("ev17", [128, 17]), sb("ev16r", [128, 16]), sb("nv32", [128, 32])
            self.load("sp", ev17[:], self.I["ev17"], w=["ev17"])
            self.load("sp", ev16r[:], self.I["ev16r"], w=["ev16r"])
            self.load("sp", nv32[:], self.I["nv32"], w=["nv32"])
            lamc = sb("lamc", [128, 3, 16])
            bcol = sb("bcol", [128, 2, 16, 32])
            ccol = sb("ccol", [128, 2, 16, 32])
            self.load("sp", lamc[:], self.I["lamc"], w=["Clam"])
            self.load("sp", bcol[:], self.I["bcol"], w=["bcol"])
            self.load("sp", ccol[:], self.I["ccol"], w=["ccol"])
            kbd = sb("kbd", [128, 16, 4, 128], BF16)
            self.memset("dve", kbd[:], 0.0, w=["kbd"])
            a, th, cre, cim = self.lam_basic(es, "C", lamc, 16)
            pwre, pwim = sb("cpwre", [128, 16, 17]), sb("cpwim", [128, 16, 17])
            with ExitStack() as es2:
                self.powers(es2, "Cp", a[:], th[:], ev17[:], 16, 17, pwre[:], pwim[:], rin=["Ca", "Cth", "ev17"])
            sct = sb("sct", [128, 16 * 32 * 3 + 32])
            zc = sct[:, 0:512].rearrange("p (q n) -> p q n", n=32)
            zs = sct[:, 512:1024].rearrange("p (q n) -> p q n", n=32)
            rho = sct[:, 1024:1536].rearrange("p (q n) -> p q n", n=32)
            zend = sct[:, 1536:1568].rearrange("p (r q) -> p r q", q=16)
            phi = sb("phi", [128, 16])
            self.ts("dve", phi[:], th[:], 16.0, None, ALU.mult, None, r=["Cth"], w=["phi"])
            with ExitStack() as es2:
                zang = self.sb(es2, "zang", [128, 16, 32], F32)
                self.tt("dve", zang[:], phi[:].unsqueeze(2).to_broadcast([128, 16, 32]),
                        nv32[:].unsqueeze(1).to_broadcast([128, 16, 32]), ALU.mult, r=["phi", "nv32"], w=["Zang"])
                self.sincos(es2, "Z", zang[:], [128, 16, 32], zs, zc)
            rh = sb("rh", [128, 16])
            self.act(rh[:], a[:], AF.Exp, r=["Ca"], w=["rh"], scale=16.0)
            self.cp("dve", rho, rh[:].unsqueeze(2).to_broadcast([128, 16, 32]), r=["rh"], w=["rho"])
            self.cp("dve", zend[:, 0, :], zc[:, :, 31], r=["Zc"], w=["zend0"])
            self.cp("dve", zend[:, 1, :], zs[:, :, 31], r=["Zs"], w=["zend1"])
            self.load("sp", self.sc_s, sct[:], w=["sc_s"], r=["Zc", "Zs", "rho", "zend0", "zend1"])
            t1, t2 = sb("ct1", [128, 4, 17, 32]), sb("ct2", [128, 4, 17, 32])
            s1, s2 = sb("cs1", [128, 16, 32]), sb("cs2", [128, 16, 32])
            bb = sb("bb", [128, 2, 16, 32], BF16)
            bbf = sb("bbf", [128, 2, 16, 32])
            cbc = lambda x: x[:].unsqueeze(2).to_broadcast([128, 16, 32])
            self.cmul("dve", bbf[:, 0], bbf[:, 1], bcol[:, 0], bcol[:, 1], cbc(cre), cbc(cim),
                      s1[:], s2[:], r=["bcol", "Ccre", "Ccim"], w="bbf", tag="bbm")
            self.cp("dve", bb[:, 0], bbf[:, 0], r=["bbfre"], w=["bbre"])
            self.cp("dve", bb[:, 1], bbf[:, 1], r=["bbfim"], w=["bbim"])
            wbk = sb("wbk", [128, 2, 16, 4, 32])
            wb = sb("wb", [128, 16, 4, 2, 128], BF16)
            cntb = 0
            for t in range(4):
                shb = [128, 4, 16, 32]
                br = bbf[:, 0, 4 * t:4 * t + 4, :].unsqueeze(2).to_broadcast(shb)
                bi = bbf[:, 1, 4 * t:4 * t + 4, :].unsqueeze(2).to_broadcast(shb)
                pr = pwre[:, 4 * t:4 * t + 4, 0:16].unsqueeze(3).to_broadcast(shb)
                pi = pwim[:, 4 * t:4 * t + 4, 0:16].unsqueeze(3).to_broadcast(shb)
                u1 = t1[:, :, 0:16, :]
                u2 = t2[:, :, 0:16, :]
                P = ["bbfre", "bbfim", "Cppwre", "Cppwim"]
                o_re = wbk[:, 0].rearrange("p k q c -> p q k c")
                o_im = wbk[:, 1].rearrange("p k q c -> p q k c")
                self.tt("dve", u1, br, pr, ALU.mult, r=P, w=["wt1"])
                self.tt("pool", u2, bi, pi, ALU.mult, r=P, w=["wt2"])
                self.tt("dve", o_re, u1, u2, ALU.subtract, r=["wt1", "wt2"], w=[("wbk", 0)])
                self.tt("dve", u1, br, pi, ALU.mult, r=P, w=["wt1"])
                self.tt("pool", u2, bi, pr, ALU.mult, r=P, w=["wt2"])
                self.tt("dve", o_im, u1, u2, ALU.add, r=["wt1", "wt2"], w=[("wbk", 1)])
                for ri in range(2):
                    for jg in range(4):
                        bnk = 4 + cntb % 4
                        cntb += 1
                        pbk = self.ps[bnk]
                        for jj in range(4):
                            j = 4 * jg + jj
                            self.tr(pbk[:, jj * 128:(jj + 1) * 128], wbk[:, ri, 15 - j].rearrange("p q c -> p (q c)"), self.identf[:],
                                    r=[("wbk", ri), "ident_f"], w=[("ps", bnk)], inc=(jj == 3))
                        eng = "act" if cntb % 2 == 0 else "dve"
                        self.cp(eng, wb[:, 4 * jg:4 * jg + 4, t, ri, :], pbk[:].rearrange("p (j c) -> p j c", c=128),
                                r=[("ps", bnk)], w=[("wb", t, ri, jg)])
            self.load("sp", self.wb_s, wb[:].rearrange("p j t r c -> p (j t r c)"), w=["wb_s"],
                      r=[("wb", t, ri, jg) for t in range(4) for ri in range(2) for jg in range(4)])
            wc = sb("wc", [128, 2, 16, 17, 32], BF16)
            P = ["ccol", "Cppwre", "Cppwim"]
            for qg in range(4):
                qs = slice(4 * qg, 4 * qg + 4)
                shp = [128, 4, 17, 32]
                cr_bc = ccol[:, 0, qs].unsqueeze(2).to_broadcast(shp)
                ci_bc = ccol[:, 1, qs].unsqueeze(2).to_broadcast(shp)
                pr_bc = pwre[:, qs, :].unsqueeze(3).to_broadcast(shp)
                pi_bc = pwim[:, qs, :].unsqueeze(3).to_broadcast(shp)
                self.tt("dve", t1[:], cr_bc, pr_bc, ALU.mult, r=P, w=["wt1"])
                self.tt("pool", t2[:], ci_bc, pi_bc, ALU.mult, r=P, w=["wt2"])
                self.tt("dve", wc[:, 0, qs], t1[:], t2[:], ALU.subtract, r=["wt1", "wt2"], w=["wcre"])
                self.tt("dve", t1[:], cr_bc, pi_bc, ALU.mult, r=P, w=["wt1"])
                self.tt("pool", t2[:], ci_bc, pr_bc, ALU.mult, r=P, w=["wt2"])
                self.stt(wc[:, 1, qs], t1[:], -1.0, t2[:], ALU.mult, ALU.subtract, r=["wt1", "wt2"], w=["wcim"])
            self.load("sp", self.wc_s, wc[:].rearrange("p a q k c -> p (a q k c)"), w=["wc_s"], r=["wcre", "wcim"])
            for t in range(4):
                pk = self.ps[t]
                for qi in range(4):
                    q = 4 * t + qi
                    o = pk[32 * qi:32 * qi + 32, :]
                    self.mm(o, bb[:, 0, q, :], wc[:, 0, q, 0:16, :], True, False, r=["bbre", "wcre"], w=[("ps", t)], tp=(0, 32 * qi))
                    self.mm(o, bb[:, 1, q, :], wc[:, 1, q, 0:16, :], False, True, r=["bbim", "wcim"], w=[("ps", t)], tp=(0, 32 * qi))
                    self.cp("act", kbd[32 * qi:32 * qi + 32, :, t, 32 * qi:32 * qi + 32],
                            o.rearrange("p (k c) -> p k c", c=32), r=[("ps", t), "kbd"], w=["kbd_%d_%d" % (t, qi)])
                deps = ["kbd"] + ["kbd_%d_%d" % (t, qi) for qi in range(4)]
                self.stt(kbd[:, 0, t, :], self.identf[:], self.vecs[:, t:t + 1], kbd[:, 0, t, :], ALU.mult, ALU.add,
                         r=deps + ["ident_f", "s5d"], w=["kbdD%d" % t])
            self.load("sp", self.kbd_s, kbd[:].rearrange("p k t c -> p (k t c)"), w=["kbd_s"],
                      r=["kbd"] + ["kbdD%d" % t for t in range(4)] + ["kbd_%d_%d" % (t, qi) for t in range(4) for qi in range(4)])
            if self.debug:
                self.load("sp", self.dbg("kbd", [128, 16 * 4 * 128], BF16), kbd[:].rearrange("p k t c -> p (k t c)"), w=["dbgk"],
                          r=["kbd"] + ["kbdD%d" % t for t in range(4)])
                self.load("sp", self.dbg("pw", [128, 2, 16 * 17]), pwre[:].rearrange("p q k -> p (q k)"), w=["dbgp"], r=["Cppwre"]) if False else None
            kb.barrier()
            kb.emit()

    def phase_s5(self, s):
        kb = self.kb
        I = self.I
        U = self.U
        if not getattr(self, "_wc_started", False):
            self._wc_started = True
            self._es_wc = ExitStack()
            st = [self.sb(self._es_wc, "wst%d" % i, [128, 2048], BF16) for i in range(4)]
            self._wcg = self.g_wcast(st)
        with ExitStack() as es:
            sb = lambda n, sh, d=F32: self.sb(es, n, sh, d)
            winu = sb("winu", [128, 8, 512], BF16)
            for f in range(4):
                self.load("pool", winu[:, :, f * 128:(f + 1) * 128], I["w_in_f"][f].rearrange("p (k c) -> p k c", c=128), w=[("winu", f)])
            xtm = [sb("xtm%d" % i, [128, 4, 1024]) for i in range(2)]
            xT = [sb("xT%d" % i, [128, 8, 512], BF16) for i in range(2)]
            for ti in range(NTILE):
                b = ti % 2
                self.load("sp", xtm[b][:], I["x"][s, ti * 512:(ti + 1) * 512, :].rearrange("(a p) d -> p a d", p=128), w=[("xtm", b)])
                self.transpose_x(xtm[b], xT[b], ("xtm", b), ("xT", b), ti)
                for t in range(4):
                    pb = self.ps[4 + (ti * 4 + t) % 4]
                    key = ("ps", 4 + (ti * 4 + t) % 4)
                    for k in range(8):
                        self.mm(pb[:], winu[:, k, t * 128:(t + 1) * 128], xT[b][:, k, :], k == 0, k == 7,
                                r=[("winu", t), ("xT", b)], w=[key])
                    self.act(U[:, t, ti * 512:(ti + 1) * 512], pb[:], AF.Copy, r=[key], w=[("U", t, ti)])
                self.wc_pull(2)
            if self.debug and s == 0:
                self.load("sp", self.dbg("uT", [128, 4 * L], BF16), U[:].rearrange("p t l -> p (t l)"), w=["dbgu"],
                          r=[("U", t, ti) for t in range(4) for ti in range(NTILE)])
            kb.barrier()
            kb.emit()
        with ExitStack() as es:
            sb = lambda n, sh, d=F32: self.sb(es, n, sh, d)
            sprev = sb("sprev", [128, 16, 2, 257], BF16)
            esBC = ExitStack()
            sloc = self.sb(esBC, "sloc", [128, 16, 2, 256], F32)
            with ExitStack() as esB:
                wb = self.sb(esB, "wb", [128, 16, 4, 2, 128], BF16)
                self.load("sp", wb[:], self.wb_s.rearrange("p (j t r c) -> p j t r c", j=16, t=4, r=2), w=["wb"])
                it = 0
                for t in range(4):
                    self.wc_pull(3)
                    uv = U[:, t, :].rearrange("p (n j) -> p j n", j=16)
                    for ri in range(2):
                        base = 4 * (it % 2)
                        it += 1
                        for j in range(16):
                            for qi in range(4):
                                rs = slice(32 * qi, 32 * qi + 32)
                                self.mm(self.ps[base + qi][:, 0:256], wb[rs, j, t, ri, :], uv[rs, j, :], j == 0, j == 15,
                                        r=["wb", "Uall"], w=[("ps", base + qi)], tp=(32 * qi, 0))
                        for qi in range(4):
                            eng = "act" if qi % 2 == 0 else "dve"
                            self.cp(eng, sloc[:, 4 * t + qi, ri, :], self.ps[base + qi][:, 0:256],
                                    r=[("ps", base + qi)], w=[("sloc", 4 * t + qi, ri)])
                kb.barrier()
                kb.emit()
            with ExitStack() as esC:
                sbc = lambda n, sh, d=F32: self.sb(esC, n, sh, d)
                sct = sbc("sct", [128, 16 * 32 * 3 + 32])
                self.load("sp", sct[:], self.sc_s, w=["sct"])
                zc = sct[:, 0:512].rearrange("p (q n) -> p q n", n=32)
                zs = sct[:, 512:1024].rearrange("p (q n) -> p q n", n=32)
                rho = sct[:, 1024:1536].rearrange("p (q n) -> p q n", n=32)
                zend = sct[:, 1536:1568].rearrange("p (r q) -> p r q", q=16)
                mod = sbc("mod", [128, 16, 2, 256])
                t1 = sbc("t1", [128, 16, 256])
                t2 = sbc("t2", [128, 16, 256])
                shp = [128, 16, 8, 32]
                v4 = lambda ap: ap.rearrange("p q (g n) -> p q g n", n=32)
                zcb = zc.unsqueeze(2).to_broadcast(shp)
                zsb = zs.unsqueeze(2).to_broadcast(shp)
                SL = [("sloc", q, r_) for q in range(16) for r_ in range(2)]
                lre, lim = v4(sloc[:, :, 0, :]), v4(sloc[:, :, 1, :])
                self.tt("dve", v4(t1[:]), lre, zcb, ALU.mult, r=SL + ["sct"], w=["ct1"])
                self.tt("pool", v4(t2[:]), lim, zsb, ALU.mult, r=SL + ["sct"], w=["ct2"])
                self.tt("dve", mod[:, :, 0, :], t1[:], t2[:], ALU.add, r=["ct1", "ct2"], w=["modre"])
                self.tt("dve", v4(t1[:]), lim, zcb, ALU.mult, r=SL + ["sct", "modre"], w=["ct1"])
                self.tt("pool", v4(t2[:]), lre, zsb, ALU.mult, r=SL + ["sct", "modre"], w=["ct2"])
                self.tt("dve", mod[:, :, 1, :], t1[:], t2[:], ALU.subtract, r=["ct1", "ct2"], w=["modim"])
                carry = sbc("carry", [128, 2, 16])
                ctmp = sbc("ctmp", [128, 4, 16])
                self.memset("dve", carry[:], 0.0, w=["carry"])
                for g in range(8):
                    for q in range(16):
                        for ri in range(2):
                            o = sloc[:, q, ri, g * 32:(g + 1) * 32]
                            d1 = mod[:, q, ri, g * 32:(g + 1) * 32]
                            ini = carry[:, ri, q:q + 1]
                            d0 = rho[:, q, :]
                            self.kb.op("dve", lambda eng, o=o, d0=d0, d1=d1, ini=ini: eng.tensor_tensor_scan(
                                out=o, data0=d0, data1=d1, initial=ini, op0=ALU.mult, op1=ALU.add),
                                reads=["modre", "modim", "carry", "sct"], writes=[("R", g)])
                    if g < 7:
                        rre = sloc[:, :, 0, g * 32 + 31]
                        rim = sloc[:, :, 1, g * 32 + 31]
                        self.tt("dve", ctmp[:, 0, :], rre, zend[:, 0, :], ALU.mult, r=[("R", g), "sct"], w=["cta"])
                        self.tt("dve", ctmp[:, 1, :], rim, zend[:, 1, :], ALU.mult, r=[("R", g), "sct"], w=["ctb"])
                        self.tt("dve", ctmp[:, 2, :], rre, zend[:, 1, :], ALU.mult, r=[("R", g), "sct"], w=["ctc"])
                        self.tt("dve", ctmp[:, 3, :], rim, zend[:, 0, :], ALU.mult, r=[("R", g), "sct"], w=["ctd"])
                        self.tt("dve", carry[:, 0, :], ctmp[:, 0, :], ctmp[:, 1, :], ALU.subtract, r=["cta", "ctb"], w=["carry"])
                        self.tt("dve", carry[:, 1, :], ctmp[:, 2, :], ctmp[:, 3, :], ALU.add, r=["ctc", "ctd", "carry"], w=["carry"])
                RR = [("R", g) for g in range(8)]
                self.memset("pool", sprev[:, :, :, 0:1], 0.0, w=["sprev0"])
                rre, rim = v4(sloc[:, :, 0, :]), v4(sloc[:, :, 1, :])
                self.tt("dve", v4(t1[:]), rre, zcb, ALU.mult, r=RR + ["sct"], w=["ct1"])
                self.tt("pool", v4(t2[:]), rim, zsb, ALU.mult, r=RR + ["sct"], w=["ct2"])
                self.tt("dve", sprev[:, :, 0, 1:257], t1[:], t2[:], ALU.subtract, r=["ct1", "ct2"], w=["sprevre"])
                self.tt("dve", v4(t1[:]), rre, zsb, ALU.mult, r=RR + ["sct", "sprevre"], w=["ct1"])
                self.tt("pool", v4(t2[:]), rim, zcb, ALU.mult, r=RR + ["sct", "sprevre"], w=["ct2"])
                self.tt("dve", sprev[:, :, 1, 1:257], t1[:], t2[:], ALU.add, r=["ct1", "ct2"], w=["sprevim"])
                if self.debug and s == 0:
                    self.load("sp", self.dbg("sprev", [128, 16 * 2 * 257], BF16), sprev[:].rearrange("p q r n -> p (q r n)"),
                              w=["dbgs"], r=["sprev0", "sprevre", "sprevim"])
                kb.barrier()
                kb.emit()
            esBC.close()
            with ExitStack() as esD:
                sbd = lambda n, sh, d=F32: self.sb(esD, n, sh, d)
                kbd = sbd("kbd", [128, 16, 4, 128], BF16)
                wc = sbd("wc", [128, 2, 16, 17, 32], BF16)
                yg = sbd("yg", [128, 4, L], BF16)
                ycr = [sbd("ycr%d" % i, [128, 16, 256], BF16) for i in range(2)]
                ysum = [sbd("ysum%d" % i, [128, 512]) for i in range(2)]
                self.load("sp", kbd[:], self.kbd_s.rearrange("p (k t c) -> p k t c", k=16, t=4), w=["kbd"])
                self.load("sp", wc[:], self.wc_s.rearrange("p (a q k c) -> p a q k c", a=2, q=16, k=17), w=["wc"])
                wglu = sbd("wglu", [128, 4, 512], BF16)
                self.load("pool", wglu[:], I["glu_w_r"], w=["wglu"])
                sig = [sbd("sig%d" % i, [128, 512]) for i in range(2)]
                sv = [sbd("sv%d" % i, [128, 4, 512]) for i in range(2)]
                sq = [sbd("sq%d" % i, [128, 4, 512], BF16) for i in range(2)]
                rstd = [sbd("rstd%d" % i, [128, 512]) for i in range(2)]

                def stageE(ti):
                    b = ti % 2
                    tsl = slice(ti * 512, (ti + 1) * 512)
                    for to in range(4):
                        pi_ = (ti * 4 + to) % 2
                        pb = self.ps[pi_]
                        for tin in range(4):
                            self.mm(pb[:], wglu[:, tin, to * 128:(to + 1) * 128], yg[:, tin, tsl], tin == 0, tin == 3,
                                    r=["wglu"] + [("yg", tin, ti)], w=[("ps", pi_)])
                        sg = sig[(ti * 4 + to) % 2]
                        sk = ("sig", (ti * 4 + to) % 2)
                        self.act(sg[:], pb[:], AF.Sigmoid, r=[("ps", pi_), "glub"], w=[sk], bias=self.vecs[:, 4 + to:5 + to])
                        self.tt("dve", sv[b][:, to, :], sg[:], yg[:, to, tsl], ALU.mult, r=[sk, ("yg", to, ti)], w=[("sv", b, to)])
                        self.tt("pool", sq[b][:, to, :], sv[b][:, to, :], sv[b][:, to, :], ALU.mult, r=[("sv", b, to)], w=[("sq", b, to)])
                def stageE2(ti):
                    b = ti % 2
                    tsl = slice(ti * 512, (ti + 1) * 512)
                    pmi = 6 + ti % 2
                    pm = self.ps[pmi]
                    for to in range(4):
                        self.mm(pm[:], self.ones512[:], sq[b][:, to, :], to == 0, to == 3, r=["ones512", ("sq", b, to)], w=[("ps", pmi)])
                    self.act(rstd[b][:], pm[:], AF.Ln, r=[("ps", pmi)], w=[("rstd", b)], bias=self.eps_ap())
                    self.act(rstd[b][:], rstd[b][:], AF.Exp, r=[("rstd", b)], w=[("rstd", b)], scale=-0.5)
                    for to in range(4):
                        self.stt(U[:, to, tsl], sv[b][:, to, :], self.vecs[:, 8 + to:9 + to], rstd[b][:], ALU.mult, ALU.mult,
                                 r=[("sv", b, to), ("rstd", b), "s5gain"], w=[("ys5", to, ti), ("U", to, ti)])

                cnt = 0
                for t in range(4):
                    self.wc_pull(9 if t < 3 else 100)
                    yc = ycr[t % 2]
                    for jp in range(8):
                        pb = self.ps[jp % 2]
                        for jj in range(2):
                            j = 2 * jp + jj
                            for qi in range(4):
                                q = 4 * t + qi
                                o = pb[32 * qi:32 * qi + 32, jj * 256:(jj + 1) * 256]
                                self.mm(o, wc[:, 0, q, j + 1, :], sprev[:, q, 0, 0:256], True, False,
                                        r=["wc", "sprev"], w=[("ps", jp % 2)], tp=(0, 32 * qi))
                                self.mm(o, wc[:, 1, q, j + 1, :], sprev[:, q, 1, 0:256], False, True,
                                        r=["wc", "sprev"], w=[("ps", jp % 2)], tp=(0, 32 * qi))
                        self.act(yc[:, 2 * jp:2 * jp + 2, :], pb[:].rearrange("p (j n) -> p j n", n=256), AF.Copy,
                                 r=[("ps", jp % 2)], w=[("ycr", t % 2)])
                    for ti in range(NTILE):
                        pb = self.ps[2 + cnt % 4]
                        pk = ("ps", 2 + cnt % 4)
                        ys = ysum[cnt % 2]
                        yk = ("ysum", cnt % 2)
                        cnt += 1
                        pv = pb[:].rearrange("p (c j) -> p c j", j=16)
                        uv = U[:, t, ti * 512:(ti + 1) * 512].rearrange("p (c j) -> p c j", j=16)
                        for tau in range(16):
                            self.mm(pv[:, :, tau:16], kbd[:, tau, t, :], uv[:, :, 0:16 - tau], tau == 0, tau == 15,
                                    r=["kbd", ("U", t, ti)], w=[pk])
                        ycv = yc[:, :, ti * 32:(ti + 1) * 32].rearrange("p j n -> p n j")
                        self.tt("dve", ys[:].rearrange("p (c j) -> p c j", j=16), pv, ycv, ALU.add,
                                r=[pk, ("ycr", t % 2)], w=[yk])
                        self.act(yg[:, t, ti * 512:(ti + 1) * 512], ys[:], AF.Gelu_apprx_tanh, r=[yk], w=[("yg", t, ti)])
                        if self.debug and s == 0:
                            self.load("sp", self.dbg("ylin", [128, 4, L])[:, t, ti * 512:(ti + 1) * 512], ys[:], r=[yk], w=[("dbgy", t, ti)])
                        if t == 3:
                            stageE(ti)
                            if ti > 0:
                                stageE2(ti - 1)
                            if ti == NTILE - 1:
                                stageE2(ti)
                if self.debug and s == 0:
                    self.load("sp", self.dbg("ys5", [128, 4 * L], BF16), U[:].rearrange("p t l -> p (t l)"), w=["dbgys5"],
                              r=[("ys5", to, ti) for to in range(4) for ti in range(NTILE)])
                kb.barrier()
                kb.emit()

        if getattr(self, "_es_wc", None) is not None:
            self.wc_pull(1000)
            self._es_wc.close()
            self._es_wc = None

    def eps_ap(self):
        return self._eps[:]

    def transpose_x(self, src, dst, skey, dkey, par):
        for k in range(8):
            pb = self.ps[(par * 8 + k) % 4]
            pk = ("ps", (par * 8 + k) % 4)
            for a in range(4):
                self.tr(pb[:, a * 128:(a + 1) * 128], src[:, a, k * 128:(k + 1) * 128], self.identf[:],
                        r=(list(skey) if isinstance(skey, list) else [skey]) + ["ident_f"], w=[pk], inc=(a == 3))
            if k % 2 == 0:
                self.act(dst[:, k, :], pb[:], AF.Copy, r=[pk], w=[dkey])
            else:
                self.cp("dve", dst[:, k, :], pb[:], r=[pk], w=[dkey])

    def layer_norm(self, xt, a, key, lnp, gi, stats, mv, sc):
        x = xt[:, a, :]
        for c in range(2):
            self.kb.op("dve", lambda eng, c=c: eng.bn_stats(out=stats[:, c, :], in_=xt[:, a, c * 512:(c + 1) * 512]),
                       reads=[key], writes=[("lnst", c)])
        self.kb.op("dve", lambda eng: eng.bn_aggr(out=mv[:], in_=stats[:].rearrange("p c s -> p (c s)")),
                   reads=[("lnst", 0), ("lnst", 1)], writes=["lnmv"])
        self.act(sc[:, 0:1], mv[:, 1:2], AF.Sqrt, r=["lnmv", "epsc"], w=["lnsc0"], bias=self.eps_ap())
        self.kb.op("dve", lambda eng: eng.reciprocal(out=sc[:, 0:1], in_=sc[:, 0:1]), reads=["lnsc0"], writes=["lnsc0"])
        self.stt(sc[:, 1:2], mv[:, 0:1], -1.0, sc[:, 0:1], ALU.mult, ALU.mult, r=["lnmv", "lnsc0"], w=["lnsc1"])
        self.act(x, x, AF.Identity, r=[key, "lnsc0", "lnsc1"], w=[key], bias=sc[:, 1:2], scale=sc[:, 0:1])
        self.tt("pool", x, x, lnp[:, gi, :], ALU.mult, r=[key, "lnp"], w=[key])
        self.tt("pool", x, x, lnp[:, gi + 1, :], ALU.add, r=[key, "lnp"], w=[key])

    def bankM(self):
        self._bm = (getattr(self, "_bm", 7) + 1 - 4) % 4 + 4
        return self._bm

    def ln2(self, x, key, g_ap, b_ap, stats, mv, sc, tag):
        for c in range(2):
            self.kb.op("dve", lambda eng, c=c: eng.bn_stats(out=stats[:, c, :], in_=x[:, c * 512:(c + 1) * 512]),
                       reads=[key], writes=[(tag + "st", c)])
        self.kb.op("dve", lambda eng: eng.bn_aggr(out=mv[:], in_=stats[:].rearrange("p c s -> p (c s)")),
                   reads=[(tag + "st", 0), (tag + "st", 1)], writes=[tag + "mv"])
        self.act(sc[:, 0:1], mv[:, 1:2], AF.Sqrt, r=[tag + "mv", "epsc"], w=[tag + "sc0"], bias=self.eps_ap())
        self.kb.op("dve", lambda eng: eng.reciprocal(out=sc[:, 0:1], in_=sc[:, 0:1]), reads=[tag + "sc0"], writes=[tag + "sc0"])
        self.stt(sc[:, 1:2], mv[:, 0:1], -1.0, sc[:, 0:1], ALU.mult, ALU.mult, r=[tag + "mv", tag + "sc0"], w=[tag + "sc1"])
        self.act(x, x, AF.Identity, r=[key, tag + "sc0", tag + "sc1"], w=[key], bias=sc[:, 1:2], scale=sc[:, 0:1])
        self.tt("pool", x, x, g_ap, ALU.mult, r=[key, "lng", "lng1"], w=[key])
        self.tt("pool", x, x, b_ap, ALU.add, r=[key, "lnb"], w=[key])

    def transpose_a(self, src, dst, a, skeys, dkey):
        for kg in range(2):
            b = self.bankM()
            pb, pk = self.ps[b], ("ps", b)
            for kk in range(4):
                k = 4 * kg + kk
                self.tr(pb[:, kk * 128:(kk + 1) * 128], src[:, a, k * 128:(k + 1) * 128], self.identf[:],
                        r=list(skeys) + ["ident_f"], w=[pk], inc=(kk == 3))
            dv = dst[:, 4 * kg:4 * kg + 4, a * 128:(a + 1) * 128]
            self.act(dv, pb[:].rearrange("p (k t) -> p k t", t=128), AF.Copy, r=[pk], w=[(dkey, a, kg)])

    def phase_tiles(self, s, ntile):
        kb, I, U, ps = self.kb, self.I, self.U, self.ps
        with ExitStack() as es:
            sb = lambda n, sh, d=F32: self.sb(es, n, sh, d)
            B = type("B", (), {})()
            B.wv = sb("wv", [128, 8, 512], BF16)
            B.winb = [sb("winb%d" % i, [128, 1024], BF16) for i in range(3)]
            B.woutb = [sb("woutb%d" % i, [128, 1024], BF16) for i in range(4)]
            B.lng = sb("lng", [128, 2, 1024])
            B.lnb = sb("lnb", [128, 2, 1024], BF16)
            B.maskT, B.xi, B.zeta, B.dec = sb("maskT", [128, 4, 128]), sb("xi", [128, 2, 64]), sb("zeta", [128, 4]), sb("dec", [128, 2])
            for f in range(4):
                self.load("pool", B.wv[:, :, f * 128:(f + 1) * 128], I["w_in_f"][8 + f].rearrange("p (k c) -> p k c", c=128), w=[("wv", f)])
            self.load("sp", B.lng[:, 0, :], I["lnp"][:, 0, :], w=["lng"])
            self.load("sp", B.lng[:, 1, :], I["lnp"][:, 2, :], w=["lng1"])
            self.load("sp", B.lnb[:].rearrange("p a d -> p (a d)"), self.lnb_s, w=["lnb"])
            for t_, nm in ((B.maskT, "maskT"), (B.xi, "xi"), (B.zeta, "zeta"), (B.dec, "dec")):
                self.load("sp", t_[:], I[nm], w=[nm])
            B.state = sb("state", [128, 2, 128])
            B.sbf = sb("sbf", [128, 8, 2, 128], BF16)
            B.qxz = sb("qxz", [128, 4, 512], BF16)
            B.halo = sb("halo", [128, 22, 2])
            self.memset("pool", B.state[:], 0.0, w=["state"])
            self.memset("pool", B.qxz[:], 0.0, w=[("qxz", h) for h in range(4)])
            self.memset("pool", B.halo[:], 0.0, w=[("halo", i) for i in range(22)])
            B.xtm = [sb("xtm%d" % i, [128, 4, 1024]) for i in range(2)]
            B.xTa = sb("xTa", [128, 8, 512], BF16)
            B.x1T = sb("x1T", [128, 8, 512], BF16)
            B.rope = sb("rope", [128, 2, 512])
            B.qT, B.kT = sb("qT", [128, 2, 512], BF16), sb("kT", [128, 2, 512], BF16)
            B.vtm = sb("vtm", [128, 4, 512], BF16)
            B.kz = sb("kz", [128, 4, 256], BF16)
            B.sgh = [sb("sgh%d" % i, [128, 512], BF16) for i in range(2)]
            B.sT = sb("sT", [128, 4, 4, 128], BF16)
            B.ftm = [sb("ftm%d" % i, [128, 512]) for i in range(2)]
            B.ftf = [sb("ftf%d" % i, [128, 512]) for i in range(3)]
            B.gsb = [sb("gsb%d" % i, [128, 512], BF16) for i in range(2)]
            B.obf, B.osq = sb("obf", [128, 512], BF16), sb("osq", [128, 512], BF16)
            B.yret = sb("yret", [128, 4, 512], BF16)
            B.actT = sb("actT", [128, 22, 512], BF16)
            B.wupb = [sb("wupb%d" % i, [128, 2048], BF16) for i in range(3)]
            B.wdnb = [sb("wdnb%d" % i, [128, 1024], BF16) for i in range(3)]
            B.st1, B.mv1, B.sc1 = sb("lnst1", [128, 2, 6]), sb("lnmv1", [128, 2]), sb("lnsc1", [128, 2])
            B.st2, B.mv2, B.sc2 = sb("lnst2", [128, 2, 6]), sb("lnmv2", [128, 2]), sb("lnsc2", [128, 2])
            if not hasattr(self, "_printed"):
                print("SBUF remaining in tile phase:", self.nc.sbuf_bytes_remaining)
                self._printed = True
            WINF = (4, 5, 6, 7, 12, 13, 14, 15)
            B.s_win = Stream(self, "pool", B.winb, "winb", lambda c: self.win_s[WINF[c]], 8)
            B.s_wout = Stream(self, "pool", B.woutb, "woutb", lambda c: self.wout_s[c], 8)
            B.s_wup = Stream(self, "sp", B.wupb, "wupb", lambda c: self.wup_s[c], 22)
            B.s_wdn = Stream(self, "sp", B.wdnb, "wdnb", lambda c: self.wdn_s[c], 22)
            self._B = B

            def drain(g):
                for _ in g:
                    pass

            def interleave(ga, gb, na=62.0, nb=62.0):
                da = db = False
                ca = cb = 0
                while not (da and db):
                    if not da and (db or ca / na <= cb / nb):
                        try:
                            next(ga)
                            ca += 1
                        except StopIteration:
                            da = True
                    elif not db:
                        try:
                            next(gb)
                            cb += 1
                        except StopIteration:
                            db = True

            drain(self.g_mixer(s, 0))
            for ti in range(ntile):
                if ti + 1 < ntile:
                    interleave(self.g_ffn(s, ti), self.g_mixer(s, ti + 1))
                else:
                    drain(self.g_ffn(s, ti))
            kb.barrier()
            kb.emit()

    def win_chunk(self, f):
        wb_, wk = self._B.s_win.get()
        return wb_[:].rearrange("p (k c) -> p k c", c=128), wk

    def g_mixer(self, s, ti):
        kb, I, U, ps, B = self.kb, self.I, self.U, self.ps, self._B
        par = ti % 2
        xtm, XK = B.xtm[par], ("xtm", par)
        xT, rope, qT, kT, vtm, kz, sT, ft = B.xTa, B.rope, B.qT, B.kT, B.vtm, B.kz, B.sT, B.ftm
        state, sbf, qxz, yret, obf, osq = B.state, B.sbf, B.qxz, B.yret, B.obf, B.osq
        maskT, xi, zeta, dec = B.maskT, B.xi, B.zeta, B.dec
        gn = self.vecs[:, 12:16]
        tsl = slice(ti * 512, (ti + 1) * 512)
        self.load("pool", xtm[:], I["x"][s, tsl, :].rearrange("(a p) d -> p a d", p=128), w=[XK])
        self.load("pool", rope[:], I["rope"][:, :, tsl], w=["rope"])
        for _ in range(5):
            yield
        for a in range(4):
            self.transpose_a(xtm, xT, a, [XK], "xTa")
            yield
        for f in range(4):
            wch, wk = self.win_chunk(4 + f)
            b = self.bankM()
            pb, pk = ps[b], ("ps", b)
            for k in range(8):
                self.mm(pb[:], wch[:, k, :], xT[:, k, :], k == 0, k == 7, r=[wk] + [("xTa", a_, g_) for a_ in range(4) for g_ in range(2)], w=[pk])
            self.tt("dve", ft[0][:], pb[:], rope[:, 0, :], ALU.mult, r=[pk, "rope"], w=[("ftm", 0)])
            for qd in range(4):
                src = (qd ^ 1) * 32
                self.tt("dve", ft[1][qd * 32:qd * 32 + 32, :], pb[src:src + 32, :], rope[qd * 32:qd * 32 + 32, 1, :], ALU.mult,
                        r=[pk, "rope"], w=[("ftm", 1)])
            dst = qT[:, f, :] if f < 2 else kT[:, f - 2, :]
            dk_ = ("qT", f) if f < 2 else ("kT", f - 2)
            self.tt("pool", dst, ft[0][:], ft[1][:], ALU.add, r=[("ftm", 0), ("ftm", 1)], w=[dk_])
            if f < 2:
                for hp in range(2):
                    h = 2 * f + hp
                    sl = slice(64 * hp, 64 * hp + 64)
                    self.tt("pool", qxz[sl, h, :].rearrange("p (n c) -> p n c", c=64), qT[sl, f, :].rearrange("p (n c) -> p n c", c=64),
                            xi[sl, f, :].unsqueeze(1).to_broadcast([64, 8, 64]), ALU.mult, r=[dk_, "xi"], w=[("qxz", h)])
            yield
        for a in range(4):
            b = self.bankM()
            pb, pk = ps[b], ("ps", b)
            for k in range(8):
                self.mm(pb[:], xT[:, k, a * 128:(a + 1) * 128], B.wv[:, k, :], k == 0, k == 7, r=[("wv", f_) for f_ in range(4)] + [("xTa", a, 0), ("xTa", a, 1)], w=[pk])
            self.cp("act", vtm[:, a, :], pb[:], r=[pk], w=[("vtm", a)])
            yield
        for a in range(4):
            asl = slice(a * 128, (a + 1) * 128)
            b = self.bankM()
            pbb, pk = ps[b][:].bitcast(BF16), ("ps", b)
            for kt in range(2):
                self.tr(pbb[:, kt * 128:(kt + 1) * 128], kT[:, kt, asl], self.identb[:], r=[("kT", kt), "ident_b"], w=[pk], inc=(kt == 1))
            self.tt("dve", kz[:, a, :].rearrange("p (h d) -> p h d", d=64), pbb[:, 0:256].rearrange("p (h d) -> p h d", d=64),
                    zeta[:].unsqueeze(2).to_broadcast([128, 4, 64]), ALU.mult, r=[pk, "zeta"], w=[("kz", a)])
            for hp in range(2):
                hb = 64 * hp
                b = self.bankM()
                pb, pk = ps[b], ("ps", b)
                for hh in range(2):
                    self.kb.op("pe", lambda eng, pb=pb, hb=hb, hh=hh, asl=asl: eng.matmul(
                        pb[:, hh * 128:(hh + 1) * 128], lhsT=kT[hb:hb + 64, hh, asl], rhs=qT[hb:hb + 64, hh, asl],
                        start=True, stop=True, tile_position=(hb, 0)),
                        reads=[("kT", hh), ("qT", hh)], writes=[pk], inc=(hh == 1))
                sv_ = sT[:, a, :, :].rearrange("p (hh par) c -> p par hh c", par=2)[:, hp]
                mv_ = maskT[:].rearrange("p (hh par) c -> p par hh c", par=2)[:, hp]
                self.tt("dve", sv_, pb[:, 0:256].rearrange("p (h c) -> p h c", c=128), mv_, ALU.mult, r=[pk, "maskT"], w=[("sT", a, hp)])
            yield
        if ti == 0:
            self.memset("pool", state[:], 0.0, w=["state"])
        for n in range(8):
            a, tb = n // 2, 64 * (n % 2)
            self.cp("act", sbf[:, n, :, :], state[:], r=["state"], w=[("sbf", n)])
            b = self.bankM()
            pb, pk = ps[b], ("ps", b)
            for h in range(4):
                hb, hh = 64 * (h % 2), h // 2
                self.kb.op("pe", lambda eng, pb=pb, h=h, hb=hb, hh=hh, a=a, tb=tb: eng.matmul(
                    pb[hb:hb + 64, hh * 128:(hh + 1) * 128], lhsT=kz[tb:tb + 64, a, h * 64:(h + 1) * 64],
                    rhs=vtm[tb:tb + 64, a, h * 128:(h + 1) * 128], start=True, stop=True, tile_position=(tb, hb)),
                    reads=[("kz", a), ("vtm", a)], writes=[pk], inc=(h == 3))
            for hh in range(2):
                self.stt(state[:, hh, :], state[:, hh, :], dec[:, hh:hh + 1], pb[:, hh * 128:(hh + 1) * 128], ALU.mult, ALU.add,
                         r=["state", pk, "dec", ("sbf", n)], w=["state"])
            if n % 2 == 1:
                yield
        for h in range(4):
            hh = h // 2
            wch, wk = self.win_chunk(12 + h)
            b = self.bankM()
            pb, pk = ps[b], ("ps", b)
            for k in range(8):
                self.mm(pb[:], wch[:, k, :], xT[:, k, :], k == 0, k == 7, r=[wk] + [("xTa", a_, g_) for a_ in range(4) for g_ in range(2)], w=[pk])
            sg, sgk = B.sgh[h % 2], ("sgh", h % 2)
            self.act(sg[:], pb[:], AF.Silu, r=[pk], w=[sgk])
            yield
            b = self.bankM()
            po, pk = ps[b], ("ps", b)
            for a in range(4):
                self.kb.op("pe", lambda eng, po=po, a=a, h=h: eng.matmul(
                    po[:, a * 128:(a + 1) * 128], lhsT=vtm[:, a, h * 128:(h + 1) * 128], rhs=sT[:, a, h, :], start=True, stop=False),
                    reads=[("vtm", a), ("sT", a, h % 2)], writes=[pk], inc=False)
                for half in range(2):
                    n = 2 * a + half
                    self.kb.op("pe", lambda eng, po=po, a=a, half=half, n=n, h=h, hh=hh: eng.matmul(
                        po[:, a * 128 + 64 * half:a * 128 + 64 * half + 64], lhsT=sbf[:, n, hh, :],
                        rhs=qxz[:, h, n * 64:(n + 1) * 64], start=False, stop=(half == 1)),
                        reads=[("sbf", n), ("qxz", h)], writes=[pk], inc=(a == 3 and half == 1))
            self.act(obf[:], po[:], AF.Copy, r=[pk], w=["obf"])
            self.act(osq[:], po[:], AF.Square, r=[pk], w=["osq"])
            self.act(ft[0][:], po[:], AF.Copy, r=[pk], w=[("ftm", 0)])
            yield
            b1 = self.bankM()
            pm, pmk = ps[b1], ("ps", b1)
            self.mm(pm[:], self.ones128[:], obf[:], True, True, r=["ones128", "obf"], w=[pmk])
            b2 = self.bankM()
            pq, pqk = ps[b2], ("ps", b2)
            self.mm(pq[:], self.ones128[:], osq[:], True, True, r=["ones128", "osq"], w=[pqk])
            self.act(ft[1][:], pm[:], AF.Square, r=[pmk], w=[("ftm", 1)])
            self.tt("dve", ft[1][:], pq[:], ft[1][:], ALU.subtract, r=[pqk, ("ftm", 1)], w=[("ftm", 1)])
            self.act(ft[1][:], ft[1][:], AF.Ln, r=[("ftm", 1), "epsc"], w=[("ftm", 1)], bias=self.eps_ap())
            self.act(ft[1][:], ft[1][:], AF.Exp, r=[("ftm", 1)], w=[("ftm", 1)], scale=-0.5)
            self.tt("dve", ft[0][:], ft[0][:], pm[:], ALU.subtract, r=[("ftm", 0), pmk, "obf"], w=[("ftm", 0)])
            self.tt("pool", ft[0][:], ft[0][:], ft[1][:], ALU.mult, r=[("ftm", 0), ("ftm", 1)], w=[("ftm", 0)])
            self.stt(yret[:, h, :], ft[0][:], gn[:, h:h + 1], sg[:], ALU.mult, ALU.mult,
                     r=[("ftm", 0), sgk, "gngain"], w=[("yret", h)])
            yield
        if self.debug and s == 0:
            self.load("sp", self.dbg("yret", [128, 4, L], BF16)[:, :, tsl], yret[:], r=[("yret", h) for h in range(4)], w=[("dbgyr", ti)])
        for qn in range(8):
            wo_, wok = B.s_wout.get()
            wo = wo_[:].rearrange("p (k c) -> p k c", c=128)
            for a in range(4):
                b = self.bankM()
                pb, pk = ps[b], ("ps", b)
                for k in range(8):
                    lhsT = U[:, k, ti * 512 + a * 128:ti * 512 + (a + 1) * 128] if k < 4 else yret[:, k - 4, a * 128:(a + 1) * 128]
                    rk = "Uall" if k < 4 else ("yret", k - 4)
                    self.mm(pb[:, 0:128], lhsT, wo[:, k, :], k == 0, k == 7, r=[wok, rk], w=[pk])
                xs = xtm[:, a, qn * 128:(qn + 1) * 128]
                self.stt(xs, xs, ALPHA, pb[:, 0:128], ALU.mult, ALU.add, r=[XK, pk], w=[("x1", par, a, qn)])
            if qn % 2 == 1:
                yield
                yield
        def ln1(a):
            kk = ("x1", par, a)
            kb.reg[kk] = kb.reg[("x1", par, a, 7)]
            self.ln2(xtm[:, a, :], kk, B.lng[:, 0, :], B.lnb[:, 0, :], B.st1, B.mv1, B.sc1, "l1")
        ln1(0)
        yield
        ln1(1)
        for _ in range(4):
            yield
        for a in range(4):
            if a + 2 < 4:
                ln1(a + 2)
            for _ in range(3):
                yield
            self.transpose_a(xtm, B.x1T, a, [("x1", par, a)], "x1T")
            yield
        if self.debug and s == 0:
            self.load("sp", self.dbg("x1", [L, 1024])[tsl, :].rearrange("(a p) d -> p a d", p=128), xtm[:],
                      r=[("x1", par, a) for a in range(4)], w=[("dbgx1", ti)])

    def g_ffn(self, s, ti):
        kb, I, ps, B = self.kb, self.I, self.ps, self._B
        par = ti % 2
        xtm = B.xtm[par]
        xT, actT, halo, ft = B.x1T, B.actT, B.halo, B.ftf
        tsl = slice(ti * 512, (ti + 1) * 512)
        cw = self.convw
        for i in range(22):
            wb_, wk = B.s_wup.get()
            ba, bg = (0, 1) if i % 2 == 0 else (2, 3)
            pa, pg = ps[ba], ps[bg]
            pak, pgk = ("ps", ba), ("ps", bg)
            for k in range(8):
                self.mm(pa[:], wb_[:, k * 256:k * 256 + 128], xT[:, k, :], k == 0, k == 7, r=[wk] + [("x1T", a_, g_) for a_ in range(4) for g_ in range(2)], w=[pak])
            for k in range(8):
                self.mm(pg[:], wb_[:, k * 256 + 128:k * 256 + 256], xT[:, k, :], k == 0, k == 7, r=[wk] + [("x1T", a_, g_) for a_ in range(4) for g_ in range(2)], w=[pgk])
            ct, ck = ft[0], ("ftf", 0)
            st, sk = ft[1 + i % 2], ("ftf", 1 + i % 2)
            self.act(ct[:], pa[:], AF.Identity, r=[pak, "convw", "convb"], w=[ck], bias=self.convb[:, i:i + 1], scale=cw[:, i, 2:3])
            self.stt(ct[:, 1:512], pa[:, 0:511], cw[:, i, 1:2], ct[:, 1:512], ALU.mult, ALU.add, r=[pak, ck], w=[ck])
            self.stt(ct[:, 2:512], pa[:, 0:510], cw[:, i, 0:1], ct[:, 2:512], ALU.mult, ALU.add, r=[pak, ck], w=[ck])
            self.stt(ct[:, 0:1], halo[:, i, 1:2], cw[:, i, 1:2], ct[:, 0:1], ALU.mult, ALU.add, r=[("halo", i), ck], w=[ck])
            self.stt(ct[:, 0:2], halo[:, i, 0:2], cw[:, i, 0:1], ct[:, 0:2], ALU.mult, ALU.add, r=[("halo", i), ck], w=[ck])
            self.cp("dve", halo[:, i, :], pa[:, 510:512], r=[pak, ck], w=[("halo", i)])
            gs, gk = B.gsb[i % 2], ("gsb", i % 2)
            self.act(gs[:], pg[:], AF.Copy, r=[pgk], w=[gk])
            self.act(st[:], ct[:], AF.Silu, r=[ck], w=[sk])
            self.tt("pool", actT[:, i, :], st[:], gs[:], ALU.mult, r=[sk, gk], w=[("actT", i)])
            yield
        if self.debug and s == 0:
            self.load("sp", self.dbg("act", [128, 22, L], BF16)[:, :, tsl], actT[:], r=[("actT", i) for i in range(22)], w=[("dbgact", ti)])
        for pss in range(2):
            for i in range(22):
                wd, wk = B.s_wdn.get()
                for aa in range(2):
                    a = 2 * pss + aa
                    for nh in range(2):
                        bi = 2 * aa + nh
                        self.kb.op("pe", lambda eng, bi=bi, i=i, a=a, nh=nh, wd=wd: eng.matmul(
                            ps[bi][:], lhsT=actT[:, i, a * 128:(a + 1) * 128], rhs=wd[:, nh * 512:(nh + 1) * 512],
                            start=(i == 0), stop=(i == 21)), reads=[("actT", i), wk], writes=[("ps", bi)], inc=(i == 21 or bi == 3))
                yield
            for aa in range(2):
                a = 2 * pss + aa
                for nh in range(2):
                    bi = 2 * aa + nh
                    xs = xtm[:, a, nh * 512:(nh + 1) * 512]
                    self.stt(xs, xs, ALPHA, ps[bi][:], ALU.mult, ALU.add, r=[("x1", par, a), ("ps", bi)], w=[("x2", par, a, nh)])
                kk = ("x2", par, a)
                kb.reg[kk] = kb.reg[("x2", par, a, 1)]
                self.ln2(xtm[:, a, :], kk, B.lng[:, 1, :], B.lnb[:, 1, :], B.st2, B.mv2, B.sc2, "l2")
                self.load("pool", self.out[s, ti * 512 + a * 128:ti * 512 + (a + 1) * 128, :], xtm[:, a, :], r=[kk, ("xtm", par)], w=[("out", ti, a)])
                yield


def build(**kw):
    return Prog(**kw)


_CACHE = {}


def kernel(**inputs):
    consts = host_consts()
    wts = host_weights(inputs)
    x = np.ascontiguousarray(inputs["x"]).astype(np.float32)
    if "prog" not in _CACHE:
        _CACHE["prog"] = build()
    prog = _CACHE["prog"]
    in_maps = []
    for c in range(8):
        m = dict(consts)
        m.update(wts)
        m["x"] = x[2 * c:2 * c + 2]
        in_maps.append(m)
    res = run_bass_kernel_spmd(prog.nc, in_maps, core_ids=list(range(8)))
    out = np.concatenate([r["out"] for r in res.results], axis=0)
    return out.astype(np.float32)
```

```python
import math
from contextlib import ExitStack
import numpy as np
import ml_dtypes
import concourse.bass as bass
import concourse.mybir as mybir
from concourse.bass_utils import run_bass_kernel_spmd

F32 = mybir.dt.float32
BF16 = mybir.dt.bfloat16
AF = mybir.ActivationFunctionType
ALU = mybir.AluOpType
NPBF = ml_dtypes.bfloat16

ENGS = ("pe", "act", "dve", "pool", "sp")
EPOCH = 12000
NDMA = 16
ALPHA = 2.0 ** 0.25
EPS = 1e-5
L = 4096
NTILE = 8
NSEQ = 2
import os as _os
TSTOP = int(_os.environ.get('TSTOP', '0'))


class KB:
    def __init__(self, nc):
        self.nc = nc
        self.ops = {e: [] for e in ENGS}
        self.cnt = {e: 0 for e in ENGS}
        self.esems = {e: [] for e in ENGS}
        self.seen = {e: {} for e in ENGS}
        self.reg = {}
        self.dsem = {}
        self.dptr = {e: 0 for e in ENGS}
        self.bulk = []
        self.keep = ("wup_s", "wdn_s")

    def new_sem(self, name):
        return self.nc.alloc_semaphore(name=name)

    def _esem(self, e, idx):
        ep = idx // EPOCH
        while len(self.esems[e]) <= ep:
            self.esems[e].append(self.new_sem(f"p_{e}_{len(self.esems[e])}"))
        return self.esems[e][ep], idx - ep * EPOCH

    def _waits_for(self, e, reads, writes):
        need = {}

        def add(t, same_ok):
            if t is None:
                return
            s, v, src = t
            if src == e and same_ok:
                return
            k = id(s)
            if self.seen[e].get(k, 0) >= v:
                return
            if k not in need or need[k][1] < v:
                need[k] = (s, v)

        for r in reads:
            ent = self.reg.get(r)
            if ent:
                add(ent[0], same_ok=(e == "pe"))
        for w in writes:
            ent = self.reg.get(w)
            if ent:
                add(ent[0], same_ok=True)
                for t in ent[1]:
                    add(t, same_ok=True)
        out = []
        for k, (s, v) in need.items():
            self.seen[e][k] = v
            out.append((s, v))
        return out

    def _commit(self, ticket, reads, writes):
        for r in reads:
            self.reg.setdefault(r, [None, []])[1].append(ticket)
        for w in writes:
            self.reg[w] = [ticket, []]

    def op(self, e, fn, reads=(), writes=(), inc=True):
        waits = self._waits_for(e, reads, writes)
        s, local = self._esem(e, self.cnt[e])
        ticket = (s, local + 1, e)
        self._commit(ticket, reads, writes)
        if inc:
            self.cnt[e] += 1
            self.ops[e].append((waits, fn, (s, 1)))
        else:
            self.ops[e].append((waits, fn, None))

    def dma(self, q, out, in_, reads=(), writes=(), bulk=False, **kw):
        if bulk:
            ent = [self.new_sem(f"db_{len(self.bulk)}"), 0]
            self.bulk.append(ent)
        else:
            if q not in self.dsem:
                self.dsem[q] = [[self.new_sem(f"d_{q}_{i}"), 0] for i in range(NDMA)]
            i = self.dptr[q]
            self.dptr[q] = (i + 1) % NDMA
            ent = self.dsem[q][i]
        s, uses = ent
        waits = self._waits_for(q, reads, writes)
        k = id(s)
        if uses > 0 and self.seen[q].get(k, 0) < 16 * uses:
            waits.append((s, 16 * uses))
            self.seen[q][k] = 16 * uses
        ent[1] = uses + 1
        ticket = (s, 16 * (uses + 1), "dma")
        self._commit(ticket, reads, writes)

        def fn(eng, out=out, in_=in_, kw=kw):
            return eng.dma_start(out=out, in_=in_, **kw)

        self.ops[q].append((waits, fn, (s, 16)))

    def barrier(self, final=False):
        ticks = []
        if final:
            for s, uses in self.bulk:
                ticks.append((s, 16 * uses))
        for e in ENGS:
            if self.cnt[e] > 0:
                s, local = self._esem(e, self.cnt[e] - 1)
                ticks.append((s, local + 1))
        for q, lst in self.dsem.items():
            for s, uses in lst:
                if uses > 0:
                    ticks.append((s, 16 * uses))
        for e in ENGS:
            w = []
            for s, v in ticks:
                if self.seen[e].get(id(s), 0) < v:
                    w.append((s, v))
                    self.seen[e][id(s)] = v
            self.ops[e].append((w, None, None))
        self.reg = {k: v for k, v in self.reg.items() if isinstance(k, tuple) and k[0] in self.keep}

    def emit(self):
        nc = self.nc
        with nc.Block() as block:
            def run(e):
                def body(eng):
                    for waits, fn, inc in self.ops[e]:
                        for (s, v) in waits:
                            eng.wait_ge(s, v)
                        if fn is not None:
                            inst = fn(eng)
                            if inc is not None:
                                inst.then_inc(inc[0], inc[1])
                return body
            block.tensor(run("pe"))
            block.scalar(run("act"))
            block.vector(run("dve"))
            block.gpsimd(run("pool"))
            block.sync(run("sp"))
        self.ops = {e: [] for e in ENGS}


def host_consts():
    c = {}
    c["ident_f"] = np.eye(128, dtype=np.float32)
    c["ident_b"] = np.eye(128, dtype=np.float32).astype(NPBF)
    c["ones512"] = np.full((128, 128), 1.0 / 512, np.float32).astype(NPBF)
    c["ones128"] = np.full((128, 128), 1.0 / 128, np.float32).astype(NPBF)
    c["ev17"] = np.tile(np.arange(17, dtype=np.float32)[None], (128, 1))
    c["ev16r"] = np.tile((15 - np.arange(16)).astype(np.float32)[None], (128, 1))
    c["nv32"] = np.tile(np.arange(1, 33, dtype=np.float32)[None], (128, 1))
    freqs = (np.float32(10000.0) ** (-(np.arange(32, dtype=np.float32)) / np.float32(32))).astype(np.float32)
    pos = np.arange(L, dtype=np.float32)
    ang = (pos[None, :] * freqs[:, None]).astype(np.float32).astype(np.float64)
    rope = np.zeros((128, 2, L), np.float32)
    for p in range(128):
        f = p % 32
        hf = (p // 32) % 2
        rope[p, 0] = np.cos(ang[f])
        rope[p, 1] = np.sin(ang[f]) * (-1.0 if hf == 0 else 1.0)
    c["rope"] = rope
    lg = np.log1p(-(2.0 ** (-5.0 - np.arange(4, dtype=np.float64))))
    m = np.arange(128)
    mask = np.zeros((128, 4, 128), np.float64)
    same = (m[:, None] // 64) == (m[None, :] // 64)
    for h in range(4):
        mask[:, h, :] = np.where(same, np.exp(np.abs(m[:, None] - m[None, :]) * lg[h]) / 8.0, 0.0)
    c["maskT"] = mask.astype(np.float32)
    xi = np.zeros((128, 2, 64), np.float64)
    dec = np.zeros((128, 2), np.float64)
    for p in range(128):
        for qt in range(2):
            h = 2 * qt + p // 64
            xi[p, qt] = np.exp((np.arange(64) + 1.0) * lg[h])
            dec[p, qt] = np.exp(64.0 * lg[h])
    c["xi"] = xi.astype(np.float32)
    c["dec"] = dec.astype(np.float32)
    zeta = np.zeros((128, 4), np.float64)
    for h in range(4):
        zeta[:, h] = np.exp((63 - (m % 64)) * lg[h]) / 8.0
    c["zeta"] = zeta.astype(np.float32)
    return c


def host_weights(inp):
    f = np.float32
    w = {}
    w["w_in_f"] = np.ascontiguousarray(inp["w_in"][0].reshape(8, 128, 16, 128).transpose(2, 1, 0, 3)).reshape(16, 128, 1024).astype(f)
    w["w_out_h"] = np.ascontiguousarray(inp["w_out"][0].reshape(8, 128, 8, 128).transpose(2, 1, 0, 3)).reshape(8, 128, 1024).astype(f)
    w["glu_w_r"] = np.ascontiguousarray(inp["s5_glu_w"][0].reshape(4, 128, 512).transpose(1, 0, 2)).astype(f)
    wu = inp["ffn_w_up"][0].reshape(8, 128, 2, 22, 128)
    w["w_up_r"] = np.ascontiguousarray(wu.transpose(3, 1, 0, 2, 4)).reshape(22, 128, 2048).astype(f)
    w["w_down_r"] = np.ascontiguousarray(inp["ffn_w_down"][0].reshape(22, 128, 1024)).astype(f)
    lre, lim, ls = inp["s5_lambda_re"][0], inp["s5_lambda_im"][0], inp["s5_log_step"][0]
    lamc = np.zeros((128, 3, 16), f)
    bcol = np.zeros((128, 2, 16, 32), f)
    ccol = np.zeros((128, 2, 16, 32), f)
    bre, bim, cre, cim = inp["s5_b_re"][0], inp["s5_b_im"][0], inp["s5_c_re"][0], inp["s5_c_im"][0]
    for q in range(16):
        for gi in range(2):
            g = 2 * q + gi
            sl = slice(64 * gi, 64 * gi + 64)
            lamc[sl, 0, q] = lre[g]
            lamc[sl, 1, q] = lim[g]
            lamc[sl, 2, q] = ls[g]
            bcol[sl, 0, q, 16 * gi:16 * gi + 16] = bre[g]
            bcol[sl, 1, q, 16 * gi:16 * gi + 16] = bim[g]
            ccol[sl, 0, q, 16 * gi:16 * gi + 16] = cre[g].T
            ccol[sl, 1, q, 16 * gi:16 * gi + 16] = cim[g].T
    w["lamc"], w["bcol"], w["ccol"] = lamc, bcol, ccol
    lamr = np.zeros((128, 3, 4, 128), f)
    brow = np.zeros((128, 2, 4, 128), f)
    for t in range(4):
        for qi in range(4):
            for gi2 in range(2):
                g2 = 8 * t + 2 * qi + gi2
                cs = slice(64 * gi2, 64 * gi2 + 64)
                rs = slice(32 * qi, 32 * qi + 32)
                lamr[rs, 0, t, cs] = lre[g2][None, :]
                lamr[rs, 1, t, cs] = lim[g2][None, :]
                lamr[rs, 2, t, cs] = ls[g2]
                rr = slice(32 * qi + 16 * gi2, 32 * qi + 16 * gi2 + 16)
                brow[rr, 0, t, cs] = bre[g2].T
                brow[rr, 1, t, cs] = bim[g2].T
    vec = lambda a, n: np.ascontiguousarray(a.reshape(n, 128).T).astype(f)
    w["s5d"] = vec(inp["s5_d"][0], 4)
    w["glub"] = vec(inp["s5_glu_b"][0], 4)
    w["s5gain"] = vec(inp["s5_out_gain"][0], 4)
    w["gngain"] = vec(inp["ret_gn_gain"][0], 4)
    w["convw"] = np.ascontiguousarray(inp["ffn_conv_w"][0].reshape(3, 22, 128).transpose(2, 1, 0)).astype(f)
    w["convb"] = vec(inp["ffn_conv_b"][0], 22)
    lnp = np.stack([inp["ln1_g"][0], inp["ln1_b"][0], inp["ln2_g"][0], inp["ln2_b"][0]], 0)
    w["lnp"] = np.ascontiguousarray(np.broadcast_to(lnp[None], (128, 4, 1024))).astype(f)
    return w


IN_SPECS = {
    "x": ([NSEQ, L, 1024], F32),
    "w_in_f": ([16, 128, 1024], F32), "w_out_h": ([8, 128, 1024], F32), "glu_w_r": ([128, 4, 512], F32),
    "w_up_r": ([22, 128, 2048], F32), "w_down_r": ([22, 128, 1024], F32),
    "lamc": ([128, 3, 16], F32), "bcol": ([128, 2, 16, 32], F32), "ccol": ([128, 2, 16, 32], F32),
    "s5d": ([128, 4], F32), "glub": ([128, 4], F32), "s5gain": ([128, 4], F32), "gngain": ([128, 4], F32),
    "convw": ([128, 22, 3], F32), "convb": ([128, 22], F32), "lnp": ([128, 4, 1024], F32),
    "ident_f": ([128, 128], F32), "ident_b": ([128, 128], BF16), "ones512": ([128, 128], BF16),
    "ones128": ([128, 128], BF16), "ev17": ([128, 17], F32), "ev16r": ([128, 16], F32), "nv32": ([128, 32], F32),
    "rope": ([128, 2, L], F32), "maskT": ([128, 4, 128], F32), "xi": ([128, 2, 64], F32),
    "dec": ([128, 2], F32), "zeta": ([128, 4], F32),
}


class Stream:
    def __init__(self, prog, q, bufs, name, src_fn, ncyc):
        self.p, self.q, self.bufs, self.name, self.src_fn, self.ncyc = prog, q, bufs, name, src_fn, ncyc
        self.issued = 0
        self.consumed = 0

    def _issue(self):
        m = self.issued
        self.issued += 1
        nb = len(self.bufs)
        self.p.load(self.q, self.bufs[m % nb][:], self.src_fn(m % self.ncyc), w=[(self.name, m % nb)])

    def get(self):
        n = self.consumed
        nb = len(self.bufs)
        while self.issued < n + nb:
            self._issue()
        self.consumed += 1
        return self.bufs[n % nb], (self.name, n % nb)


class Prog:
    def __init__(self, stages=("prep", "s5", "tile"), nseq=NSEQ, ntile=NTILE, debug=False):
        self.nc = nc = bass.Bass("TRN2", target_bir_lowering=False)
        self.kb = KB(nc)
        self.debug = debug
        self.I = {}
        for name, (shape, dt) in IN_SPECS.items():
            self.I[name] = nc.dram_tensor(name, shape, dt, kind="ExternalInput").ap()
        self.out = nc.dram_tensor("out", [NSEQ, L, 1024], F32, kind="ExternalOutput").ap()
        self.D = {}
        self.wup_s = nc.dram_tensor("wup_s", [22, 128, 2048], BF16, kind="Internal").ap()
        self.wdn_s = nc.dram_tensor("wdn_s", [22, 128, 1024], BF16, kind="Internal").ap()
        self.win_s = nc.dram_tensor("win_s", [16, 128, 1024], BF16, kind="Internal").ap()
        self.wout_s = nc.dram_tensor("wout_s", [8, 128, 1024], BF16, kind="Internal").ap()
        self.lnb_s = nc.dram_tensor("lnb_s", [128, 2048], BF16, kind="Internal").ap()
        self.kbd_s = nc.dram_tensor("kbd_s", [128, 16 * 4 * 128], BF16, kind="Internal").ap()
        self.wb_s = nc.dram_tensor("wb_s", [128, 16 * 4 * 2 * 128], BF16, kind="Internal").ap()
        self.wc_s = nc.dram_tensor("wc_s", [128, 2 * 16 * 17 * 32], BF16, kind="Internal").ap()
        self.sc_s = nc.dram_tensor("sc_s", [128, 16 * 32 * 3 + 16 * 2], F32, kind="Internal").ap()
        self.ps = [nc.alloc_psum_tensor(f"ps{i}", [128, 512], F32) for i in range(8)]
        self.uid = 0
        with ExitStack() as es:
            self.es_perm = es
            self.alloc_perm()
            if "prep" in stages:
                self.phase_prep()
            for s in range(nseq):
                if "s5" in stages:
                    self.phase_s5(s)
                if "tile" in stages:
                    self.phase_tiles(s, ntile)
            self.kb.barrier(final=True)
            self.kb.emit()

    def sb(self, es, name, shape, dt):
        self.uid += 1
        return es.enter_context(self.nc.sbuf_tensor(f"{name}_{self.uid}", shape, dt))

    def dbg(self, name, shape, dt=F32):
        if name not in self.D:
            self.D[name] = self.nc.dram_tensor("dbg_" + name, shape, dt, kind="ExternalOutput").ap()
        return self.D[name]

    def tt(self, e, out, a, b, op, r, w):
        self.kb.op(e, lambda eng: eng.tensor_tensor(out=out, in0=a, in1=b, op=op), reads=r, writes=w)

    def ts(self, e, out, a, s1, s2, op0, op1, r, w):
        if op1 is None:
            self.kb.op(e, lambda eng: eng.tensor_scalar(out=out, in0=a, scalar1=s1, scalar2=None, op0=op0), reads=r, writes=w)
        else:
            self.kb.op(e, lambda eng: eng.tensor_scalar(out=out, in0=a, scalar1=s1, scalar2=s2, op0=op0, op1=op1), reads=r, writes=w)

    def stt(self, out, a, sc, b, op0, op1, r, w):
        self.kb.op("dve", lambda eng: eng.scalar_tensor_tensor(out=out, in0=a, scalar=sc, in1=b, op0=op0, op1=op1), reads=r, writes=w)

    def act(self, out, in_, func, r, w, bias=None, scale=None):
        kw = {}
        if bias is not None:
            kw["bias"] = bias
        if scale is not None:
            kw["scale"] = scale
        self.kb.op("act", lambda eng: eng.activation(out=out, in_=in_, func=func, **kw), reads=r, writes=w)

    def cp(self, e, out, in_, r, w):
        if e == "act":
            self.kb.op(e, lambda eng: eng.activation(out=out, in_=in_, func=AF.Copy), reads=r, writes=w)
        else:
            self.kb.op(e, lambda eng: eng.tensor_copy(out=out, in_=in_), reads=r, writes=w)

    def mm(self, out, lhsT, rhs, start, stop, r, w, tp=None):
        kw = {}
        if tp is not None:
            kw["tile_position"] = tp
        self.kb.op("pe", lambda eng: eng.matmul(out, lhsT=lhsT, rhs=rhs, start=start, stop=stop, **kw),
                   reads=r, writes=w, inc=stop)

    def tr(self, out, in_, ident, r, w, inc=True):
        self.kb.op("pe", lambda eng: eng.transpose(out, in_, ident), reads=r, writes=w, inc=inc)

    def memset(self, e, ap, val, w):
        self.kb.op(e, lambda eng: eng.memset(ap, val), writes=w)

    def load(self, q, dst, src, w, r=(), bulk=False):
        self.kb.dma(q, dst, src, reads=r, writes=w, bulk=bulk)

    def alloc_perm(self):
        es = self.es_perm
        sb = lambda n, s, d: self.sb(es, n, s, d)
        self.identf = sb("identf", [128, 128], F32)
        self.identb = sb("identb", [128, 128], BF16)
        self.ones512 = sb("ones512", [128, 128], BF16)
        self.ones128 = sb("ones128", [128, 128], BF16)
        self.U = sb("U", [128, 4, L], BF16)
        self.vecs = sb("vecs", [128, 16], F32)
        self.convw = sb("convw", [128, 22, 3], F32)
        self.convb = sb("convb", [128, 22], F32)
        self._eps = sb("epsc", [128, 1], F32)
        self.memset("pool", self._eps[:], EPS, w=["epsc"])
        for dst, nm in ((self.identf, "ident_f"), (self.identb, "ident_b"), (self.ones512, "ones512"),
                        (self.ones128, "ones128"), (self.convw, "convw"), (self.convb, "convb")):
            self.load("sp", dst[:], self.I[nm], w=[nm])
        for i, nm in enumerate(("s5d", "glub", "s5gain", "gngain")):
            self.load("sp", self.vecs[:, 4 * i:4 * i + 4], self.I[nm], w=[nm])

    def g_wcast(self, st):
        if True:
            NB = len(st)
            n = 0
            for i in range(22):
                for (src, dst, cols, key) in ((self.I["w_up_r"], self.wup_s, 2048, "wup_s"), (self.I["w_down_r"], self.wdn_s, 1024, "wdn_s")):
                    b = st[n % NB]
                    bk = ("wst", n % NB)
                    n += 1
                    self.load("pool", b[:, 0:cols], src[i], w=[bk])
                    self.load("sp", dst[i], b[:, 0:cols], r=[bk], w=[(key, i)])
                    yield
            for f in (4, 5, 6, 7, 12, 13, 14, 15):
                b = st[n % NB]
                bk = ("wst", n % NB)
                n += 1
                self.load("pool", b[:, 0:1024], self.I["w_in_f"][f], w=[bk])
                self.load("sp", self.win_s[f], b[:, 0:1024], r=[bk], w=[("win_s", f)])
                yield
            for qn in range(8):
                b = st[n % NB]
                bk = ("wst", n % NB)
                n += 1
                self.load("pool", b[:, 0:1024], self.I["w_out_h"][qn], w=[bk])
                self.load("sp", self.wout_s[qn], b[:, 0:1024], r=[bk], w=[("wout_s", qn)])
                yield
            for j, row in enumerate((1, 3)):
                b = st[n % NB]
                bk = ("wst", n % NB)
                n += 1
                self.load("pool", b[:, 0:1024], self.I["lnp"][:, row, :], w=[bk])
                self.load("sp", self.lnb_s[:, j * 1024:(j + 1) * 1024], b[:, 0:1024], r=[bk], w=[("lnb_s", j)])
                yield

    def wc_pull(self, n):
        g = getattr(self, "_wcg", None)
        if g is None:
            return
        for _ in range(n):
            try:
                next(g)
            except StopIteration:
                self._wcg = None
                return

    def sincos(self, es, tag, ang, shape, s_out, c_out):
        k = self.sb(es, tag + "k", shape, F32)
        r = self.sb(es, tag + "r", shape, F32)
        kk, rr = k[:], r[:]
        M = 12582912.0
        TWO_PI = 2 * math.pi
        C1 = 6.28125
        C2 = float(np.float32(TWO_PI - C1))
        C3 = TWO_PI - C1 - C2
        PIS = 3.1415925
        A = tag + "ang"
        self.ts("dve", kk, ang, 1.0 / TWO_PI, M, ALU.mult, ALU.add, r=[A], w=[tag + "k"])
        self.ts("dve", kk, kk, M, None, ALU.subtract, None, r=[tag + "k"], w=[tag + "k"])
        self.stt(rr, kk, -C1, ang, ALU.mult, ALU.add, r=[A, tag + "k"], w=[tag + "r"])
        self.stt(rr, kk, -C2, rr, ALU.mult, ALU.add, r=[tag + "r", tag + "k"], w=[tag + "r"])
        self.stt(rr, kk, -C3, rr, ALU.mult, ALU.add, r=[tag + "r", tag + "k"], w=[tag + "r"])
        self.ts("dve", rr, rr, -PIS, PIS, ALU.max, ALU.min, r=[tag + "r"], w=[tag + "r"])
        self.act(s_out, rr, AF.Sin, r=[tag + "r"], w=[tag + "s"])
        self.stt(kk, rr, -1.0, rr, ALU.mult, ALU.max, r=[tag + "r"], w=[tag + "k"])
        self.ts("dve", kk, kk, -1.0, math.pi / 2, ALU.mult, ALU.add, r=[tag + "k"], w=[tag + "k"])
        self.act(c_out, kk, AF.Sin, r=[tag + "k"], w=[tag + "c"])

    def cmul(self, e, ore, oim, are, aim, bre, bim, t1, t2, r, w, tag):
        T1, T2 = tag + "t1", tag + "t2"
        self.tt(e, t1, are, bre, ALU.mult, r=r, w=[T1])
        self.tt(e, t2, aim, bim, ALU.mult, r=r, w=[T2])
        self.tt(e, ore, t1, t2, ALU.subtract, r=[T1, T2], w=[w + "re"])
        self.tt(e, t1, are, bim, ALU.mult, r=r, w=[T1])
        self.tt(e, t2, aim, bre, ALU.mult, r=r, w=[T2])
        self.tt(e, oim, t1, t2, ALU.add, r=[T1, T2], w=[w + "im"])

    def lam_basic(self, es, tag, lam3, X):
        sb = lambda n: self.sb(es, tag + n, [128, X], F32)
        dtv, a, th, cre, cim = sb("dtv"), sb("a"), sb("th"), sb("cre"), sb("cim")
        t1, t2, t3, t4 = sb("t1"), sb("t2"), sb("t3"), sb("t4")
        LAM = tag + "lam"
        self.act(dtv[:], lam3[:, 2, :], AF.Exp, r=[LAM], w=[tag + "dtv"])
        self.tt("dve", a[:], lam3[:, 0, :], dtv[:], ALU.mult, r=[LAM, tag + "dtv"], w=[tag + "a"])
        self.tt("dve", th[:], lam3[:, 1, :], dtv[:], ALU.mult, r=[LAM, tag + "dtv"], w=[tag + "th"])
        mag, s1, c1 = sb("mag"), sb("s1"), sb("c1")
        self.act(mag[:], a[:], AF.Exp, r=[tag + "a"], w=[tag + "mag"])
        with ExitStack() as es2:
            self.ts("dve", t1[:], th[:], 1.0, None, ALU.mult, None, r=[tag + "th"], w=[tag + "lbang"])
            self.sincos(es2, tag + "lb", t1[:], [128, X], s1[:], c1[:])
        self.tt("dve", t1[:], mag[:], c1[:], ALU.mult, r=[tag + "mag", tag + "lbc"], w=[tag + "nr"])
        self.ts("dve", t1[:], t1[:], -1.0, None, ALU.add, None, r=[tag + "nr"], w=[tag + "nr"])
        self.tt("dve", t2[:], mag[:], s1[:], ALU.mult, r=[tag + "mag", tag + "lbs"], w=[tag + "ni"])
        self.tt("dve", t3[:], lam3[:, 0, :], lam3[:, 0, :], ALU.mult, r=[LAM], w=[tag + "d1"])
        self.tt("dve", t4[:], lam3[:, 1, :], lam3[:, 1, :], ALU.mult, r=[LAM], w=[tag + "d2"])
        self.tt("dve", t3[:], t3[:], t4[:], ALU.add, r=[tag + "d1", tag + "d2"], w=[tag + "d1"])
        self.kb.op("dve", lambda eng: eng.reciprocal(out=t3[:], in_=t3[:]), reads=[tag + "d1"], writes=[tag + "d1"])
        self.tt("dve", cre[:], t1[:], lam3[:, 0, :], ALU.mult, r=[tag + "nr", LAM], w=[tag + "cre"])
        self.tt("dve", t4[:], t2[:], lam3[:, 1, :], ALU.mult, r=[tag + "ni", LAM, tag + "d1"], w=[tag + "d2"])
        self.tt("dve", cre[:], cre[:], t4[:], ALU.add, r=[tag + "cre", tag + "d2"], w=[tag + "cre"])
        self.tt("dve", cre[:], cre[:], t3[:], ALU.mult, r=[tag + "cre", tag + "d1"], w=[tag + "cre"])
        self.tt("dve", cim[:], t2[:], lam3[:, 0, :], ALU.mult, r=[tag + "ni", LAM], w=[tag + "cim"])
        self.tt("dve", t4[:], t1[:], lam3[:, 1, :], ALU.mult, r=[tag + "nr", LAM, tag + "cre"], w=[tag + "d2"])
        self.tt("dve", cim[:], cim[:], t4[:], ALU.subtract, r=[tag + "cim", tag + "d2"], w=[tag + "cim"])
        self.tt("dve", cim[:], cim[:], t3[:], ALU.mult, r=[tag + "cim", tag + "d1"], w=[tag + "cim"])
        return a, th, cre, cim

    def powers(self, es, tag, a_ap, th_ap, ev_ap, X, E, pwre, pwim, rin):
        shp = [128, X, E]
        arg = self.sb(es, tag + "arg", shp, F32)
        ang = self.sb(es, tag + "ang", shp, F32)
        sn = self.sb(es, tag + "sn", shp, F32)
        cs = self.sb(es, tag + "cs", shp, F32)
        abc = a_ap.unsqueeze(2).to_broadcast(shp)
        tbc = th_ap.unsqueeze(2).to_broadcast(shp)
        ebc = ev_ap.unsqueeze(1).to_broadcast(shp)
        self.tt("dve", arg[:], abc, ebc, ALU.mult, r=rin, w=[tag + "arg"])
        self.act(arg[:], arg[:], AF.Exp, r=[tag + "arg"], w=[tag + "arg"])
        self.tt("dve", ang[:], tbc, ebc, ALU.mult, r=rin, w=[tag + "ang"])
        self.sincos(es, tag, ang[:], shp, sn[:], cs[:])
        self.tt("dve", pwre, arg[:], cs[:], ALU.mult, r=[tag + "arg", tag + "c"], w=[tag + "pwre"])
        self.tt("dve", pwim, arg[:], sn[:], ALU.mult, r=[tag + "arg", tag + "s"], w=[tag + "pwim"])

    def phase_prep(self):
        kb = self.kb
        with ExitStack() as es:
            sb = lambda n, s, d=F32: self.sb(es, n, s, d)
            ev17, ev16r, nv32 = sb("ev17", [128, 17]), sb("ev16r", [128, 16]), sb("nv32", [128, 32])
            self.load("sp", ev17[:], self.I["ev17"], w=["ev17"])
            self.load("sp", ev16r[:], self.I["ev16r"], w=["ev16r"])
            self.load("sp", nv32[:], self.I["nv32"], w=["nv32"])
            lamc = sb("lamc", [128, 3, 16])
            bcol = sb("bcol", [128, 2, 16, 32])
            ccol = sb("ccol", [128, 2, 16, 32])
            self.load("sp", lamc[:], self.I["lamc"], w=["Clam"])
            self.load("sp", bcol[:], self.I["bcol"], w=["bcol"])
            self.load("sp", ccol[:], self.I["ccol"], w=["ccol"])
            kbd = sb("kbd", [128, 16, 4, 128], BF16)
            self.memset("dve", kbd[:], 0.0, w=["kbd"])
            a, th, cre, cim = self.lam_basic(es, "C", lamc, 16)
            pwre, pwim = sb("cpwre", [128, 16, 17]), sb("cpwim", [128, 16, 17])
            with ExitStack() as es2:
                self.powers(es2, "Cp", a[:], th[:], ev17[:], 16, 17, pwre[:], pwim[:], rin=["Ca", "Cth", "ev17"])
            sct = sb("sct", [128, 16 * 32 * 3 + 32])
            zc = sct[:, 0:512].rearrange("p (q n) -> p q n", n=32)
            zs = sct[:, 512:1024].rearrange("p (q n) -> p q n", n=32)
            rho = sct[:, 1024:1536].rearrange("p (q n) -> p q n", n=32)
            zend = sct[:, 1536:1568].rearrange("p (r q) -> p r q", q=16)
            phi = sb("phi", [128, 16])
            self.ts("dve", phi[:], th[:], 16.0, None, ALU.mult, None, r=["Cth"], w=["phi"])
            with ExitStack() as es2:
                zang = self.sb(es2, "zang", [128, 16, 32], F32)
                self.tt("dve", zang[:], phi[:].unsqueeze(2).to_broadcast([128, 16, 32]),
                        nv32[:].unsqueeze(1).to_broadcast([128, 16, 32]), ALU.mult, r=["phi", "nv32"], w=["Zang"])
                self.sincos(es2, "Z", zang[:], [128, 16, 32], zs, zc)
            rh = sb("rh", [128, 16])
            self.act(rh[:], a[:], AF.Exp, r=["Ca"], w=["rh"], scale=16.0)
            self.cp("dve", rho, rh[:].unsqueeze(2).to_broadcast([128, 16, 32]), r=["rh"], w=["rho"])
            self.cp("dve", zend[:, 0, :], zc[:, :, 31], r=["Zc"], w=["zend0"])
            self.cp("dve", zend[:, 1, :], zs[:, :, 31], r=["Zs"], w=["zend1"])
            self.load("sp", self.sc_s, sct[:], w=["sc_s"], r=["Zc", "Zs", "rho", "zend0", "zend1"])
            t1, t2 = sb("ct1", [128, 4, 17, 32]), sb("ct2", [128, 4, 17, 32])
            s1, s2 = sb("cs1", [128, 16, 32]), sb("cs2", [128, 16, 32])
            bb = sb("bb", [128, 2, 16, 32], BF16)
            bbf = sb("bbf", [128, 2, 16, 32])
            cbc = lambda x: x[:].unsqueeze(2).to_broadcast([128, 16, 32])
            self.cmul("dve", bbf[:, 0], bbf[:, 1], bcol[:, 0], bcol[:, 1], cbc(cre), cbc(cim),
                      s1[:], s2[:], r=["bcol", "Ccre", "Ccim"], w="bbf", tag="bbm")
            self.cp("dve", bb[:, 0], bbf[:, 0], r=["bbfre"], w=["bbre"])
            self.cp("dve", bb[:, 1], bbf[:, 1], r=["bbfim"], w=["bbim"])
            wbk = sb("wbk", [128, 2, 16, 4, 32])
            wb = sb("wb", [128, 16, 4, 2, 128], BF16)
            cntb = 0
            for t in range(4):
                shb = [128, 4, 16, 32]
                br = bbf[:, 0, 4 * t:4 * t + 4, :].unsqueeze(2).to_broadcast(shb)
                bi = bbf[:, 1, 4 * t:4 * t + 4, :].unsqueeze(2).to_broadcast(shb)
                pr = pwre[:, 4 * t:4 * t + 4, 0:16].unsqueeze(3).to_broadcast(shb)
                pi = pwim[:, 4 * t:4 * t + 4, 0:16].unsqueeze(3).to_broadcast(shb)
                u1 = t1[:, :, 0:16, :]
                u2 = t2[:, :, 0:16, :]
                P = ["bbfre", "bbfim", "Cppwre", "Cppwim"]
                o_re = wbk[:, 0].rearrange("p k q c -> p q k c")
                o_im = wbk[:, 1].rearrange("p k q c -> p q k c")
                self.tt("dve", u1, br, pr, ALU.mult, r=P, w=["wt1"])
                self.tt("pool", u2, bi, pi, ALU.mult, r=P, w=["wt2"])
                self.tt("dve", o_re, u1, u2, ALU.subtract, r=["wt1", "wt2"], w=[("wbk", 0)])
                self.tt("dve", u1, br, pi, ALU.mult, r=P, w=["wt1"])
                self.tt("pool", u2, bi, pr, ALU.mult, r=P, w=["wt2"])
                self.tt("dve", o_im, u1, u2, ALU.add, r=["wt1", "wt2"], w=[("wbk", 1)])
                for ri in range(2):
                    for jg in range(4):
                        bnk = 4 + cntb % 4
                        cntb += 1
                        pbk = self.ps[bnk]
                        for jj in range(4):
                            j = 4 * jg + jj
                            self.tr(pbk[:, jj * 128:(jj + 1) * 128], wbk[:, ri, 15 - j].rearrange("p q c -> p (q c)"), self.identf[:],
                                    r=[("wbk", ri), "ident_f"], w=[("ps", bnk)], inc=(jj == 3))
                        eng = "act" if cntb % 2 == 0 else "dve"
                        self.cp(eng, wb[:, 4 * jg:4 * jg + 4, t, ri, :], pbk[:].rearrange("p (j c) -> p j c", c=128),
                                r=[("ps", bnk)], w=[("wb", t, ri, jg)])
            self.load("sp", self.wb_s, wb[:].rearrange("p j t r c -> p (j t r c)"), w=["wb_s"],
                      r=[("wb", t, ri, jg) for t in range(4) for ri in range(2) for jg in range(4)])
            wc = sb("wc", [128, 2, 16, 17, 32], BF16)
            P = ["ccol", "Cppwre", "Cppwim"]
            for qg in range(4):
                qs = slice(4 * qg, 4 * qg + 4)
                shp = [128, 4, 17, 32]
                cr_bc = ccol[:, 0, qs].unsqueeze(2).to_broadcast(shp)
                ci_bc = ccol[:, 1, qs].unsqueeze(2).to_broadcast(shp)
                pr_bc = pwre[:, qs, :].unsqueeze(3).to_broadcast(shp)
                pi_bc = pwim[:, qs, :].unsqueeze(3).to_broadcast(shp)
                self.tt("dve", t1[:], cr_bc, pr_bc, ALU.mult, r=P, w=["wt1"])
                self.tt("pool", t2[:], ci_bc, pi_bc, ALU.mult, r=P, w=["wt2"])
                self.tt("dve", wc[:, 0, qs], t1[:], t2[:], ALU.subtract, r=["wt1", "wt2"], w=["wcre"])
                self.tt("dve", t1[:], cr_bc, pi_bc, ALU.mult, r=P, w=["wt1"])
                self.tt("pool", t2[:], ci_bc, pr_bc, ALU.mult, r=P, w=["wt2"])
                self.stt(wc[:, 1, qs], t1[:], -1.0, t2[:], ALU.mult, ALU.subtract, r=["wt1", "wt2"], w=["wcim"])
            self.load("sp", self.wc_s, wc[:].rearrange("p a q k c -> p (a q k c)"), w=["wc_s"], r=["wcre", "wcim"])
            for t in range(4):
                pk = self.ps[t]
                for qi in range(4):
                    q = 4 * t + qi
                    o = pk[32 * qi:32 * qi + 32, :]
                    self.mm(o, bb[:, 0, q, :], wc[:, 0, q, 0:16, :], True, False, r=["bbre", "wcre"], w=[("ps", t)], tp=(0, 32 * qi))
                    self.mm(o, bb[:, 1, q, :], wc[:, 1, q, 0:16, :], False, True, r=["bbim", "wcim"], w=[("ps", t)], tp=(0, 32 * qi))
                    self.cp("act", kbd[32 * qi:32 * qi + 32, :, t, 32 * qi:32 * qi + 32],
                            o.rearrange("p (k c) -> p k c", c=32), r=[("ps", t), "kbd"], w=["kbd_%d_%d" % (t, qi)])
                deps = ["kbd"] + ["kbd_%d_%d" % (t, qi) for qi in range(4)]
                self.stt(kbd[:, 0, t, :], self.identf[:], self.vecs[:, t:t + 1], kbd[:, 0, t, :], ALU.mult, ALU.add,
                         r=deps + ["ident_f", "s5d"], w=["kbdD%d" % t])
            self.load("sp", self.kbd_s, kbd[:].rearrange("p k t c -> p (k t c)"), w=["kbd_s"],
                      r=["kbd"] + ["kbdD%d" % t for t in range(4)] + ["kbd_%d_%d" % (t, qi) for t in range(4) for qi in range(4)])
            if self.debug:
                self.load("sp", self.dbg("kbd", [128, 16 * 4 * 128], BF16), kbd[:].rearrange("p k t c -> p (k t c)"), w=["dbgk"],
                          r=["kbd"] + ["kbdD%d" % t for t in range(4)])
                self.load("sp", self.dbg("pw", [128, 2, 16 * 17]), pwre[:].rearrange("p q k -> p (q k)"), w=["dbgp"], r=["Cppwre"]) if False else None
            kb.barrier()
            kb.emit()

    def phase_s5(self, s):
        kb = self.kb
        I = self.I
        U = self.U
        if not getattr(self, "_wc_started", False):
            self._wc_started = True
            self._es_wc = ExitStack()
            st = [self.sb(self._es_wc, "wst%d" % i, [128, 2048], BF16) for i in range(4)]
            self._wcg = self.g_wcast(st)
        with ExitStack() as es:
            sb = lambda n, sh, d=F32: self.sb(es, n, sh, d)
            winu = sb("winu", [128, 8, 512], BF16)
            for f in range(4):
                self.load("pool", winu[:, :, f * 128:(f + 1) * 128], I["w_in_f"][f].rearrange("p (k c) -> p k c", c=128), w=[("winu", f)])
            xtm = [sb("xtm%d" % i, [128, 4, 1024]) for i in range(2)]
            xT = [sb("xT%d" % i, [128, 8, 512], BF16) for i in range(2)]
            for ti in range(NTILE):
                b = ti % 2
                self.load("sp", xtm[b][:], I["x"][s, ti * 512:(ti + 1) * 512, :].rearrange("(a p) d -> p a d", p=128), w=[("xtm", b)])
                self.transpose_x(xtm[b], xT[b], ("xtm", b), ("xT", b), ti)
                for t in range(4):
                    pb = self.ps[4 + (ti * 4 + t) % 4]
                    key = ("ps", 4 + (ti * 4 + t) % 4)
                    for k in range(8):
                        self.mm(pb[:], winu[:, k, t * 128:(t + 1) * 128], xT[b][:, k, :], k == 0, k == 7,
                                r=[("winu", t), ("xT", b)], w=[key])
                    self.act(U[:, t, ti * 512:(ti + 1) * 512], pb[:], AF.Copy, r=[key], w=[("U", t, ti)])
                self.wc_pull(2)
            if self.debug and s == 0:
                self.load("sp", self.dbg("uT", [128, 4 * L], BF16), U[:].rearrange("p t l -> p (t l)"), w=["dbgu"],
                          r=[("U", t, ti) for t in range(4) for ti in range(NTILE)])
            kb.barrier()
            kb.emit()
        with ExitStack() as es:
            sb = lambda n, sh, d=F32: self.sb(es, n, sh, d)
            sprev = sb("sprev", [128, 16, 2, 257], BF16)
            esBC = ExitStack()
            sloc = self.sb(esBC, "sloc", [128, 16, 2, 256], F32)
            with ExitStack() as esB:
                wb = self.sb(esB, "wb", [128, 16, 4, 2, 128], BF16)
                self.load("sp", wb[:], self.wb_s.rearrange("p (j t r c) -> p j t r c", j=16, t=4, r=2), w=["wb"])
                it = 0
                for t in range(4):
                    self.wc_pull(3)
                    uv = U[:, t, :].rearrange("p (n j) -> p j n", j=16)
                    for ri in range(2):
                        base = 4 * (it % 2)
                        it += 1
                        for j in range(16):
                            for qi in range(4):
                                rs = slice(32 * qi, 32 * qi + 32)
                                self.mm(self.ps[base + qi][:, 0:256], wb[rs, j, t, ri, :], uv[rs, j, :], j == 0, j == 15,
                                        r=["wb", "Uall"], w=[("ps", base + qi)], tp=(32 * qi, 0))
                        for qi in range(4):
                            eng = "act" if qi % 2 == 0 else "dve"
                            self.cp(eng, sloc[:, 4 * t + qi, ri, :], self.ps[base + qi][:, 0:256],
                                    r=[("ps", base + qi)], w=[("sloc", 4 * t + qi, ri)])
                kb.barrier()
                kb.emit()
            with ExitStack() as esC:
                sbc = lambda n, sh, d=F32: self.sb(esC, n, sh, d)
                sct = sbc("sct", [128, 16 * 32 * 3 + 32])
                self.load("sp", sct[:], self.sc_s, w=["sct"])
                zc = sct[:, 0:512].rearrange("p (q n) -> p q n", n=32)
                zs = sct[:, 512:1024].rearrange("p (q n) -> p q n", n=32)
                rho = sct[:, 1024:1536].rearrange("p (q n) -> p q n", n=32)
                zend = sct[:, 1536:1568].rearrange("p (r q) -> p r q", q=16)
                mod = sbc("mod", [128, 16, 2, 256])
                t1 = sbc("t1", [128, 16, 256])
                t2 = sbc("t2", [128, 16, 256])
                shp = [128, 16, 8, 32]
                v4 = lambda ap: ap.rearrange("p q (g n) -> p q g n", n=32)
                zcb = zc.unsqueeze(2).to_broadcast(shp)
                zsb = zs.unsqueeze(2).to_broadcast(shp)
                SL = [("sloc", q, r_) for q in range(16) for r_ in range(2)]
                lre, lim = v4(sloc[:, :, 0, :]), v4(sloc[:, :, 1, :])
                self.tt("dve", v4(t1[:]), lre, zcb, ALU.mult, r=SL + ["sct"], w=["ct1"])
                self.tt("pool", v4(t2[:]), lim, zsb, ALU.mult, r=SL + ["sct"], w=["ct2"])
                self.tt("dve", mod[:, :, 0, :], t1[:], t2[:], ALU.add, r=["ct1", "ct2"], w=["modre"])
                self.tt("dve", v4(t1[:]), lim, zcb, ALU.mult, r=SL + ["sct", "modre"], w=["ct1"])
                self.tt("pool", v4(t2[:]), lre, zsb, ALU.mult, r=SL + ["sct", "modre"], w=["ct2"])
                self.tt("dve", mod[:, :, 1, :], t1[:], t2[:], ALU.subtract, r=["ct1", "ct2"], w=["modim"])
                carry = sbc("carry", [128, 2, 16])
                ctmp = sbc("ctmp", [128, 4, 16])
                self.memset("dve", carry[:], 0.0, w=["carry"])
                for g in range(8):
                    for q in range(16):
                        for ri in range(2):
                            o = sloc[:, q, ri, g * 32:(g + 1) * 32]
                            d1 = mod[:, q, ri, g * 32:(g + 1) * 32]
                            ini = carry[:, ri, q:q + 1]
                            d0 = rho[:, q, :]
                            self.kb.op("dve", lambda eng, o=o, d0=d0, d1=d1, ini=ini: eng.tensor_tensor_scan(
                                out=o, data0=d0, data1=d1, initial=ini, op0=ALU.mult, op1=ALU.add),
                                reads=["modre", "modim", "carry", "sct"], writes=[("R", g)])
                    if g < 7:
                        rre = sloc[:, :, 0, g * 32 + 31]
                        rim = sloc[:, :, 1, g * 32 + 31]
                        self.tt("dve", ctmp[:, 0, :], rre, zend[:, 0, :], ALU.mult, r=[("R", g), "sct"], w=["cta"])
                        self.tt("dve", ctmp[:, 1, :], rim, zend[:, 1, :], ALU.mult, r=[("R", g), "sct"], w=["ctb"])
                        self.tt("dve", ctmp[:, 2, :], rre, zend[:, 1, :], ALU.mult, r=[("R", g), "sct"], w=["ctc"])
                        self.tt("dve", ctmp[:, 3, :], rim, zend[:, 0, :], ALU.mult, r=[("R", g), "sct"], w=["ctd"])
                        self.tt("dve", carry[:, 0, :], ctmp[:, 0, :], ctmp[:, 1, :], ALU.subtract, r=["cta", "ctb"], w=["carry"])
                        self.tt("dve", carry[:, 1, :], ctmp[:, 2, :], ctmp[:, 3, :], ALU.add, r=["ctc", "ctd", "carry"], w=["carry"])
                RR = [("R", g) for g in range(8)]
                self.memset("pool", sprev[:, :, :, 0:1], 0.0, w=["sprev0"])
                rre, rim = v4(sloc[:, :, 0, :]), v4(sloc[:, :, 1, :])
                self.tt("dve", v4(t1[:]), rre, zcb, ALU.mult, r=RR + ["sct"], w=["ct1"])
                self.tt("pool", v4(t2[:]), rim, zsb, ALU.mult, r=RR + ["sct"], w=["ct2"])
                self.tt("dve", sprev[:, :, 0, 1:257], t1[:], t2[:], ALU.subtract, r=["ct1", "ct2"], w=["sprevre"])
                self.tt("dve", v4(t1[:]), rre, zsb, ALU.mult, r=RR + ["sct", "sprevre"], w=["ct1"])
                self.tt("pool", v4(t2[:]), rim, zcb, ALU.mult, r=RR + ["sct", "sprevre"], w=["ct2"])
                self.tt("dve", sprev[:, :, 1, 1:257], t1[:], t2[:], ALU.add, r=["ct1", "ct2"], w=["sprevim"])
                if self.debug and s == 0:
                    self.load("sp", self.dbg("sprev", [128, 16 * 2 * 257], BF16), sprev[:].rearrange("p q r n -> p (q r n)"),
                              w=["dbgs"], r=["sprev0", "sprevre", "sprevim"])
                kb.barrier()
                kb.emit()
            esBC.close()
            with ExitStack() as esD:
                sbd = lambda n, sh, d=F32: self.sb(esD, n, sh, d)
                kbd = sbd("kbd", [128, 16, 4, 128], BF16)
                wc = sbd("wc", [128, 2, 16, 17, 32], BF16)
                yg = sbd("yg", [128, 4, L], BF16)
                ycr = [sbd("ycr%d" % i, [128, 16, 256], BF16) for i in range(2)]
                ysum = [sbd("ysum%d" % i, [128, 512]) for i in range(2)]
                self.load("sp", kbd[:], self.kbd_s.rearrange("p (k t c) -> p k t c", k=16, t=4), w=["kbd"])
                self.load("sp", wc[:], self.wc_s.rearrange("p (a q k c) -> p a q k c", a=2, q=16, k=17), w=["wc"])
                wglu = sbd("wglu", [128, 4, 512], BF16)
                self.load("pool", wglu[:], I["glu_w_r"], w=["wglu"])
                sig = [sbd("sig%d" % i, [128, 512]) for i in range(2)]
                sv = [sbd("sv%d" % i, [128, 4, 512]) for i in range(2)]
                sq = [sbd("sq%d" % i, [128, 4, 512], BF16) for i in range(2)]
                rstd = [sbd("rstd%d" % i, [128, 512]) for i in range(2)]

                def stageE(ti):
                    b = ti % 2
                    tsl = slice(ti * 512, (ti + 1) * 512)
                    for to in range(4):
                        pi_ = (ti * 4 + to) % 2
                        pb = self.ps[pi_]
                        for tin in range(4):
                            self.mm(pb[:], wglu[:, tin, to * 128:(to + 1) * 128], yg[:, tin, tsl], tin == 0, tin == 3,
                                    r=["wglu"] + [("yg", tin, ti)], w=[("ps", pi_)])
                        sg = sig[(ti * 4 + to) % 2]
                        sk = ("sig", (ti * 4 + to) % 2)
                        self.act(sg[:], pb[:], AF.Sigmoid, r=[("ps", pi_), "glub"], w=[sk], bias=self.vecs[:, 4 + to:5 + to])
                        self.tt("dve", sv[b][:, to, :], sg[:], yg[:, to, tsl], ALU.mult, r=[sk, ("yg", to, ti)], w=[("sv", b, to)])
                        self.tt("pool", sq[b][:, to, :], sv[b][:, to, :], sv[b][:, to, :], ALU.mult, r=[("sv", b, to)], w=[("sq", b, to)])
                def stageE2(ti):
                    b = ti % 2
                    tsl = slice(ti * 512, (ti + 1) * 512)
                    pmi = 6 + ti % 2
                    pm = self.ps[pmi]
                    for to in range(4):
                        self.mm(pm[:], self.ones512[:], sq[b][:, to, :], to == 0, to == 3, r=["ones512", ("sq", b, to)], w=[("ps", pmi)])
                    self.act(rstd[b][:], pm[:], AF.Ln, r=[("ps", pmi)], w=[("rstd", b)], bias=self.eps_ap())
                    self.act(rstd[b][:], rstd[b][:], AF.Exp, r=[("rstd", b)], w=[("rstd", b)], scale=-0.5)
                    for to in range(4):
                        self.stt(U[:, to, tsl], sv[b][:, to, :], self.vecs[:, 8 + to:9 + to], rstd[b][:], ALU.mult, ALU.mult,
                                 r=[("sv", b, to), ("rstd", b), "s5gain"], w=[("ys5", to, ti), ("U", to, ti)])

                cnt = 0
                for t in range(4):
                    self.wc_pull(9 if t < 3 else 100)
                    yc = ycr[t % 2]
                    for jp in range(8):
                        pb = self.ps[jp % 2]
                        for jj in range(2):
                            j = 2 * jp + jj
                            for qi in range(4):
                                q = 4 * t + qi
                                o = pb[32 * qi:32 * qi + 32, jj * 256:(jj + 1) * 256]
                                self.mm(o, wc[:, 0, q, j + 1, :], sprev[:, q, 0, 0:256], True, False,
                                        r=["wc", "sprev"], w=[("ps", jp % 2)], tp=(0, 32 * qi))
                                self.mm(o, wc[:, 1, q, j + 1, :], sprev[:, q, 1, 0:256], False, True,
                                        r=["wc", "sprev"], w=[("ps", jp % 2)], tp=(0, 32 * qi))
                        self.act(yc[:, 2 * jp:2 * jp + 2, :], pb[:].rearrange("p (j n) -> p j n", n=256), AF.Copy,
                                 r=[("ps", jp % 2)], w=[("ycr", t % 2)])
                    for ti in range(NTILE):
                        pb = self.ps[2 + cnt % 4]
                        pk = ("ps", 2 + cnt % 4)
                        ys = ysum[cnt % 2]
                        yk = ("ysum", cnt % 2)
                        cnt += 1
                        pv = pb[:].rearrange("p (c j) -> p c j", j=16)
                        uv = U[:, t, ti * 512:(ti + 1) * 512].rearrange("p (c j) -> p c j", j=16)
                        for tau in range(16):
                            self.mm(pv[:, :, tau:16], kbd[:, tau, t, :], uv[:, :, 0:16 - tau], tau == 0, tau == 15,
                                    r=["kbd", ("U", t, ti)], w=[pk])
                        ycv = yc[:, :, ti * 32:(ti + 1) * 32].rearrange("p j n -> p n j")
                        self.tt("dve", ys[:].rearrange("p (c j) -> p c j", j=16), pv, ycv, ALU.add,
                                r=[pk, ("ycr", t % 2)], w=[yk])
                        self.act(yg[:, t, ti * 512:(ti + 1) * 512], ys[:], AF.Gelu_apprx_tanh, r=[yk], w=[("yg", t, ti)])
                        if self.debug and s == 0:
                            self.load("sp", self.dbg("ylin", [128, 4, L])[:, t, ti * 512:(ti + 1) * 512], ys[:], r=[yk], w=[("dbgy", t, ti)])
                        if t == 3:
                            stageE(ti)
                            if ti > 0:
                                stageE2(ti - 1)
                            if ti == NTILE - 1:
                                stageE2(ti)
                if self.debug and s == 0:
                    self.load("sp", self.dbg("ys5", [128, 4 * L], BF16), U[:].rearrange("p t l -> p (t l)"), w=["dbgys5"],
                              r=[("ys5", to, ti) for to in range(4) for ti in range(NTILE)])
                kb.barrier()
                kb.emit()

        if getattr(self, "_es_wc", None) is not None:
            self.wc_pull(1000)
            self._es_wc.close()
            self._es_wc = None

    def eps_ap(self):
        return self._eps[:]

    def transpose_x(self, src, dst, skey, dkey, par):
        for k in range(8):
            pb = self.ps[(par * 8 + k) % 4]
            pk = ("ps", (par * 8 + k) % 4)
            for a in range(4):
                self.tr(pb[:, a * 128:(a + 1) * 128], src[:, a, k * 128:(k + 1) * 128], self.identf[:],
                        r=(list(skey) if isinstance(skey, list) else [skey]) + ["ident_f"], w=[pk], inc=(a == 3))
            if k % 2 == 0:
                self.act(dst[:, k, :], pb[:], AF.Copy, r=[pk], w=[dkey])
            else:
                self.cp("dve", dst[:, k, :], pb[:], r=[pk], w=[dkey])

    def layer_norm(self, xt, a, key, lnp, gi, stats, mv, sc):
        x = xt[:, a, :]
        for c in range(2):
            self.kb.op("dve", lambda eng, c=c: eng.bn_stats(out=stats[:, c, :], in_=xt[:, a, c * 512:(c + 1) * 512]),
                       reads=[key], writes=[("lnst", c)])
        self.kb.op("dve", lambda eng: eng.bn_aggr(out=mv[:], in_=stats[:].rearrange("p c s -> p (c s)")),
                   reads=[("lnst", 0), ("lnst", 1)], writes=["lnmv"])
        self.act(sc[:, 0:1], mv[:, 1:2], AF.Sqrt, r=["lnmv", "epsc"], w=["lnsc0"], bias=self.eps_ap())
        self.kb.op("dve", lambda eng: eng.reciprocal(out=sc[:, 0:1], in_=sc[:, 0:1]), reads=["lnsc0"], writes=["lnsc0"])
        self.stt(sc[:, 1:2], mv[:, 0:1], -1.0, sc[:, 0:1], ALU.mult, ALU.mult, r=["lnmv", "lnsc0"], w=["lnsc1"])
        self.act(x, x, AF.Identity, r=[key, "lnsc0", "lnsc1"], w=[key], bias=sc[:, 1:2], scale=sc[:, 0:1])
        self.tt("pool", x, x, lnp[:, gi, :], ALU.mult, r=[key, "lnp"], w=[key])
        self.tt("pool", x, x, lnp[:, gi + 1, :], ALU.add, r=[key, "lnp"], w=[key])

    def bankM(self):
        self._bm = (getattr(self, "_bm", 7) + 1 - 4) % 4 + 4
        return self._bm

    def ln2(self, x, key, g_ap, b_ap, stats, mv, sc, tag):
        for c in range(2):
            self.kb.op("dve", lambda eng, c=c: eng.bn_stats(out=stats[:, c, :], in_=x[:, c * 512:(c + 1) * 512]),
                       reads=[key], writes=[(tag + "st", c)])
        self.kb.op("dve", lambda eng: eng.bn_aggr(out=mv[:], in_=stats[:].rearrange("p c s -> p (c s)")),
                   reads=[(tag + "st", 0), (tag + "st", 1)], writes=[tag + "mv"])
        self.act(sc[:, 0:1], mv[:, 1:2], AF.Sqrt, r=[tag + "mv", "epsc"], w=[tag + "sc0"], bias=self.eps_ap())
        self.kb.op("dve", lambda eng: eng.reciprocal(out=sc[:, 0:1], in_=sc[:, 0:1]), reads=[tag + "sc0"], writes=[tag + "sc0"])
        self.stt(sc[:, 1:2], mv[:, 0:1], -1.0, sc[:, 0:1], ALU.mult, ALU.mult, r=[tag + "mv", tag + "sc0"], w=[tag + "sc1"])
        self.act(x, x, AF.Identity, r=[key, tag + "sc0", tag + "sc1"], w=[key], bias=sc[:, 1:2], scale=sc[:, 0:1])
        self.tt("pool", x, x, g_ap, ALU.mult, r=[key, "lng", "lng1"], w=[key])
        self.tt("pool", x, x, b_ap, ALU.add, r=[key, "lnb"], w=[key])

    def transpose_a(self, src, dst, a, skeys, dkey):
        for kg in range(2):
            b = self.bankM()
            pb, pk = self.ps[b], ("ps", b)
            for kk in range(4):
                k = 4 * kg + kk
                self.tr(pb[:, kk * 128:(kk + 1) * 128], src[:, a, k * 128:(k + 1) * 128], self.identf[:],
                        r=list(skeys) + ["ident_f"], w=[pk], inc=(kk == 3))
            dv = dst[:, 4 * kg:4 * kg + 4, a * 128:(a + 1) * 128]
            self.act(dv, pb[:].rearrange("p (k t) -> p k t", t=128), AF.Copy, r=[pk], w=[(dkey, a, kg)])

    def phase_tiles(self, s, ntile):
        kb, I, U, ps = self.kb, self.I, self.U, self.ps
        with ExitStack() as es:
            sb = lambda n, sh, d=F32: self.sb(es, n, sh, d)
            B = type("B", (), {})()
            B.wv = sb("wv", [128, 8, 512], BF16)
            B.winb = [sb("winb%d" % i, [128, 1024], BF16) for i in range(3)]
            B.woutb = [sb("woutb%d" % i, [128, 1024], BF16) for i in range(4)]
            B.lng = sb("lng", [128, 2, 1024])
            B.lnb = sb("lnb", [128, 2, 1024], BF16)
            B.maskT, B.xi, B.zeta, B.dec = sb("maskT", [128, 4, 128]), sb("xi", [128, 2, 64]), sb("zeta", [128, 4]), sb("dec", [128, 2])
            for f in range(4):
                self.load("pool", B.wv[:, :, f * 128:(f + 1) * 128], I["w_in_f"][8 + f].rearrange("p (k c) -> p k c", c=128), w=[("wv", f)])
            self.load("sp", B.lng[:, 0, :], I["lnp"][:, 0, :], w=["lng"])
            self.load("sp", B.lng[:, 1, :], I["lnp"][:, 2, :], w=["lng1"])
            self.load("sp", B.lnb[:].rearrange("p a d -> p (a d)"), self.lnb_s, w=["lnb"])
            for t_, nm in ((B.maskT, "maskT"), (B.xi, "xi"), (B.zeta, "zeta"), (B.dec, "dec")):
                self.load("sp", t_[:], I[nm], w=[nm])
            B.state = sb("state", [128, 2, 128])
            B.sbf = sb("sbf", [128, 8, 2, 128], BF16)
            B.qxz = sb("qxz", [128, 4, 512], BF16)
            B.halo = sb("halo", [128, 22, 2])
            self.memset("pool", B.state[:], 0.0, w=["state"])
            self.memset("pool", B.qxz[:], 0.0, w=[("qxz", h) for h in range(4)])
            self.memset("pool", B.halo[:], 0.0, w=[("halo", i) for i in range(22)])
            B.xtm = [sb("xtm%d" % i, [128, 4, 1024]) for i in range(2)]
            B.xTa = sb("xTa", [128, 8, 512], BF16)
            B.x1T = sb("x1T", [128, 8, 512], BF16)
            B.rope = sb("rope", [128, 2, 512])
            B.qT, B.kT = sb("qT", [128, 2, 512], BF16), sb("kT", [128, 2, 512], BF16)
            B.vtm = sb("vtm", [128, 4, 512], BF16)
            B.kz = sb("kz", [128, 4, 256], BF16)
            B.sgh = [sb("sgh%d" % i, [128, 512], BF16) for i in range(2)]
            B.sT = sb("sT", [128, 4, 4, 128], BF16)
            B.ftm = [sb("ftm%d" % i, [128, 512]) for i in range(2)]
            B.ftf = [sb("ftf%d" % i, [128, 512]) for i in range(3)]
            B.gsb = [sb("gsb%d" % i, [128, 512], BF16) for i in range(2)]
            B.obf, B.osq = sb("obf", [128, 512], BF16), sb("osq", [128, 512], BF16)
            B.yret = sb("yret", [128, 4, 512], BF16)
            B.actT = sb("actT", [128, 22, 512], BF16)
            B.wupb = [sb("wupb%d" % i, [128, 2048], BF16) for i in range(3)]
            B.wdnb = [sb("wdnb%d" % i, [128, 1024], BF16) for i in range(3)]
            B.st1, B.mv1, B.sc1 = sb("lnst1", [128, 2, 6]), sb("lnmv1", [128, 2]), sb("lnsc1", [128, 2])
            B.st2, B.mv2, B.sc2 = sb("lnst2", [128, 2, 6]), sb("lnmv2", [128, 2]), sb("lnsc2", [128, 2])
            if not hasattr(self, "_printed"):
                print("SBUF remaining in tile phase:", self.nc.sbuf_bytes_remaining)
                self._printed = True
            WINF = (4, 5, 6, 7, 12, 13, 14, 15)
            B.s_win = Stream(self, "pool", B.winb, "winb", lambda c: self.win_s[WINF[c]], 8)
            B.s_wout = Stream(self, "pool", B.woutb, "woutb", lambda c: self.wout_s[c], 8)
            B.s_wup = Stream(self, "sp", B.wupb, "wupb", lambda c: self.wup_s[c], 22)
            B.s_wdn = Stream(self, "sp", B.wdnb, "wdnb", lambda c: self.wdn_s[c], 22)
            self._B = B

            def drain(g):
                for _ in g:
                    pass

            def interleave(ga, gb, na=62.0, nb=62.0):
                da = db = False
                ca = cb = 0
                while not (da and db):
                    if not da and (db or ca / na <= cb / nb):
                        try:
                            next(ga)
                            ca += 1
                        except StopIteration:
                            da = True
                    elif not db:
                        try:
                            next(gb)
                            cb += 1
                        except StopIteration:
                            db = True

            drain(self.g_mixer(s, 0))
            for ti in range(ntile):
                if ti + 1 < ntile:
                    interleave(self.g_ffn(s, ti), self.g_mixer(s, ti + 1))
                else:
                    drain(self.g_ffn(s, ti))
            kb.barrier()
            kb.emit()

    def win_chunk(self, f):
        wb_, wk = self._B.s_win.get()
        return wb_[:].rearrange("p (k c) -> p k c", c=128), wk

    def g_mixer(self, s, ti):
        kb, I, U, ps, B = self.kb, self.I, self.U, self.ps, self._B
        par = ti % 2
        xtm, XK = B.xtm[par], ("xtm", par)
        xT, rope, qT, kT, vtm, kz, sT, ft = B.xTa, B.rope, B.qT, B.kT, B.vtm, B.kz, B.sT, B.ftm
        state, sbf, qxz, yret, obf, osq = B.state, B.sbf, B.qxz, B.yret, B.obf, B.osq
        maskT, xi, zeta, dec = B.maskT, B.xi, B.zeta, B.dec
        gn = self.vecs[:, 12:16]
        tsl = slice(ti * 512, (ti + 1) * 512)
        self.load("pool", xtm[:], I["x"][s, tsl, :].rearrange("(a p) d -> p a d", p=128), w=[XK])
        self.load("pool", rope[:], I["rope"][:, :, tsl], w=["rope"])
        for _ in range(5):
            yield
        for a in range(4):
            self.transpose_a(xtm, xT, a, [XK], "xTa")
            yield
        for f in range(4):
            wch, wk = self.win_chunk(4 + f)
            b = self.bankM()
            pb, pk = ps[b], ("ps", b)
            for k in range(8):
                self.mm(pb[:], wch[:, k, :], xT[:, k, :], k == 0, k == 7, r=[wk] + [("xTa", a_, g_) for a_ in range(4) for g_ in range(2)], w=[pk])
            self.tt("dve", ft[0][:], pb[:], rope[:, 0, :], ALU.mult, r=[pk, "rope"], w=[("ftm", 0)])
            for qd in range(4):
                src = (qd ^ 1) * 32
                self.tt("dve", ft[1][qd * 32:qd * 32 + 32, :], pb[src:src + 32, :], rope[qd * 32:qd * 32 + 32, 1, :], ALU.mult,
                        r=[pk, "rope"], w=[("ftm", 1)])
            dst = qT[:, f, :] if f < 2 else kT[:, f - 2, :]
            dk_ = ("qT", f) if f < 2 else ("kT", f - 2)
            self.tt("pool", dst, ft[0][:], ft[1][:], ALU.add, r=[("ftm", 0), ("ftm", 1)], w=[dk_])
            if f < 2:
                for hp in range(2):
                    h = 2 * f + hp
                    sl = slice(64 * hp, 64 * hp + 64)
                    self.tt("pool", qxz[sl, h, :].rearrange("p (n c) -> p n c", c=64), qT[sl, f, :].rearrange("p (n c) -> p n c", c=64),
                            xi[sl, f, :].unsqueeze(1).to_broadcast([64, 8, 64]), ALU.mult, r=[dk_, "xi"], w=[("qxz", h)])
            yield
        for a in range(4):
            b = self.bankM()
            pb, pk = ps[b], ("ps", b)
            for k in range(8):
                self.mm(pb[:], xT[:, k, a * 128:(a + 1) * 128], B.wv[:, k, :], k == 0, k == 7, r=[("wv", f_) for f_ in range(4)] + [("xTa", a, 0), ("xTa", a, 1)], w=[pk])
            self.cp("act", vtm[:, a, :], pb[:], r=[pk], w=[("vtm", a)])
            yield
        for a in range(4):
            asl = slice(a * 128, (a + 1) * 128)
            b = self.bankM()
            pbb, pk = ps[b][:].bitcast(BF16), ("ps", b)
            for kt in range(2):
                self.tr(pbb[:, kt * 128:(kt + 1) * 128], kT[:, kt, asl], self.identb[:], r=[("kT", kt), "ident_b"], w=[pk], inc=(kt == 1))
            self.tt("dve", kz[:, a, :].rearrange("p (h d) -> p h d", d=64), pbb[:, 0:256].rearrange("p (h d) -> p h d", d=64),
                    zeta[:].unsqueeze(2).to_broadcast([128, 4, 64]), ALU.mult, r=[pk, "zeta"], w=[("kz", a)])
            for hp in range(2):
                hb = 64 * hp
                b = self.bankM()
                pb, pk = ps[b], ("ps", b)
                for hh in range(2):
                    self.kb.op("pe", lambda eng, pb=pb, hb=hb, hh=hh, asl=asl: eng.matmul(
                        pb[:, hh * 128:(hh + 1) * 128], lhsT=kT[hb:hb + 64, hh, asl], rhs=qT[hb:hb + 64, hh, asl],
                        start=True, stop=True, tile_position=(hb, 0)),
                        reads=[("kT", hh), ("qT", hh)], writes=[pk], inc=(hh == 1))
                sv_ = sT[:, a, :, :].rearrange("p (hh par) c -> p par hh c", par=2)[:, hp]
                mv_ = maskT[:].rearrange("p (hh par) c -> p par hh c", par=2)[:, hp]
                self.tt("dve", sv_, pb[:, 0:256].rearrange("p (h c) -> p h c", c=128), mv_, ALU.mult, r=[pk, "maskT"], w=[("sT", a, hp)])
            yield
        if ti == 0:
            self.memset("pool", state[:], 0.0, w=["state"])
        for n in range(8):
            a, tb = n // 2, 64 * (n % 2)
            self.cp("act", sbf[:, n, :, :], state[:], r=["state"], w=[("sbf", n)])
            b = self.bankM()
            pb, pk = ps[b], ("ps", b)
            for h in range(4):
                hb, hh = 64 * (h % 2), h // 2
                self.kb.op("pe", lambda eng, pb=pb, h=h, hb=hb, hh=hh, a=a, tb=tb: eng.matmul(
                    pb[hb:hb + 64, hh * 128:(hh + 1) * 128], lhsT=kz[tb:tb + 64, a, h * 64:(h + 1) * 64],
                    rhs=vtm[tb:tb + 64, a, h * 128:(h + 1) * 128], start=True, stop=True, tile_position=(tb, hb)),
                    reads=[("kz", a), ("vtm", a)], writes=[pk], inc=(h == 3))
            for hh in range(2):
                self.stt(state[:, hh, :], state[:, hh, :], dec[:, hh:hh + 1], pb[:, hh * 128:(hh + 1) * 128], ALU.mult, ALU.add,
                         r=["state", pk, "dec", ("sbf", n)], w=["state"])
            if n % 2 == 1:
                yield
        for h in range(4):
            hh = h // 2
            wch, wk = self.win_chunk(12 + h)
            b = self.bankM()
            pb, pk = ps[b], ("ps", b)
            for k in range(8):
                self.mm(pb[:], wch[:, k, :], xT[:, k, :], k == 0, k == 7, r=[wk] + [("xTa", a_, g_) for a_ in range(4) for g_ in range(2)], w=[pk])
            sg, sgk = B.sgh[h % 2], ("sgh", h % 2)
            self.act(sg[:], pb[:], AF.Silu, r=[pk], w=[sgk])
            yield
            b = self.bankM()
            po, pk = ps[b], ("ps", b)
            for a in range(4):
                self.kb.op("pe", lambda eng, po=po, a=a, h=h: eng.matmul(
                    po[:, a * 128:(a + 1) * 128], lhsT=vtm[:, a, h * 128:(h + 1) * 128], rhs=sT[:, a, h, :], start=True, stop=False),
                    reads=[("vtm", a), ("sT", a, h % 2)], writes=[pk], inc=False)
                for half in range(2):
                    n = 2 * a + half
                    self.kb.op("pe", lambda eng, po=po, a=a, half=half, n=n, h=h, hh=hh: eng.matmul(
                        po[:, a * 128 + 64 * half:a * 128 + 64 * half + 64], lhsT=sbf[:, n, hh, :],
                        rhs=qxz[:, h, n * 64:(n + 1) * 64], start=False, stop=(half == 1)),
                        reads=[("sbf", n), ("qxz", h)], writes=[pk], inc=(a == 3 and half == 1))
            self.act(obf[:], po[:], AF.Copy, r=[pk], w=["obf"])
            self.act(osq[:], po[:], AF.Square, r=[pk], w=["osq"])
            self.act(ft[0][:], po[:], AF.Copy, r=[pk], w=[("ftm", 0)])
            yield
            b1 = self.bankM()
            pm, pmk = ps[b1], ("ps", b1)
            self.mm(pm[:], self.ones128[:], obf[:], True, True, r=["ones128", "obf"], w=[pmk])
            b2 = self.bankM()
            pq, pqk = ps[b2], ("ps", b2)
            self.mm(pq[:], self.ones128[:], osq[:], True, True, r=["ones128", "osq"], w=[pqk])
            self.act(ft[1][:], pm[:], AF.Square, r=[pmk], w=[("ftm", 1)])
            self.tt("dve", ft[1][:], pq[:], ft[1][:], ALU.subtract, r=[pqk, ("ftm", 1)], w=[("ftm", 1)])
            self.act(ft[1][:], ft[1][:], AF.Ln, r=[("ftm", 1), "epsc"], w=[("ftm", 1)], bias=self.eps_ap())
            self.act(ft[1][:], ft[1][:], AF.Exp, r=[("ftm", 1)], w=[("ftm", 1)], scale=-0.5)
            self.tt("dve", ft[0][:], ft[0][:], pm[:], ALU.subtract, r=[("ftm", 0), pmk, "obf"], w=[("ftm", 0)])
            self.tt("pool", ft[0][:], ft[0][:], ft[1][:], ALU.mult, r=[("ftm", 0), ("ftm", 1)], w=[("ftm", 0)])
            self.stt(yret[:, h, :], ft[0][:], gn[:, h:h + 1], sg[:], ALU.mult, ALU.mult,
                     r=[("ftm", 0), sgk, "gngain"], w=[("yret", h)])
            yield
        if self.debug and s == 0:
            self.load("sp", self.dbg("yret", [128, 4, L], BF16)[:, :, tsl], yret[:], r=[("yret", h) for h in range(4)], w=[("dbgyr", ti)])
        for qn in range(8):
            wo_, wok = B.s_wout.get()
            wo = wo_[:].rearrange("p (k c) -> p k c", c=128)
            for a in range(4):
                b = self.bankM()
                pb, pk = ps[b], ("ps", b)
                for k in range(8):
                    lhsT = U[:, k, ti * 512 + a * 128:ti * 512 + (a + 1) * 128] if k < 4 else yret[:, k - 4, a * 128:(a + 1) * 128]
                    rk = "Uall" if k < 4 else ("yret", k - 4)
                    self.mm(pb[:, 0:128], lhsT, wo[:, k, :], k == 0, k == 7, r=[wok, rk], w=[pk])
                xs = xtm[:, a, qn * 128:(qn + 1) * 128]
                self.stt(xs, xs, ALPHA, pb[:, 0:128], ALU.mult, ALU.add, r=[XK, pk], w=[("x1", par, a, qn)])
            if qn % 2 == 1:
                yield
                yield
        def ln1(a):
            kk = ("x1", par, a)
            kb.reg[kk] = kb.reg[("x1", par, a, 7)]
            self.ln2(xtm[:, a, :], kk, B.lng[:, 0, :], B.lnb[:, 0, :], B.st1, B.mv1, B.sc1, "l1")
        ln1(0)
        yield
        ln1(1)
        for _ in range(4):
            yield
        for a in range(4):
            if a + 2 < 4:
                ln1(a + 2)
            for _ in range(3):
                yield
            self.transpose_a(xtm, B.x1T, a, [("x1", par, a)], "x1T")
            yield
        if self.debug and s == 0:
            self.load("sp", self.dbg("x1", [L, 1024])[tsl, :].rearrange("(a p) d -> p a d", p=128), xtm[:],
                      r=[("x1", par, a) for a in range(4)], w=[("dbgx1", ti)])

    def g_ffn(self, s, ti):
        kb, I, ps, B = self.kb, self.I, self.ps, self._B
        par = ti % 2
        xtm = B.xtm[par]
        xT, actT, halo, ft = B.x1T, B.actT, B.halo, B.ftf
        tsl = slice(ti * 512, (ti + 1) * 512)
        cw = self.convw
        for i in range(22):
            wb_, wk = B.s_wup.get()
            ba, bg = (0, 1) if i % 2 == 0 else (2, 3)
            pa, pg = ps[ba], ps[bg]
            pak, pgk = ("ps", ba), ("ps", bg)
            for k in range(8):
                self.mm(pa[:], wb_[:, k * 256:k * 256 + 128], xT[:, k, :], k == 0, k == 7, r=[wk] + [("x1T", a_, g_) for a_ in range(4) for g_ in range(2)], w=[pak])
            for k in range(8):
                self.mm(pg[:], wb_[:, k * 256 + 128:k * 256 + 256], xT[:, k, :], k == 0, k == 7, r=[wk] + [("x1T", a_, g_) for a_ in range(4) for g_ in range(2)], w=[pgk])
            ct, ck = ft[i % 2], ("ftf", i % 2)
            st, sk = ft[2], ("ftf", 2)
            self.act(ct[:], pa[:], AF.Identity, r=[pak, "convw", "convb"], w=[ck], bias=self.convb[:, i:i + 1], scale=cw[:, i, 2:3])
            self.stt(ct[:, 1:512], pa[:, 0:511], cw[:, i, 1:2], ct[:, 1:512], ALU.mult, ALU.add, r=[pak, ck], w=[ck])
            self.stt(ct[:, 2:512], pa[:, 0:510], cw[:, i, 0:1], ct[:, 2:512], ALU.mult, ALU.add, r=[pak, ck], w=[ck])
            self.stt(ct[:, 0:1], halo[:, i, 1:2], cw[:, i, 1:2], ct[:, 0:1], ALU.mult, ALU.add, r=[("halo", i), ck], w=[ck])
            self.stt(ct[:, 0:2], halo[:, i, 0:2], cw[:, i, 0:1], ct[:, 0:2], ALU.mult, ALU.add, r=[("halo", i), ck], w=[ck])
            self.cp("dve", halo[:, i, :], pa[:, 510:512], r=[pak, ck], w=[("halo", i)])
            gs, gk = B.gsb[i % 2], ("gsb", i % 2)
            self.act(gs[:], pg[:], AF.Copy, r=[pgk], w=[gk])
            self.act(st[:], ct[:], AF.Silu, r=[ck], w=[sk])
            self.tt("pool", actT[:, i, :], st[:], gs[:], ALU.mult, r=[sk, gk], w=[("actT", i)])
            yield
        if self.debug and s == 0:
            self.load("sp", self.dbg("act", [128, 22, L], BF16)[:, :, tsl], actT[:], r=[("actT", i) for i in range(22)], w=[("dbgact", ti)])
        for pss in range(2):
            for i in range(22):
                wd, wk = B.s_wdn.get()
                for aa in range(2):
                    a = 2 * pss + aa
                    for nh in range(2):
                        bi = 2 * aa + nh
                        self.kb.op("pe", lambda eng, bi=bi, i=i, a=a, nh=nh, wd=wd: eng.matmul(
                            ps[bi][:], lhsT=actT[:, i, a * 128:(a + 1) * 128], rhs=wd[:, nh * 512:(nh + 1) * 512],
                            start=(i == 0), stop=(i == 21)), reads=[("actT", i), wk], writes=[("ps", bi)], inc=(i == 21 or bi == 3))
                yield
            for aa in range(2):
                a = 2 * pss + aa
                for nh in range(2):
                    bi = 2 * aa + nh
                    xs = xtm[:, a, nh * 512:(nh + 1) * 512]
                    self.stt(xs, xs, ALPHA, ps[bi][:], ALU.mult, ALU.add, r=[("x1", par, a), ("ps", bi)], w=[("x2", par, a, nh)])
                kk = ("x2", par, a)
                kb.reg[kk] = kb.reg[("x2", par, a, 1)]
                self.ln2(xtm[:, a, :], kk, B.lng[:, 1, :], B.lnb[:, 1, :], B.st2, B.mv2, B.sc2, "l2")
                self.load("pool", self.out[s, ti * 512 + a * 128:ti * 512 + (a + 1) * 128, :], xtm[:, a, :], r=[kk, ("xtm", par)], w=[("out", ti, a)])
                yield


def build(**kw):
    return Prog(**kw)


_CACHE = {}


def kernel(**inputs):
    consts = host_consts()
    wts = host_weights(inputs)
    x = np.ascontiguousarray(inputs["x"]).astype(np.float32)
    if "prog" not in _CACHE:
        _CACHE["prog"] = build()
    prog = _CACHE["prog"]
    in_maps = []
    for c in range(8):
        m = dict(consts)
        m.update(wts)
        m["x"] = x[2 * c:2 * c + 2]
        in_maps.append(m)
    res = run_bass_kernel_spmd(prog.nc, in_maps, core_ids=list(range(8)))
    out = np.concatenate([r["out"] for r in res.results], axis=0)
    return out.astype(np.float32)
```

```python
import math
from contextlib import ExitStack
import numpy as np
import ml_dtypes
import concourse.bass as bass
import concourse.mybir as mybir
from concourse.bass_utils import run_bass_kernel_spmd

F32 = mybir.dt.float32
BF16 = mybir.dt.bfloat16
AF = mybir.ActivationFunctionType
ALU = mybir.AluOpType
NPBF = ml_dtypes.bfloat16

ENGS = ("pe", "act", "dve", "pool", "sp")
EPOCH = 12000
NDMA = 8
ALPHA = 2.0 ** 0.25
EPS = 1e-5
L = 4096
NTILE = 8
NSEQ = 2
import os as _os
TSTOP = int(_os.environ.get('TSTOP', '0'))


class KB:
    def __init__(self, nc):
        self.nc = nc
        self.ops = {e: [] for e in ENGS}
        self.cnt = {e: 0 for e in ENGS}
        self.esems = {e: [] for e in ENGS}
        self.seen = {e: {} for e in ENGS}
        self.reg = {}
        self.dsem = {}
        self.dptr = {e: 0 for e in ENGS}
        self.bulk = []
        self.keep = ("wup_s", "wdn_s")

    def new_sem(self, name):
        return self.nc.alloc_semaphore(name=name)

    def _esem(self, e, idx):
        ep = idx // EPOCH
        while len(self.esems[e]) <= ep:
            self.esems[e].append(self.new_sem(f"p_{e}_{len(self.esems[e])}"))
        return self.esems[e][ep], idx - ep * EPOCH

    def _waits_for(self, e, reads, writes):
        need = {}

        def add(t, same_ok):
            if t is None:
                return
            s, v, src = t
            if src == e and same_ok:
                return
            k = id(s)
            if self.seen[e].get(k, 0) >= v:
                return
            if k not in need or need[k][1] < v:
                need[k] = (s, v)

        for r in reads:
            ent = self.reg.get(r)
            if ent:
                add(ent[0], same_ok=(e == "pe"))
        for w in writes:
            ent = self.reg.get(w)
            if ent:
                add(ent[0], same_ok=True)
                for t in ent[1]:
                    add(t, same_ok=True)
        out = []
        for k, (s, v) in need.items():
            self.seen[e][k] = v
            out.append((s, v))
        return out

    def _commit(self, ticket, reads, writes):
        for r in reads:
            self.reg.setdefault(r, [None, []])[1].append(ticket)
        for w in writes:
            self.reg[w] = [ticket, []]

    def op(self, e, fn, reads=(), writes=(), inc=True):
        waits = self._waits_for(e, reads, writes)
        s, local = self._esem(e, self.cnt[e])
        ticket = (s, local + 1, e)
        self._commit(ticket, reads, writes)
        if inc:
            self.cnt[e] += 1
            self.ops[e].append((waits, fn, (s, 1)))
        else:
            self.ops[e].append((waits, fn, None))

    def dma(self, q, out, in_, reads=(), writes=(), bulk=False, **kw):
        if bulk:
            ent = [self.new_sem(f"db_{len(self.bulk)}"), 0]
            self.bulk.append(ent)
        else:
            if q not in self.dsem:
                self.dsem[q] = [[self.new_sem(f"d_{q}_{i}"), 0] for i in range(NDMA)]
            i = self.dptr[q]
            self.dptr[q] = (i + 1) % NDMA
            ent = self.dsem[q][i]
        s, uses = ent
        waits = self._waits_for(q, reads, writes)
        k = id(s)
        if uses > 0 and self.seen[q].get(k, 0) < 16 * uses:
            waits.append((s, 16 * uses))
            self.seen[q][k] = 16 * uses
        ent[1] = uses + 1
        ticket = (s, 16 * (uses + 1), "dma")
        self._commit(ticket, reads, writes)

        def fn(eng, out=out, in_=in_, kw=kw):
            return eng.dma_start(out=out, in_=in_, **kw)

        self.ops[q].append((waits, fn, (s, 16)))

    def barrier(self, final=False):
        ticks = []
        if final:
            for s, uses in self.bulk:
                ticks.append((s, 16 * uses))
        for e in ENGS:
            if self.cnt[e] > 0:
                s, local = self._esem(e, self.cnt[e] - 1)
                ticks.append((s, local + 1))
        for q, lst in self.dsem.items():
            for s, uses in lst:
                if uses > 0:
                    ticks.append((s, 16 * uses))
        for e in ENGS:
            w = []
            for s, v in ticks:
                if self.seen[e].get(id(s), 0) < v:
                    w.append((s, v))
                    self.seen[e][id(s)] = v
            self.ops[e].append((w, None, None))
        self.reg = {k: v for k, v in self.reg.items() if isinstance(k, tuple) and k[0] in self.keep}

    def emit(self):
        nc = self.nc
        with nc.Block() as block:
            def run(e):
                def body(eng):
                    for waits, fn, inc in self.ops[e]:
                        for (s, v) in waits:
                            eng.wait_ge(s, v)
                        if fn is not None:
                            inst = fn(eng)
                            if inc is not None:
                                inst.then_inc(inc[0], inc[1])
                return body
            block.tensor(run("pe"))
            block.scalar(run("act"))
            block.vector(run("dve"))
            block.gpsimd(run("pool"))
            block.sync(run("sp"))
        self.ops = {e: [] for e in ENGS}


def host_consts():
    c = {}
    c["ident_f"] = np.eye(128, dtype=np.float32)
    c["ident_b"] = np.eye(128, dtype=np.float32).astype(NPBF)
    c["ones512"] = np.full((128, 128), 1.0 / 512, np.float32).astype(NPBF)
    c["ones128"] = np.full((128, 128), 1.0 / 128, np.float32).astype(NPBF)
    c["ev17"] = np.tile(np.arange(17, dtype=np.float32)[None], (128, 1))
    c["ev16r"] = np.tile((15 - np.arange(16)).astype(np.float32)[None], (128, 1))
    c["nv32"] = np.tile(np.arange(1, 33, dtype=np.float32)[None], (128, 1))
    freqs = (np.float32(10000.0) ** (-(np.arange(32, dtype=np.float32)) / np.float32(32))).astype(np.float32)
    pos = np.arange(L, dtype=np.float32)
    ang = (pos[None, :] * freqs[:, None]).astype(np.float32).astype(np.float64)
    rope = np.zeros((128, 2, L), np.float32)
    for p in range(128):
        f = p % 32
        hf = (p // 32) % 2
        rope[p, 0] = np.cos(ang[f])
        rope[p, 1] = np.sin(ang[f]) * (-1.0 if hf == 0 else 1.0)
    c["rope"] = rope
    lg = np.log1p(-(2.0 ** (-5.0 - np.arange(4, dtype=np.float64))))
    m = np.arange(128)
    mask = np.zeros((128, 4, 128), np.float64)
    same = (m[:, None] // 64) == (m[None, :] // 64)
    for h in range(4):
        mask[:, h, :] = np.where(same, np.exp(np.abs(m[:, None] - m[None, :]) * lg[h]) / 8.0, 0.0)
    c["maskT"] = mask.astype(np.float32)
    xi = np.zeros((128, 2, 64), np.float64)
    dec = np.zeros((128, 2), np.float64)
    for p in range(128):
        for qt in range(2):
            h = 2 * qt + p // 64
            xi[p, qt] = np.exp((np.arange(64) + 1.0) * lg[h])
            dec[p, qt] = np.exp(64.0 * lg[h])
    c["xi"] = xi.astype(np.float32)
    c["dec"] = dec.astype(np.float32)
    zeta = np.zeros((128, 4), np.float64)
    for h in range(4):
        zeta[:, h] = np.exp((63 - (m % 64)) * lg[h]) / 8.0
    c["zeta"] = zeta.astype(np.float32)
    return c


def host_weights(inp):
    f = np.float32
    w = {}
    w["w_in_f"] = np.ascontiguousarray(inp["w_in"][0].reshape(8, 128, 16, 128).transpose(2, 1, 0, 3)).reshape(16, 128, 1024).astype(f)
    w["w_out_h"] = np.ascontiguousarray(inp["w_out"][0].reshape(8, 128, 8, 128).transpose(2, 1, 0, 3)).reshape(8, 128, 1024).astype(f)
    w["glu_w_r"] = np.ascontiguousarray(inp["s5_glu_w"][0].reshape(4, 128, 512).transpose(1, 0, 2)).astype(f)
    wu = inp["ffn_w_up"][0].reshape(8, 128, 2, 22, 128)
    w["w_up_r"] = np.ascontiguousarray(wu.transpose(3, 1, 0, 2, 4)).reshape(22, 128, 2048).astype(f)
    w["w_down_r"] = np.ascontiguousarray(inp["ffn_w_down"][0].reshape(22, 128, 1024)).astype(f)
    lre, lim, ls = inp["s5_lambda_re"][0], inp["s5_lambda_im"][0], inp["s5_log_step"][0]
    lamc = np.zeros((128, 3, 16), f)
    bcol = np.zeros((128, 2, 16, 32), f)
    ccol = np.zeros((128, 2, 16, 32), f)
    bre, bim, cre, cim = inp["s5_b_re"][0], inp["s5_b_im"][0], inp["s5_c_re"][0], inp["s5_c_im"][0]
    for q in range(16):
        for gi in range(2):
            g = 2 * q + gi
            sl = slice(64 * gi, 64 * gi + 64)
            lamc[sl, 0, q] = lre[g]
            lamc[sl, 1, q] = lim[g]
            lamc[sl, 2, q] = ls[g]
            bcol[sl, 0, q, 16 * gi:16 * gi + 16] = bre[g]
            bcol[sl, 1, q, 16 * gi:16 * gi + 16] = bim[g]
            ccol[sl, 0, q, 16 * gi:16 * gi + 16] = cre[g].T
            ccol[sl, 1, q, 16 * gi:16 * gi + 16] = cim[g].T
    w["lamc"], w["bcol"], w["ccol"] = lamc, bcol, ccol
    lamr = np.zeros((128, 3, 4, 128), f)
    brow = np.zeros((128, 2, 4, 128), f)
    for t in range(4):
        for qi in range(4):
            for gi2 in range(2):
                g2 = 8 * t + 2 * qi + gi2
                cs = slice(64 * gi2, 64 * gi2 + 64)
                rs = slice(32 * qi, 32 * qi + 32)
                lamr[rs, 0, t, cs] = lre[g2][None, :]
                lamr[rs, 1, t, cs] = lim[g2][None, :]
                lamr[rs, 2, t, cs] = ls[g2]
                rr = slice(32 * qi + 16 * gi2, 32 * qi + 16 * gi2 + 16)
                brow[rr, 0, t, cs] = bre[g2].T
                brow[rr, 1, t, cs] = bim[g2].T
    vec = lambda a, n: np.ascontiguousarray(a.reshape(n, 128).T).astype(f)
    w["s5d"] = vec(inp["s5_d"][0], 4)
    w["glub"] = vec(inp["s5_glu_b"][0], 4)
    w["s5gain"] = vec(inp["s5_out_gain"][0], 4)
    w["gngain"] = vec(inp["ret_gn_gain"][0], 4)
    w["convw"] = np.ascontiguousarray(inp["ffn_conv_w"][0].reshape(3, 22, 128).transpose(2, 1, 0)).astype(f)
    w["convb"] = vec(inp["ffn_conv_b"][0], 22)
    lnp = np.stack([inp["ln1_g"][0], inp["ln1_b"][0], inp["ln2_g"][0], inp["ln2_b"][0]], 0)
    w["lnp"] = np.ascontiguousarray(np.broadcast_to(lnp[None], (128, 4, 1024))).astype(f)
    return w


IN_SPECS = {
    "x": ([NSEQ, L, 1024], F32),
    "w_in_f": ([16, 128, 1024], F32), "w_out_h": ([8, 128, 1024], F32), "glu_w_r": ([128, 4, 512], F32),
    "w_up_r": ([22, 128, 2048], F32), "w_down_r": ([22, 128, 1024], F32),
    "lamc": ([128, 3, 16], F32), "bcol": ([128, 2, 16, 32], F32), "ccol": ([128, 2, 16, 32], F32),
    "s5d": ([128, 4], F32), "glub": ([128, 4], F32), "s5gain": ([128, 4], F32), "gngain": ([128, 4], F32),
    "convw": ([128, 22, 3], F32), "convb": ([128, 22], F32), "lnp": ([128, 4, 1024], F32),
    "ident_f": ([128, 128], F32), "ident_b": ([128, 128], BF16), "ones512": ([128, 128], BF16),
    "ones128": ([128, 128], BF16), "ev17": ([128, 17], F32), "ev16r": ([128, 16], F32), "nv32": ([128, 32], F32),
    "rope": ([128, 2, L], F32), "maskT": ([128, 4, 128], F32), "xi": ([128, 2, 64], F32),
    "dec": ([128, 2], F32), "zeta": ([128, 4], F32),
}


class Stream:
    def __init__(self, prog, q, bufs, name, src_fn, ncyc):
        self.p, self.q, self.bufs, self.name, self.src_fn, self.ncyc = prog, q, bufs, name, src_fn, ncyc
        self.issued = 0
        self.consumed = 0

    def _issue(self):
        m = self.issued
        self.issued += 1
        nb = len(self.bufs)
        self.p.load(self.q, self.bufs[m % nb][:], self.src_fn(m % self.ncyc), w=[(self.name, m % nb)])

    def get(self):
        n = self.consumed
        nb = len(self.bufs)
        while self.issued < n + nb:
            self._issue()
        self.consumed += 1
        return self.bufs[n % nb], (self.name, n % nb)


class Prog:
    def __init__(self, stages=("prep", "s5", "tile"), nseq=NSEQ, ntile=NTILE, debug=False):
        self.nc = nc = bass.Bass("TRN2", target_bir_lowering=False)
        self.kb = KB(nc)
        self.debug = debug
        self.I = {}
        for name, (shape, dt) in IN_SPECS.items():
            self.I[name] = nc.dram_tensor(name, shape, dt, kind="ExternalInput").ap()
        self.out = nc.dram_tensor("out", [NSEQ, L, 1024], F32, kind="ExternalOutput").ap()
        self.D = {}
        self.wup_s = nc.dram_tensor("wup_s", [22, 128, 2048], BF16, kind="Internal").ap()
        self.wdn_s = nc.dram_tensor("wdn_s", [22, 128, 1024], BF16, kind="Internal").ap()
        self.win_s = nc.dram_tensor("win_s", [16, 128, 1024], BF16, kind="Internal").ap()
        self.wout_s = nc.dram_tensor("wout_s", [8, 128, 1024], BF16, kind="Internal").ap()
        self.lnb_s = nc.dram_tensor("lnb_s", [128, 2048], BF16, kind="Internal").ap()
        self.kbd_s = nc.dram_tensor("kbd_s", [128, 16 * 4 * 128], BF16, kind="Internal").ap()
        self.wb_s = nc.dram_tensor("wb_s", [128, 16 * 4 * 2 * 128], BF16, kind="Internal").ap()
        self.wc_s = nc.dram_tensor("wc_s", [128, 2 * 16 * 17 * 32], BF16, kind="Internal").ap()
        self.sc_s = nc.dram_tensor("sc_s", [128, 16 * 32 * 3 + 16 * 2], F32, kind="Internal").ap()
        self.ps = [nc.alloc_psum_tensor(f"ps{i}", [128, 512], F32) for i in range(8)]
        self.uid = 0
        with ExitStack() as es:
            self.es_perm = es
            self.alloc_perm()
            if "prep" in stages:
                self.phase_prep()
            for s in range(nseq):
                if "s5" in stages:
                    self.phase_s5(s)
                if "tile" in stages:
                    self.phase_tiles(s, ntile)
            self.kb.barrier(final=True)
            self.kb.emit()

    def sb(self, es, name, shape, dt):
        self.uid += 1
        return es.enter_context(self.nc.sbuf_tensor(f"{name}_{self.uid}", shape, dt))

    def dbg(self, name, shape, dt=F32):
        if name not in self.D:
            self.D[name] = self.nc.dram_tensor("dbg_" + name, shape, dt, kind="ExternalOutput").ap()
        return self.D[name]

    def tt(self, e, out, a, b, op, r, w):
        self.kb.op(e, lambda eng: eng.tensor_tensor(out=out, in0=a, in1=b, op=op), reads=r, writes=w)

    def ts(self, e, out, a, s1, s2, op0, op1, r, w):
        if op1 is None:
            self.kb.op(e, lambda eng: eng.tensor_scalar(out=out, in0=a, scalar1=s1, scalar2=None, op0=op0), reads=r, writes=w)
        else:
            self.kb.op(e, lambda eng: eng.tensor_scalar(out=out, in0=a, scalar1=s1, scalar2=s2, op0=op0, op1=op1), reads=r, writes=w)

    def stt(self, out, a, sc, b, op0, op1, r, w):
        self.kb.op("dve", lambda eng: eng.scalar_tensor_tensor(out=out, in0=a, scalar=sc, in1=b, op0=op0, op1=op1), reads=r, writes=w)

    def act(self, out, in_, func, r, w, bias=None, scale=None):
        kw = {}
        if bias is not None:
            kw["bias"] = bias
        if scale is not None:
            kw["scale"] = scale
        self.kb.op("act", lambda eng: eng.activation(out=out, in_=in_, func=func, **kw), reads=r, writes=w)

    def cp(self, e, out, in_, r, w):
        if e == "act":
            self.kb.op(e, lambda eng: eng.activation(out=out, in_=in_, func=AF.Copy), reads=r, writes=w)
        else:
            self.kb.op(e, lambda eng: eng.tensor_copy(out=out, in_=in_), reads=r, writes=w)

    def mm(self, out, lhsT, rhs, start, stop, r, w, tp=None):
        kw = {}
        if tp is not None:
            kw["tile_position"] = tp
        self.kb.op("pe", lambda eng: eng.matmul(out, lhsT=lhsT, rhs=rhs, start=start, stop=stop, **kw),
                   reads=r, writes=w, inc=stop)

    def tr(self, out, in_, ident, r, w, inc=True):
        self.kb.op("pe", lambda eng: eng.transpose(out, in_, ident), reads=r, writes=w, inc=inc)

    def memset(self, e, ap, val, w):
        self.kb.op(e, lambda eng: eng.memset(ap, val), writes=w)

    def load(self, q, dst, src, w, r=(), bulk=False):
        self.kb.dma(q, dst, src, reads=r, writes=w, bulk=bulk)

    def alloc_perm(self):
        es = self.es_perm
        sb = lambda n, s, d: self.sb(es, n, s, d)
        self.identf = sb("identf", [128, 128], F32)
        self.identb = sb("identb", [128, 128], BF16)
        self.ones512 = sb("ones512", [128, 128], BF16)
        self.ones128 = sb("ones128", [128, 128], BF16)
        self.U = sb("U", [128, 4, L], BF16)
        self.vecs = sb("vecs", [128, 16], F32)
        self.convw = sb("convw", [128, 22, 3], F32)
        self.convb = sb("convb", [128, 22], F32)
        self._eps = sb("epsc", [128, 1], F32)
        self.memset("pool", self._eps[:], EPS, w=["epsc"])
        for dst, nm in ((self.identf, "ident_f"), (self.identb, "ident_b"), (self.ones512, "ones512"),
                        (self.ones128, "ones128"), (self.convw, "convw"), (self.convb, "convb")):
            self.load("sp", dst[:], self.I[nm], w=[nm])
        for i, nm in enumerate(("s5d", "glub", "s5gain", "gngain")):
            self.load("sp", self.vecs[:, 4 * i:4 * i + 4], self.I[nm], w=[nm])

    def g_wcast(self, st):
        if True:
            NB = len(st)
            n = 0
            for i in range(22):
                for (src, dst, cols, key) in ((self.I["w_up_r"], self.wup_s, 2048, "wup_s"), (self.I["w_down_r"], self.wdn_s, 1024, "wdn_s")):
                    b = st[n % NB]
                    bk = ("wst", n % NB)
                    n += 1
                    self.load("pool", b[:, 0:cols], src[i], w=[bk])
                    self.load("sp", dst[i], b[:, 0:cols], r=[bk], w=[(key, i)])
                    yield
            for f in (4, 5, 6, 7, 12, 13, 14, 15):
                b = st[n % NB]
                bk = ("wst", n % NB)
                n += 1
                self.load("pool", b[:, 0:1024], self.I["w_in_f"][f], w=[bk])
                self.load("sp", self.win_s[f], b[:, 0:1024], r=[bk], w=[("win_s", f)])
                yield
            for qn in range(8):
                b = st[n % NB]
                bk = ("wst", n % NB)
                n += 1
                self.load("pool", b[:, 0:1024], self.I["w_out_h"][qn], w=[bk])
                self.load("sp", self.wout_s[qn], b[:, 0:1024], r=[bk], w=[("wout_s", qn)])
                yield
            for j, row in enumerate((1, 3)):
                b = st[n % NB]
                bk = ("wst", n % NB)
                n += 1
                self.load("pool", b[:, 0:1024], self.I["lnp"][:, row, :], w=[bk])
                self.load("sp", self.lnb_s[:, j * 1024:(j + 1) * 1024], b[:, 0:1024], r=[bk], w=[("lnb_s", j)])
                yield

    def wc_pull(self, n):
        g = getattr(self, "_wcg", None)
        if g is None:
            return
        for _ in range(n):
            try:
                next(g)
            except StopIteration:
                self._wcg = None
                return

    def sincos(self, es, tag, ang, shape, s_out, c_out):
        k = self.sb(es, tag + "k", shape, F32)
        r = self.sb(es, tag + "r", shape, F32)
        kk, rr = k[:], r[:]
        M = 12582912.0
        TWO_PI = 2 * math.pi
        C1 = 6.28125
        C2 = float(np.float32(TWO_PI - C1))
        C3 = TWO_PI - C1 - C2
        PIS = 3.1415925
        A = tag + "ang"
        self.ts("dve", kk, ang, 1.0 / TWO_PI, M, ALU.mult, ALU.add, r=[A], w=[tag + "k"])
        self.ts("dve", kk, kk, M, None, ALU.subtract, None, r=[tag + "k"], w=[tag + "k"])
        self.stt(rr, kk, -C1, ang, ALU.mult, ALU.add, r=[A, tag + "k"], w=[tag + "r"])
        self.stt(rr, kk, -C2, rr, ALU.mult, ALU.add, r=[tag + "r", tag + "k"], w=[tag + "r"])
        self.stt(rr, kk, -C3, rr, ALU.mult, ALU.add, r=[tag + "r", tag + "k"], w=[tag + "r"])
        self.ts("dve", rr, rr, -PIS, PIS, ALU.max, ALU.min, r=[tag + "r"], w=[tag + "r"])
        self.act(s_out, rr, AF.Sin, r=[tag + "r"], w=[tag + "s"])
        self.stt(kk, rr, -1.0, rr, ALU.mult, ALU.max, r=[tag + "r"], w=[tag + "k"])
        self.ts("dve", kk, kk, -1.0, math.pi / 2, ALU.mult, ALU.add, r=[tag + "k"], w=[tag + "k"])
        self.act(c_out, kk, AF.Sin, r=[tag + "k"], w=[tag + "c"])

    def cmul(self, e, ore, oim, are, aim, bre, bim, t1, t2, r, w, tag):
        T1, T2 = tag + "t1", tag + "t2"
        self.tt(e, t1, are, bre, ALU.mult, r=r, w=[T1])
        self.tt(e, t2, aim, bim, ALU.mult, r=r, w=[T2])
        self.tt(e, ore, t1, t2, ALU.subtract, r=[T1, T2], w=[w + "re"])
        self.tt(e, t1, are, bim, ALU.mult, r=r, w=[T1])
        self.tt(e, t2, aim, bre, ALU.mult, r=r, w=[T2])
        self.tt(e, oim, t1, t2, ALU.add, r=[T1, T2], w=[w + "im"])

    def lam_basic(self, es, tag, lam3, X):
        sb = lambda n: self.sb(es, tag + n, [128, X], F32)
        dtv, a, th, cre, cim = sb("dtv"), sb("a"), sb("th"), sb("cre"), sb("cim")
        t1, t2, t3, t4 = sb("t1"), sb("t2"), sb("t3"), sb("t4")
        LAM = tag + "lam"
        self.act(dtv[:], lam3[:, 2, :], AF.Exp, r=[LAM], w=[tag + "dtv"])
        self.tt("dve", a[:], lam3[:, 0, :], dtv[:], ALU.mult, r=[LAM, tag + "dtv"], w=[tag + "a"])
        self.tt("dve", th[:], lam3[:, 1, :], dtv[:], ALU.mult, r=[LAM, tag + "dtv"], w=[tag + "th"])
        mag, s1, c1 = sb("mag"), sb("s1"), sb("c1")
        self.act(mag[:], a[:], AF.Exp, r=[tag + "a"], w=[tag + "mag"])
        with ExitStack() as es2:
            self.ts("dve", t1[:], th[:], 1.0, None, ALU.mult, None, r=[tag + "th"], w=[tag + "lbang"])
            self.sincos(es2, tag + "lb", t1[:], [128, X], s1[:], c1[:])
        self.tt("dve", t1[:], mag[:], c1[:], ALU.mult, r=[tag + "mag", tag + "lbc"], w=[tag + "nr"])
        self.ts("dve", t1[:], t1[:], -1.0, None, ALU.add, None, r=[tag + "nr"], w=[tag + "nr"])
        self.tt("dve", t2[:], mag[:], s1[:], ALU.mult, r=[tag + "mag", tag + "lbs"], w=[tag + "ni"])
        self.tt("dve", t3[:], lam3[:, 0, :], lam3[:, 0, :], ALU.mult, r=[LAM], w=[tag + "d1"])
        self.tt("dve", t4[:], lam3[:, 1, :], lam3[:, 1, :], ALU.mult, r=[LAM], w=[tag + "d2"])
        self.tt("dve", t3[:], t3[:], t4[:], ALU.add, r=[tag + "d1", tag + "d2"], w=[tag + "d1"])
        self.kb.op("dve", lambda eng: eng.reciprocal(out=t3[:], in_=t3[:]), reads=[tag + "d1"], writes=[tag + "d1"])
        self.tt("dve", cre[:], t1[:], lam3[:, 0, :], ALU.mult, r=[tag + "nr", LAM], w=[tag + "cre"])
        self.tt("dve", t4[:], t2[:], lam3[:, 1, :], ALU.mult, r=[tag + "ni", LAM, tag + "d1"], w=[tag + "d2"])
        self.tt("dve", cre[:], cre[:], t4[:], ALU.add, r=[tag + "cre", tag + "d2"], w=[tag + "cre"])
        self.tt("dve", cre[:], cre[:], t3[:], ALU.mult, r=[tag + "cre", tag + "d1"], w=[tag + "cre"])
        self.tt("dve", cim[:], t2[:], lam3[:, 0, :], ALU.mult, r=[tag + "ni", LAM], w=[tag + "cim"])
        self.tt("dve", t4[:], t1[:], lam3[:, 1, :], ALU.mult, r=[tag + "nr", LAM, tag + "cre"], w=[tag + "d2"])
        self.tt("dve", cim[:], cim[:], t4[:], ALU.subtract, r=[tag + "cim", tag + "d2"], w=[tag + "cim"])
        self.tt("dve", cim[:], cim[:], t3[:], ALU.mult, r=[tag + "cim", tag + "d1"], w=[tag + "cim"])
        return a, th, cre, cim

    def powers(self, es, tag, a_ap, th_ap, ev_ap, X, E, pwre, pwim, rin):
        shp = [128, X, E]
        arg = self.sb(es, tag + "arg", shp, F32)
        ang = self.sb(es, tag + "ang", shp, F32)
        sn = self.sb(es, tag + "sn", shp, F32)
        cs = self.sb(es, tag + "cs", shp, F32)
        abc = a_ap.unsqueeze(2).to_broadcast(shp)
        tbc = th_ap.unsqueeze(2).to_broadcast(shp)
        ebc = ev_ap.unsqueeze(1).to_broadcast(shp)
        self.tt("dve", arg[:], abc, ebc, ALU.mult, r=rin, w=[tag + "arg"])
        self.act(arg[:], arg[:], AF.Exp, r=[tag + "arg"], w=[tag + "arg"])
        self.tt("dve", ang[:], tbc, ebc, ALU.mult, r=rin, w=[tag + "ang"])
        self.sincos(es, tag, ang[:], shp, sn[:], cs[:])
        self.tt("dve", pwre, arg[:], cs[:], ALU.mult, r=[tag + "arg", tag + "c"], w=[tag + "pwre"])
        self.tt("dve", pwim, arg[:], sn[:], ALU.mult, r=[tag + "arg", tag + "s"], w=[tag + "pwim"])

    def phase_prep(self):
        kb = self.kb
        with ExitStack() as es:
            sb = lambda n, s, d=F32: self.sb(es, n, s, d)
            ev17, ev16r, nv32 = sb("ev17", [128, 17]), sb("ev16r", [128, 16]), sb("nv32", [128, 32])
            self.load("sp", ev17[:], self.I["ev17"], w=["ev17"])
            self.load("sp", ev16r[:], self.I["ev16r"], w=["ev16r"])
            self.load("sp", nv32[:], self.I["nv32"], w=["nv32"])
            lamc = sb("lamc", [128, 3, 16])
            bcol = sb("bcol", [128, 2, 16, 32])
            ccol = sb("ccol", [128, 2, 16, 32])
            self.load("sp", lamc[:], self.I["lamc"], w=["Clam"])
            self.load("sp", bcol[:], self.I["bcol"], w=["bcol"])
            self.load("sp", ccol[:], self.I["ccol"], w=["ccol"])
            kbd = sb("kbd", [128, 16, 4, 128], BF16)
            self.memset("dve", kbd[:], 0.0, w=["kbd"])
            a, th, cre, cim = self.lam_basic(es, "C", lamc, 16)
            pwre, pwim = sb("cpwre", [128, 16, 17]), sb("cpwim", [128, 16, 17])
            with ExitStack() as es2:
                self.powers(es2, "Cp", a[:], th[:], ev17[:], 16, 17, pwre[:], pwim[:], rin=["Ca", "Cth", "ev17"])
            sct = sb("sct", [128, 16 * 32 * 3 + 32])
            zc = sct[:, 0:512].rearrange("p (q n) -> p q n", n=32)
            zs = sct[:, 512:1024].rearrange("p (q n) -> p q n", n=32)
            rho = sct[:, 1024:1536].rearrange("p (q n) -> p q n", n=32)
            zend = sct[:, 1536:1568].rearrange("p (r q) -> p r q", q=16)
            phi = sb("phi", [128, 16])
            self.ts("dve", phi[:], th[:], 16.0, None, ALU.mult, None, r=["Cth"], w=["phi"])
            with ExitStack() as es2:
                zang = self.sb(es2, "zang", [128, 16, 32], F32)
                self.tt("dve", zang[:], phi[:].unsqueeze(2).to_broadcast([128, 16, 32]),
                        nv32[:].unsqueeze(1).to_broadcast([128, 16, 32]), ALU.mult, r=["phi", "nv32"], w=["Zang"])
                self.sincos(es2, "Z", zang[:], [128, 16, 32], zs, zc)
            rh = sb("rh", [128, 16])
            self.act(rh[:], a[:], AF.Exp, r=["Ca"], w=["rh"], scale=16.0)
            self.cp("dve", rho, rh[:].unsqueeze(2).to_broadcast([128, 16, 32]), r=["rh"], w=["rho"])
            self.cp("dve", zend[:, 0, :], zc[:, :, 31], r=["Zc"], w=["zend0"])
            self.cp("dve", zend[:, 1, :], zs[:, :, 31], r=["Zs"], w=["zend1"])
            self.load("sp", self.sc_s, sct[:], w=["sc_s"], r=["Zc", "Zs", "rho", "zend0", "zend1"])
            t1, t2 = sb("ct1", [128, 4, 17, 32]), sb("ct2", [128, 4, 17, 32])
            s1, s2 = sb("cs1", [128, 16, 32]), sb("cs2", [128, 16, 32])
            bb = sb("bb", [128, 2, 16, 32], BF16)
            bbf = sb("bbf", [128, 2, 16, 32])
            cbc = lambda x: x[:].unsqueeze(2).to_broadcast([128, 16, 32])
            self.cmul("dve", bbf[:, 0], bbf[:, 1], bcol[:, 0], bcol[:, 1], cbc(cre), cbc(cim),
                      s1[:], s2[:], r=["bcol", "Ccre", "Ccim"], w="bbf", tag="bbm")
            self.cp("dve", bb[:, 0], bbf[:, 0], r=["bbfre"], w=["bbre"])
            self.cp("dve", bb[:, 1], bbf[:, 1], r=["bbfim"], w=["bbim"])
            wbk = sb("wbk", [128, 2, 16, 4, 32])
            wb = sb("wb", [128, 16, 4, 2, 128], BF16)
            cntb = 0
            for t in range(4):
                shb = [128, 4, 16, 32]
                br = bbf[:, 0, 4 * t:4 * t + 4, :].unsqueeze(2).to_broadcast(shb)
                bi = bbf[:, 1, 4 * t:4 * t + 4, :].unsqueeze(2).to_broadcast(shb)
                pr = pwre[:, 4 * t:4 * t + 4, 0:16].unsqueeze(3).to_broadcast(shb)
                pi = pwim[:, 4 * t:4 * t + 4, 0:16].unsqueeze(3).to_broadcast(shb)
                u1 = t1[:, :, 0:16, :]
                u2 = t2[:, :, 0:16, :]
                P = ["bbfre", "bbfim", "Cppwre", "Cppwim"]
                o_re = wbk[:, 0].rearrange("p k q c -> p q k c")
                o_im = wbk[:, 1].rearrange("p k q c -> p q k c")
                self.tt("dve", u1, br, pr, ALU.mult, r=P, w=["wt1"])
                self.tt("pool", u2, bi, pi, ALU.mult, r=P, w=["wt2"])
                self.tt("dve", o_re, u1, u2, ALU.subtract, r=["wt1", "wt2"], w=[("wbk", 0)])
                self.tt("dve", u1, br, pi, ALU.mult, r=P, w=["wt1"])
                self.tt("pool", u2, bi, pr, ALU.mult, r=P, w=["wt2"])
                self.tt("dve", o_im, u1, u2, ALU.add, r=["wt1", "wt2"], w=[("wbk", 1)])
                for ri in range(2):
                    for jg in range(4):
                        bnk = 4 + cntb % 4
                        cntb += 1
                        pbk = self.ps[bnk]
                        for jj in range(4):
                            j = 4 * jg + jj
                            self.tr(pbk[:, jj * 128:(jj + 1) * 128], wbk[:, ri, 15 - j].rearrange("p q c -> p (q c)"), self.identf[:],
                                    r=[("wbk", ri), "ident_f"], w=[("ps", bnk)], inc=(jj == 3))
                        eng = "act" if cntb % 2 == 0 else "dve"
                        self.cp(eng, wb[:, 4 * jg:4 * jg + 4, t, ri, :], pbk[:].rearrange("p (j c) -> p j c", c=128),
                                r=[("ps", bnk)], w=[("wb", t, ri, jg)])
            self.load("sp", self.wb_s, wb[:].rearrange("p j t r c -> p (j t r c)"), w=["wb_s"],
                      r=[("wb", t, ri, jg) for t in range(4) for ri in range(2) for jg in range(4)])
            wc = sb("wc", [128, 2, 16, 17, 32], BF16)
            P = ["ccol", "Cppwre", "Cppwim"]
            for qg in range(4):
                qs = slice(4 * qg, 4 * qg + 4)
                shp = [128, 4, 17, 32]
                cr_bc = ccol[:, 0, qs].unsqueeze(2).to_broadcast(shp)
                ci_bc = ccol[:, 1, qs].unsqueeze(2).to_broadcast(shp)
                pr_bc = pwre[:, qs, :].unsqueeze(3).to_broadcast(shp)
                pi_bc = pwim[:, qs, :].unsqueeze(3).to_broadcast(shp)
                self.tt("dve", t1[:], cr_bc, pr_bc, ALU.mult, r=P, w=["wt1"])
                self.tt("pool", t2[:], ci_bc, pi_bc, ALU.mult, r=P, w=["wt2"])
                self.tt("dve", wc[:, 0, qs], t1[:], t2[:], ALU.subtract, r=["wt1", "wt2"], w=["wcre"])
                self.tt("dve", t1[:], cr_bc, pi_bc, ALU.mult, r=P, w=["wt1"])
                self.tt("pool", t2[:], ci_bc, pr_bc, ALU.mult, r=P, w=["wt2"])
                self.stt(wc[:, 1, qs], t1[:], -1.0, t2[:], ALU.mult, ALU.subtract, r=["wt1", "wt2"], w=["wcim"])
            self.load("sp", self.wc_s, wc[:].rearrange("p a q k c -> p (a q k c)"), w=["wc_s"], r=["wcre", "wcim"])
            for t in range(4):
                pk = self.ps[t]
                for qi in range(4):
                    q = 4 * t + qi
                    o = pk[32 * qi:32 * qi + 32, :]
                    self.mm(o, bb[:, 0, q, :], wc[:, 0, q, 0:16, :], True, False, r=["bbre", "wcre"], w=[("ps", t)], tp=(0, 32 * qi))
                    self.mm(o, bb[:, 1, q, :], wc[:, 1, q, 0:16, :], False, True, r=["bbim", "wcim"], w=[("ps", t)], tp=(0, 32 * qi))
                    self.cp("act", kbd[32 * qi:32 * qi + 32, :, t, 32 * qi:32 * qi + 32],
                            o.rearrange("p (k c) -> p k c", c=32), r=[("ps", t), "kbd"], w=["kbd_%d_%d" % (t, qi)])
                deps = ["kbd"] + ["kbd_%d_%d" % (t, qi) for qi in range(4)]
                self.stt(kbd[:, 0, t, :], self.identf[:], self.vecs[:, t:t + 1], kbd[:, 0, t, :], ALU.mult, ALU.add,
                         r=deps + ["ident_f", "s5d"], w=["kbdD%d" % t])
            self.load("sp", self.kbd_s, kbd[:].rearrange("p k t c -> p (k t c)"), w=["kbd_s"],
                      r=["kbd"] + ["kbdD%d" % t for t in range(4)] + ["kbd_%d_%d" % (t, qi) for t in range(4) for qi in range(4)])
            if self.debug:
                self.load("sp", self.dbg("kbd", [128, 16 * 4 * 128], BF16), kbd[:].rearrange("p k t c -> p (k t c)"), w=["dbgk"],
                          r=["kbd"] + ["kbdD%d" % t for t in range(4)])
                self.load("sp", self.dbg("pw", [128, 2, 16 * 17]), pwre[:].rearrange("p q k -> p (q k)"), w=["dbgp"], r=["Cppwre"]) if False else None
            kb.barrier()
            kb.emit()

    def phase_s5(self, s):
        kb = self.kb
        I = self.I
        U = self.U
        if not getattr(self, "_wc_started", False):
            self._wc_started = True
            self._es_wc = ExitStack()
            st = [self.sb(self._es_wc, "wst%d" % i, [128, 2048], BF16) for i in range(4)]
            self._wcg = self.g_wcast(st)
        with ExitStack() as es:
            sb = lambda n, sh, d=F32: self.sb(es, n, sh, d)
            winu = sb("winu", [128, 8, 512], BF16)
            for f in range(4):
                self.load("pool", winu[:, :, f * 128:(f + 1) * 128], I["w_in_f"][f].rearrange("p (k c) -> p k c", c=128), w=[("winu", f)])
            xtm = [sb("xtm%d" % i, [128, 4, 1024]) for i in range(2)]
            xT = [sb("xT%d" % i, [128, 8, 512], BF16) for i in range(2)]
            for ti in range(NTILE):
                b = ti % 2
                self.load("sp", xtm[b][:], I["x"][s, ti * 512:(ti + 1) * 512, :].rearrange("(a p) d -> p a d", p=128), w=[("xtm", b)])
                self.transpose_x(xtm[b], xT[b], ("xtm", b), ("xT", b), ti)
                for t in range(4):
                    pb = self.ps[4 + (ti * 4 + t) % 4]
                    key = ("ps", 4 + (ti * 4 + t) % 4)
                    for k in range(8):
                        self.mm(pb[:], winu[:, k, t * 128:(t + 1) * 128], xT[b][:, k, :], k == 0, k == 7,
                                r=[("winu", t), ("xT", b)], w=[key])
                    self.act(U[:, t, ti * 512:(ti + 1) * 512], pb[:], AF.Copy, r=[key], w=[("U", t, ti)])
                self.wc_pull(2)
            if self.debug and s == 0:
                self.load("sp", self.dbg("uT", [128, 4 * L], BF16), U[:].rearrange("p t l -> p (t l)"), w=["dbgu"],
                          r=[("U", t, ti) for t in range(4) for ti in range(NTILE)])
            kb.barrier()
            kb.emit()
        with ExitStack() as es:
            sb = lambda n, sh, d=F32: self.sb(es, n, sh, d)
            sprev = sb("sprev", [128, 16, 2, 257], BF16)
            esBC = ExitStack()
            sloc = self.sb(esBC, "sloc", [128, 16, 2, 256], F32)
            with ExitStack() as esB:
                wb = self.sb(esB, "wb", [128, 16, 4, 2, 128], BF16)
                self.load("sp", wb[:], self.wb_s.rearrange("p (j t r c) -> p j t r c", j=16, t=4, r=2), w=["wb"])
                it = 0
                for t in range(4):
                    self.wc_pull(3)
                    uv = U[:, t, :].rearrange("p (n j) -> p j n", j=16)
                    for ri in range(2):
                        base = 4 * (it % 2)
                        it += 1
                        for j in range(16):
                            for qi in range(4):
                                rs = slice(32 * qi, 32 * qi + 32)
                                self.mm(self.ps[base + qi][:, 0:256], wb[rs, j, t, ri, :], uv[rs, j, :], j == 0, j == 15,
                                        r=["wb", "Uall"], w=[("ps", base + qi)], tp=(32 * qi, 0))
                        for qi in range(4):
                            eng = "act" if qi % 2 == 0 else "dve"
                            self.cp(eng, sloc[:, 4 * t + qi, ri, :], self.ps[base + qi][:, 0:256],
                                    r=[("ps", base + qi)], w=[("sloc", 4 * t + qi, ri)])
                kb.barrier()
                kb.emit()
            with ExitStack() as esC:
                sbc = lambda n, sh, d=F32: self.sb(esC, n, sh, d)
                sct = sbc("sct", [128, 16 * 32 * 3 + 32])
                self.load("sp", sct[:], self.sc_s, w=["sct"])
                zc = sct[:, 0:512].rearrange("p (q n) -> p q n", n=32)
                zs = sct[:, 512:1024].rearrange("p (q n) -> p q n", n=32)
                rho = sct[:, 1024:1536].rearrange("p (q n) -> p q n", n=32)
                zend = sct[:, 1536:1568].rearrange("p (r q) -> p r q", q=16)
                mod = sbc("mod", [128, 16, 2, 256])
                t1 = sbc("t1", [128, 16, 256])
                t2 = sbc("t2", [128, 16, 256])
                shp = [128, 16, 8, 32]
                v4 = lambda ap: ap.rearrange("p q (g n) -> p q g n", n=32)
                zcb = zc.unsqueeze(2).to_broadcast(shp)
                zsb = zs.unsqueeze(2).to_broadcast(shp)
                SL = [("sloc", q, r_) for q in range(16) for r_ in range(2)]
                lre, lim = v4(sloc[:, :, 0, :]), v4(sloc[:, :, 1, :])
                self.tt("dve", v4(t1[:]), lre, zcb, ALU.mult, r=SL + ["sct"], w=["ct1"])
                self.tt("pool", v4(t2[:]), lim, zsb, ALU.mult, r=SL + ["sct"], w=["ct2"])
                self.tt("dve", mod[:, :, 0, :], t1[:], t2[:], ALU.add, r=["ct1", "ct2"], w=["modre"])
                self.tt("pool", v4(t1[:]), lim, zcb, ALU.mult, r=SL + ["sct", "modre"], w=["ct1"])
                self.tt("pool", v4(t2[:]), lre, zsb, ALU.mult, r=SL + ["sct", "modre"], w=["ct2"])
                self.tt("dve", mod[:, :, 1, :], t1[:], t2[:], ALU.subtract, r=["ct1", "ct2"], w=["modim"])
                carry = sbc("carry", [128, 2, 16])
                ctmp = sbc("ctmp", [128, 4, 16])
                self.memset("dve", carry[:], 0.0, w=["carry"])
                for g in range(8):
                    for q in range(16):
                        for ri in range(2):
                            o = sloc[:, q, ri, g * 32:(g + 1) * 32]
                            d1 = mod[:, q, ri, g * 32:(g + 1) * 32]
                            ini = carry[:, ri, q:q + 1]
                            d0 = rho[:, q, :]
                            self.kb.op("dve", lambda eng, o=o, d0=d0, d1=d1, ini=ini: eng.tensor_tensor_scan(
                                out=o, data0=d0, data1=d1, initial=ini, op0=ALU.mult, op1=ALU.add),
                                reads=["modre", "modim", "carry", "sct"], writes=[("R", g)])
                    if g < 7:
                        rre = sloc[:, :, 0, g * 32 + 31]
                        rim = sloc[:, :, 1, g * 32 + 31]
                        self.tt("dve", ctmp[:, 0, :], rre, zend[:, 0, :], ALU.mult, r=[("R", g), "sct"], w=["cta"])
                        self.tt("dve", ctmp[:, 1, :], rim, zend[:, 1, :], ALU.mult, r=[("R", g), "sct"], w=["ctb"])
                        self.tt("dve", ctmp[:, 2, :], rre, zend[:, 1, :], ALU.mult, r=[("R", g), "sct"], w=["ctc"])
                        self.tt("dve", ctmp[:, 3, :], rim, zend[:, 0, :], ALU.mult, r=[("R", g), "sct"], w=["ctd"])
                        self.tt("dve", carry[:, 0, :], ctmp[:, 0, :], ctmp[:, 1, :], ALU.subtract, r=["cta", "ctb"], w=["carry"])
                        self.tt("dve", carry[:, 1, :], ctmp[:, 2, :], ctmp[:, 3, :], ALU.add, r=["ctc", "ctd", "carry"], w=["carry"])
                RR = [("R", g) for g in range(8)]
                self.memset("pool", sprev[:, :, :, 0:1], 0.0, w=["sprev0"])
                rre, rim = v4(sloc[:, :, 0, :]), v4(sloc[:, :, 1, :])
                self.tt("dve", v4(t1[:]), rre, zcb, ALU.mult, r=RR + ["sct"], w=["ct1"])
                self.tt("pool", v4(t2[:]), rim, zsb, ALU.mult, r=RR + ["sct"], w=["ct2"])
                self.tt("dve", sprev[:, :, 0, 1:257], t1[:], t2[:], ALU.subtract, r=["ct1", "ct2"], w=["sprevre"])
                self.tt("dve", v4(t1[:]), rre, zsb, ALU.mult, r=RR + ["sct", "sprevre"], w=["ct1"])
                self.tt("pool", v4(t2[:]), rim, zcb, ALU.mult, r=RR + ["sct", "sprevre"], w=["ct2"])
                self.tt("dve", sprev[:, :, 1, 1:257], t1[:], t2[:], ALU.add, r=["ct1", "ct2"], w=["sprevim"])
                if self.debug and s == 0:
                    self.load("sp", self.dbg("sprev", [128, 16 * 2 * 257], BF16), sprev[:].rearrange("p q r n -> p (q r n)"),
                              w=["dbgs"], r=["sprev0", "sprevre", "sprevim"])
                kb.barrier()
                kb.emit()
            esBC.close()
            with ExitStack() as esD:
                sbd = lambda n, sh, d=F32: self.sb(esD, n, sh, d)
                kbd = sbd("kbd", [128, 16, 4, 128], BF16)
                wc = sbd("wc", [128, 2, 16, 17, 32], BF16)
                yg = sbd("yg", [128, 4, L], BF16)
                ycr = [sbd("ycr%d" % i, [128, 16, 256], BF16) for i in range(2)]
                ysum = [sbd("ysum%d" % i, [128, 512]) for i in range(2)]
                self.load("sp", kbd[:], self.kbd_s.rearrange("p (k t c) -> p k t c", k=16, t=4), w=["kbd"])
                self.load("sp", wc[:], self.wc_s.rearrange("p (a q k c) -> p a q k c", a=2, q=16, k=17), w=["wc"])
                wglu = sbd("wglu", [128, 4, 512], BF16)
                self.load("pool", wglu[:], I["glu_w_r"], w=["wglu"])
                sig = [sbd("sig%d" % i, [128, 512]) for i in range(2)]
                sv = [sbd("sv%d" % i, [128, 4, 512]) for i in range(2)]
                sq = [sbd("sq%d" % i, [128, 4, 512], BF16) for i in range(2)]
                rstd = [sbd("rstd%d" % i, [128, 512]) for i in range(2)]

                def stageE(ti):
                    b = ti % 2
                    tsl = slice(ti * 512, (ti + 1) * 512)
                    for to in range(4):
                        pi_ = (ti * 4 + to) % 2
                        pb = self.ps[pi_]
                        for tin in range(4):
                            self.mm(pb[:], wglu[:, tin, to * 128:(to + 1) * 128], yg[:, tin, tsl], tin == 0, tin == 3,
                                    r=["wglu"] + [("yg", tin, ti)], w=[("ps", pi_)])
                        sg = sig[(ti * 4 + to) % 2]
                        sk = ("sig", (ti * 4 + to) % 2)
                        self.act(sg[:], pb[:], AF.Sigmoid, r=[("ps", pi_), "glub"], w=[sk], bias=self.vecs[:, 4 + to:5 + to])
                        self.tt("dve", sv[b][:, to, :], sg[:], yg[:, to, tsl], ALU.mult, r=[sk, ("yg", to, ti)], w=[("sv", b, to)])
                        self.tt("pool", sq[b][:, to, :], sv[b][:, to, :], sv[b][:, to, :], ALU.mult, r=[("sv", b, to)], w=[("sq", b, to)])
                def stageE2(ti):
                    b = ti % 2
                    tsl = slice(ti * 512, (ti + 1) * 512)
                    pmi = 6 + ti % 2
                    pm = self.ps[pmi]
                    for to in range(4):
                        self.mm(pm[:], self.ones512[:], sq[b][:, to, :], to == 0, to == 3, r=["ones512", ("sq", b, to)], w=[("ps", pmi)])
                    self.act(rstd[b][:], pm[:], AF.Ln, r=[("ps", pmi)], w=[("rstd", b)], bias=self.eps_ap())
                    self.act(rstd[b][:], rstd[b][:], AF.Exp, r=[("rstd", b)], w=[("rstd", b)], scale=-0.5)
                    for to in range(4):
                        self.stt(U[:, to, tsl], sv[b][:, to, :], self.vecs[:, 8 + to:9 + to], rstd[b][:], ALU.mult, ALU.mult,
                                 r=[("sv", b, to), ("rstd", b), "s5gain"], w=[("ys5", to, ti), ("U", to, ti)])

                cnt = 0
                for t in range(4):
                    self.wc_pull(9 if t < 3 else 100)
                    yc = ycr[t % 2]
                    for jp in range(8):
                        pb = self.ps[jp % 2]
                        for jj in range(2):
                            j = 2 * jp + jj
                            for qi in range(4):
                                q = 4 * t + qi
                                o = pb[32 * qi:32 * qi + 32, jj * 256:(jj + 1) * 256]
                                self.mm(o, wc[:, 0, q, j + 1, :], sprev[:, q, 0, 0:256], True, False,
                                        r=["wc", "sprev"], w=[("ps", jp % 2)], tp=(0, 32 * qi))
                                self.mm(o, wc[:, 1, q, j + 1, :], sprev[:, q, 1, 0:256], False, True,
                                        r=["wc", "sprev"], w=[("ps", jp % 2)], tp=(0, 32 * qi))
                        self.act(yc[:, 2 * jp:2 * jp + 2, :], pb[:].rearrange("p (j n) -> p j n", n=256), AF.Copy,
                                 r=[("ps", jp % 2)], w=[("ycr", t % 2)])
                    for ti in range(NTILE):
                        pb = self.ps[2 + cnt % 4]
                        pk = ("ps", 2 + cnt % 4)
                        ys = ysum[cnt % 2]
                        yk = ("ysum", cnt % 2)
                        cnt += 1
                        pv = pb[:].rearrange("p (c j) -> p c j", j=16)
                        uv = U[:, t, ti * 512:(ti + 1) * 512].rearrange("p (c j) -> p c j", j=16)
                        for tau in range(16):
                            self.mm(pv[:, :, tau:16], kbd[:, tau, t, :], uv[:, :, 0:16 - tau], tau == 0, tau == 15,
                                    r=["kbd", ("U", t, ti)], w=[pk])
                        ycv = yc[:, :, ti * 32:(ti + 1) * 32].rearrange("p j n -> p n j")
                        self.tt("dve", ys[:].rearrange("p (c j) -> p c j", j=16), pv, ycv, ALU.add,
                                r=[pk, ("ycr", t % 2)], w=[yk])
                        self.act(yg[:, t, ti * 512:(ti + 1) * 512], ys[:], AF.Gelu_apprx_tanh, r=[yk], w=[("yg", t, ti)])
                        if self.debug and s == 0:
                            self.load("sp", self.dbg("ylin", [128, 4, L])[:, t, ti * 512:(ti + 1) * 512], ys[:], r=[yk], w=[("dbgy", t, ti)])
                        if t == 3:
                            stageE(ti)
                            if ti > 0:
                                stageE2(ti - 1)
                            if ti == NTILE - 1:
                                stageE2(ti)
                if self.debug and s == 0:
                    self.load("sp", self.dbg("ys5", [128, 4 * L], BF16), U[:].rearrange("p t l -> p (t l)"), w=["dbgys5"],
                              r=[("ys5", to, ti) for to in range(4) for ti in range(NTILE)])
                kb.barrier()
                kb.emit()

        if getattr(self, "_es_wc", None) is not None:
            self.wc_pull(1000)
            self._es_wc.close()
            self._es_wc = None

    def eps_ap(self):
        return self._eps[:]

    def transpose_x(self, src, dst, skey, dkey, par):
        for k in range(8):
            pb = self.ps[(par * 8 + k) % 4]
            pk = ("ps", (par * 8 + k) % 4)
            for a in range(4):
                self.tr(pb[:, a * 128:(a + 1) * 128], src[:, a, k * 128:(k + 1) * 128], self.identf[:],
                        r=(list(skey) if isinstance(skey, list) else [skey]) + ["ident_f"], w=[pk], inc=(a == 3))
            if k % 2 == 0:
                self.act(dst[:, k, :], pb[:], AF.Copy, r=[pk], w=[dkey])
            else:
                self.cp("dve", dst[:, k, :], pb[:], r=[pk], w=[dkey])

    def layer_norm(self, xt, a, key, lnp, gi, stats, mv, sc):
        x = xt[:, a, :]
        for c in range(2):
            self.kb.op("dve", lambda eng, c=c: eng.bn_stats(out=stats[:, c, :], in_=xt[:, a, c * 512:(c + 1) * 512]),
                       reads=[key], writes=[("lnst", c)])
        self.kb.op("dve", lambda eng: eng.bn_aggr(out=mv[:], in_=stats[:].rearrange("p c s -> p (c s)")),
                   reads=[("lnst", 0), ("lnst", 1)], writes=["lnmv"])
        self.act(sc[:, 0:1], mv[:, 1:2], AF.Sqrt, r=["lnmv", "epsc"], w=["lnsc0"], bias=self.eps_ap())
        self.kb.op("dve", lambda eng: eng.reciprocal(out=sc[:, 0:1], in_=sc[:, 0:1]), reads=["lnsc0"], writes=["lnsc0"])
        self.stt(sc[:, 1:2], mv[:, 0:1], -1.0, sc[:, 0:1], ALU.mult, ALU.mult, r=["lnmv", "lnsc0"], w=["lnsc1"])
        self.act(x, x, AF.Identity, r=[key, "lnsc0", "lnsc1"], w=[key], bias=sc[:, 1:2], scale=sc[:, 0:1])
        self.tt("pool", x, x, lnp[:, gi, :], ALU.mult, r=[key, "lnp"], w=[key])
        self.tt("pool", x, x, lnp[:, gi + 1, :], ALU.add, r=[key, "lnp"], w=[key])

    def bankM(self):
        self._bm = (getattr(self, "_bm", 7) + 1 - 4) % 4 + 4
        return self._bm

    def ln2(self, x, key, g_ap, b_ap, stats, mv, sc, tag):
        for c in range(2):
            self.kb.op("dve", lambda eng, c=c: eng.bn_stats(out=stats[:, c, :], in_=x[:, c * 512:(c + 1) * 512]),
                       reads=[key], writes=[(tag + "st", c)])
        self.kb.op("dve", lambda eng: eng.bn_aggr(out=mv[:], in_=stats[:].rearrange("p c s -> p (c s)")),
                   reads=[(tag + "st", 0), (tag + "st", 1)], writes=[tag + "mv"])
        self.act(sc[:, 0:1], mv[:, 1:2], AF.Sqrt, r=[tag + "mv", "epsc"], w=[tag + "sc0"], bias=self.eps_ap())
        self.kb.op("dve", lambda eng: eng.reciprocal(out=sc[:, 0:1], in_=sc[:, 0:1]), reads=[tag + "sc0"], writes=[tag + "sc0"])
        self.stt(sc[:, 1:2], mv[:, 0:1], -1.0, sc[:, 0:1], ALU.mult, ALU.mult, r=[tag + "mv", tag + "sc0"], w=[tag + "sc1"])
        self.act(x, x, AF.Identity, r=[key, tag + "sc0", tag + "sc1"], w=[key], bias=sc[:, 1:2], scale=sc[:, 0:1])
        self.tt("pool", x, x, g_ap, ALU.mult, r=[key, "lng", "lng1"], w=[key])
        self.tt("pool", x, x, b_ap, ALU.add, r=[key, "lnb"], w=[key])

    def transpose_a(self, src, dst, a, skeys, dkey):
        for kg in range(2):
            b = self.bankM()
            pb, pk = self.ps[b], ("ps", b)
            for kk in range(4):
                k = 4 * kg + kk
                self.tr(pb[:, kk * 128:(kk + 1) * 128], src[:, a, k * 128:(k + 1) * 128], self.identf[:],
                        r=list(skeys) + ["ident_f"], w=[pk], inc=(kk == 3))
            dv = dst[:, 4 * kg:4 * kg + 4, a * 128:(a + 1) * 128]
            self.act(dv, pb[:].rearrange("p (k t) -> p k t", t=128), AF.Copy, r=[pk], w=[(dkey, a, kg)])

    def phase_tiles(self, s, ntile):
        kb, I, U, ps = self.kb, self.I, self.U, self.ps
        with ExitStack() as es:
            sb = lambda n, sh, d=F32: self.sb(es, n, sh, d)
            B = type("B", (), {})()
            B.wv = sb("wv", [128, 8, 512], BF16)
            B.winb = [sb("winb%d" % i, [128, 1024], BF16) for i in range(3)]
            B.woutb = [sb("woutb%d" % i, [128, 1024], BF16) for i in range(4)]
            B.lng = sb("lng", [128, 2, 1024])
            B.lnb = sb("lnb", [128, 2, 1024], BF16)
            B.maskT, B.xi, B.zeta, B.dec = sb("maskT", [128, 4, 128]), sb("xi", [128, 2, 64]), sb("zeta", [128, 4]), sb("dec", [128, 2])
            for f in range(4):
                self.load("pool", B.wv[:, :, f * 128:(f + 1) * 128], I["w_in_f"][8 + f].rearrange("p (k c) -> p k c", c=128), w=[("wv", f)])
            self.load("sp", B.lng[:, 0, :], I["lnp"][:, 0, :], w=["lng"])
            self.load("sp", B.lng[:, 1, :], I["lnp"][:, 2, :], w=["lng1"])
            self.load("sp", B.lnb[:].rearrange("p a d -> p (a d)"), self.lnb_s, w=["lnb"])
            for t_, nm in ((B.maskT, "maskT"), (B.xi, "xi"), (B.zeta, "zeta"), (B.dec, "dec")):
                self.load("sp", t_[:], I[nm], w=[nm])
            B.state = sb("state", [128, 2, 128])
            B.sbf = sb("sbf", [128, 8, 2, 128], BF16)
            B.qxz = sb("qxz", [128, 4, 512], BF16)
            B.halo = sb("halo", [128, 22, 2])
            self.memset("pool", B.state[:], 0.0, w=["state"])
            self.memset("pool", B.qxz[:], 0.0, w=[("qxz", h) for h in range(4)])
            self.memset("pool", B.halo[:], 0.0, w=[("halo", i) for i in range(22)])
            B.xtm = [sb("xtm%d" % i, [128, 4, 1024]) for i in range(2)]
            B.xTa = sb("xTa", [128, 8, 512], BF16)
            B.x1T = sb("x1T", [128, 8, 512], BF16)
            B.rope = sb("rope", [128, 2, 512])
            B.qT, B.kT = sb("qT", [128, 2, 512], BF16), sb("kT", [128, 2, 512], BF16)
            B.vtm = sb("vtm", [128, 4, 512], BF16)
            B.kz = sb("kz", [128, 4, 256], BF16)
            B.sgh = [sb("sgh%d" % i, [128, 512], BF16) for i in range(2)]
            B.sT = sb("sT", [128, 4, 4, 128], BF16)
            B.ftm = [sb("ftm%d" % i, [128, 512]) for i in range(2)]
            B.ftf = [sb("ftf%d" % i, [128, 512]) for i in range(3)]
            B.gsb = [sb("gsb%d" % i, [128, 512], BF16) for i in range(2)]
            B.obf, B.osq = sb("obf", [128, 512], BF16), sb("osq", [128, 512], BF16)
            B.yret = sb("yret", [128, 4, 512], BF16)
            B.actT = sb("actT", [128, 22, 512], BF16)
            B.wupb = [sb("wupb%d" % i, [128, 2048], BF16) for i in range(3)]
            B.wdnb = [sb("wdnb%d" % i, [128, 1024], BF16) for i in range(3)]
            B.st1, B.mv1, B.sc1 = sb("lnst1", [128, 2, 6]), sb("lnmv1", [128, 2]), sb("lnsc1", [128, 2])
            B.st2, B.mv2, B.sc2 = sb("lnst2", [128, 2, 6]), sb("lnmv2", [128, 2]), sb("lnsc2", [128, 2])
            if not hasattr(self, "_printed"):
                print("SBUF remaining in tile phase:", self.nc.sbuf_bytes_remaining)
                self._printed = True
            WINF = (4, 5, 6, 7, 12, 13, 14, 15)
            B.s_win = Stream(self, "pool", B.winb, "winb", lambda c: self.win_s[WINF[c]], 8)
            B.s_wout = Stream(self, "pool", B.woutb, "woutb", lambda c: self.wout_s[c], 8)
            B.s_wup = Stream(self, "sp", B.wupb, "wupb", lambda c: self.wup_s[c], 22)
            B.s_wdn = Stream(self, "sp", B.wdnb, "wdnb", lambda c: self.wdn_s[c], 22)
            self._B = B

            def drain(g):
                for _ in g:
                    pass

            def interleave(ga, gb, na=62.0, nb=62.0):
                da = db = False
                ca = cb = 0
                while not (da and db):
                    if not da and (db or ca / na <= cb / nb):
                        try:
                            next(ga)
                            ca += 1
                        except StopIteration:
                            da = True
                    elif not db:
                        try:
                            next(gb)
                            cb += 1
                        except StopIteration:
                            db = True

            drain(self.g_mixer(s, 0))
            for ti in range(ntile):
                if ti + 1 < ntile:
                    interleave(self.g_ffn(s, ti), self.g_mixer(s, ti + 1))
                else:
                    drain(self.g_ffn(s, ti))
            kb.barrier()
            kb.emit()

    def win_chunk(self, f):
        wb_, wk = self._B.s_win.get()
        return wb_[:].rearrange("p (k c) -> p k c", c=128), wk

    def g_mixer(self, s, ti):
        kb, I, U, ps, B = self.kb, self.I, self.U, self.ps, self._B
        par = ti % 2
        xtm, XK = B.xtm[par], ("xtm", par)
        xT, rope, qT, kT, vtm, kz, sT, ft = B.xTa, B.rope, B.qT, B.kT, B.vtm, B.kz, B.sT, B.ftm
        state, sbf, qxz, yret, obf, osq = B.state, B.sbf, B.qxz, B.yret, B.obf, B.osq
        maskT, xi, zeta, dec = B.maskT, B.xi, B.zeta, B.dec
        gn = self.vecs[:, 12:16]
        tsl = slice(ti * 512, (ti + 1) * 512)
        self.load("pool", xtm[:], I["x"][s, tsl, :].rearrange("(a p) d -> p a d", p=128), w=[XK])
        self.load("pool", rope[:], I["rope"][:, :, tsl], w=["rope"])
        for _ in range(5):
            yield
        for a in range(4):
            self.transpose_a(xtm, xT, a, [XK], "xTa")
            yield
        for f in range(4):
            wch, wk = self.win_chunk(4 + f)
            b = self.bankM()
            pb, pk = ps[b], ("ps", b)
            for k in range(8):
                self.mm(pb[:], wch[:, k, :], xT[:, k, :], k == 0, k == 7, r=[wk] + [("xTa", a_, g_) for a_ in range(4) for g_ in range(2)], w=[pk])
            self.tt("dve", ft[0][:], pb[:], rope[:, 0, :], ALU.mult, r=[pk, "rope"], w=[("ftm", 0)])
            for qd in range(4):
                src = (qd ^ 1) * 32
                self.tt("dve", ft[1][qd * 32:qd * 32 + 32, :], pb[src:src + 32, :], rope[qd * 32:qd * 32 + 32, 1, :], ALU.mult,
                        r=[pk, "rope"], w=[("ftm", 1)])
            dst = qT[:, f, :] if f < 2 else kT[:, f - 2, :]
            dk_ = ("qT", f) if f < 2 else ("kT", f - 2)
            self.tt("pool", dst, ft[0][:], ft[1][:], ALU.add, r=[("ftm", 0), ("ftm", 1)], w=[dk_])
            if f < 2:
                for hp in range(2):
                    h = 2 * f + hp
                    sl = slice(64 * hp, 64 * hp + 64)
                    self.tt("pool", qxz[sl, h, :].rearrange("p (n c) -> p n c", c=64), qT[sl, f, :].rearrange("p (n c) -> p n c", c=64),
                            xi[sl, f, :].unsqueeze(1).to_broadcast([64, 8, 64]), ALU.mult, r=[dk_, "xi"], w=[("qxz", h)])
            yield
        for a in range(4):
            b = self.bankM()
            pb, pk = ps[b], ("ps", b)
            for k in range(8):
                self.mm(pb[:], xT[:, k, a * 128:(a + 1) * 128], B.wv[:, k, :], k == 0, k == 7, r=[("wv", f_) for f_ in range(4)] + [("xTa", a, 0), ("xTa", a, 1)], w=[pk])
            self.cp("act", vtm[:, a, :], pb[:], r=[pk], w=[("vtm", a)])
            yield
        for a in range(4):
            asl = slice(a * 128, (a + 1) * 128)
            b = self.bankM()
            pbb, pk = ps[b][:].bitcast(BF16), ("ps", b)
            for kt in range(2):
                self.tr(pbb[:, kt * 128:(kt + 1) * 128], kT[:, kt, asl], self.identb[:], r=[("kT", kt), "ident_b"], w=[pk], inc=(kt == 1))
            self.tt("dve", kz[:, a, :].rearrange("p (h d) -> p h d", d=64), pbb[:, 0:256].rearrange("p (h d) -> p h d", d=64),
                    zeta[:].unsqueeze(2).to_broadcast([128, 4, 64]), ALU.mult, r=[pk, "zeta"], w=[("kz", a)])
            for hp in range(2):
                hb = 64 * hp
                b = self.bankM()
                pb, pk = ps[b], ("ps", b)
                for hh in range(2):
                    self.kb.op("pe", lambda eng, pb=pb, hb=hb, hh=hh, asl=asl: eng.matmul(
                        pb[:, hh * 128:(hh + 1) * 128], lhsT=kT[hb:hb + 64, hh, asl], rhs=qT[hb:hb + 64, hh, asl],
                        start=True, stop=True, tile_position=(hb, 0)),
                        reads=[("kT", hh), ("qT", hh)], writes=[pk], inc=(hh == 1))
                sv_ = sT[:, a, :, :].rearrange("p (hh par) c -> p par hh c", par=2)[:, hp]
                mv_ = maskT[:].rearrange("p (hh par) c -> p par hh c", par=2)[:, hp]
                self.tt("dve", sv_, pb[:, 0:256].rearrange("p (h c) -> p h c", c=128), mv_, ALU.mult, r=[pk, "maskT"], w=[("sT", a, hp)])
            yield
        if ti == 0:
            self.memset("pool", state[:], 0.0, w=["state"])
        for n in range(8):
            a, tb = n // 2, 64 * (n % 2)
            self.cp("act", sbf[:, n, :, :], state[:], r=["state"], w=[("sbf", n)])
            b = self.bankM()
            pb, pk = ps[b], ("ps", b)
            for h in range(4):
                hb, hh = 64 * (h % 2), h // 2
                self.kb.op("pe", lambda eng, pb=pb, h=h, hb=hb, hh=hh, a=a, tb=tb: eng.matmul(
                    pb[hb:hb + 64, hh * 128:(hh + 1) * 128], lhsT=kz[tb:tb + 64, a, h * 64:(h + 1) * 64],
                    rhs=vtm[tb:tb + 64, a, h * 128:(h + 1) * 128], start=True, stop=True, tile_position=(tb, hb)),
                    reads=[("kz", a), ("vtm", a)], writes=[pk], inc=(h == 3))
            for hh in range(2):
                self.stt(state[:, hh, :], state[:, hh, :], dec[:, hh:hh + 1], pb[:, hh * 128:(hh + 1) * 128], ALU.mult, ALU.add,
                         r=["state", pk, "dec", ("sbf", n)], w=["state"])
            if n % 2 == 1:
                yield
        for h in range(4):
            hh = h // 2
            wch, wk = self.win_chunk(12 + h)
            b = self.bankM()
            pb, pk = ps[b], ("ps", b)
            for k in range(8):
                self.mm(pb[:], wch[:, k, :], xT[:, k, :], k == 0, k == 7, r=[wk] + [("xTa", a_, g_) for a_ in range(4) for g_ in range(2)], w=[pk])
            sg, sgk = B.sgh[h % 2], ("sgh", h % 2)
            self.act(sg[:], pb[:], AF.Silu, r=[pk], w=[sgk])
            yield
            b = self.bankM()
            po, pk = ps[b], ("ps", b)
            for a in range(4):
                self.kb.op("pe", lambda eng, po=po, a=a, h=h: eng.matmul(
                    po[:, a * 128:(a + 1) * 128], lhsT=vtm[:, a, h * 128:(h + 1) * 128], rhs=sT[:, a, h, :], start=True, stop=False),
                    reads=[("vtm", a), ("sT", a, h % 2)], writes=[pk], inc=False)
                for half in range(2):
                    n = 2 * a + half
                    self.kb.op("pe", lambda eng, po=po, a=a, half=half, n=n, h=h, hh=hh: eng.matmul(
                        po[:, a * 128 + 64 * half:a * 128 + 64 * half + 64], lhsT=sbf[:, n, hh, :],
                        rhs=qxz[:, h, n * 64:(n + 1) * 64], start=False, stop=(half == 1)),
                        reads=[("sbf", n), ("qxz", h)], writes=[pk], inc=(a == 3 and half == 1))
            self.act(obf[:], po[:], AF.Copy, r=[pk], w=["obf"])
            self.act(osq[:], po[:], AF.Square, r=[pk], w=["osq"])
            self.act(ft[0][:], po[:], AF.Copy, r=[pk], w=[("ftm", 0)])
            yield
            b1 = self.bankM()
            pm, pmk = ps[b1], ("ps", b1)
            self.mm(pm[:], self.ones128[:], obf[:], True, True, r=["ones128", "obf"], w=[pmk])
            b2 = self.bankM()
            pq, pqk = ps[b2], ("ps", b2)
            self.mm(pq[:], self.ones128[:], osq[:], True, True, r=["ones128", "osq"], w=[pqk])
            self.act(ft[1][:], pm[:], AF.Square, r=[pmk], w=[("ftm", 1)])
            self.tt("dve", ft[1][:], pq[:], ft[1][:], ALU.subtract, r=[pqk, ("ftm", 1)], w=[("ftm", 1)])
            self.act(ft[1][:], ft[1][:], AF.Ln, r=[("ftm", 1), "epsc"], w=[("ftm", 1)], bias=self.eps_ap())
            self.act(ft[1][:], ft[1][:], AF.Exp, r=[("ftm", 1)], w=[("ftm", 1)], scale=-0.5)
            self.tt("dve", ft[0][:], ft[0][:], pm[:], ALU.subtract, r=[("ftm", 0), pmk, "obf"], w=[("ftm", 0)])
            self.tt("pool", ft[0][:], ft[0][:], ft[1][:], ALU.mult, r=[("ftm", 0), ("ftm", 1)], w=[("ftm", 0)])
            self.stt(yret[:, h, :], ft[0][:], gn[:, h:h + 1], sg[:], ALU.mult, ALU.mult,
                     r=[("ftm", 0), sgk, "gngain"], w=[("yret", h)])
            yield
        if self.debug and s == 0:
            self.load("sp", self.dbg("yret", [128, 4, L], BF16)[:, :, tsl], yret[:], r=[("yret", h) for h in range(4)], w=[("dbgyr", ti)])
        for qn in range(8):
            wo_, wok = B.s_wout.get()
            wo = wo_[:].rearrange("p (k c) -> p k c", c=128)
            for a in range(4):
                b = self.bankM()
                pb, pk = ps[b], ("ps", b)
                for k in range(8):
                    lhsT = U[:, k, ti * 512 + a * 128:ti * 512 + (a + 1) * 128] if k < 4 else yret[:, k - 4, a * 128:(a + 1) * 128]
                    rk = "Uall" if k < 4 else ("yret", k - 4)
                    self.mm(pb[:, 0:128], lhsT, wo[:, k, :], k == 0, k == 7, r=[wok, rk], w=[pk])
                xs = xtm[:, a, qn * 128:(qn + 1) * 128]
                self.stt(xs, xs, ALPHA, pb[:, 0:128], ALU.mult, ALU.add, r=[XK, pk], w=[("x1", par, a, qn)])
            if qn % 2 == 1:
                yield
                yield
        def ln1(a):
            kk = ("x1", par, a)
            kb.reg[kk] = kb.reg[("x1", par, a, 7)]
            self.ln2(xtm[:, a, :], kk, B.lng[:, 0, :], B.lnb[:, 0, :], B.st1, B.mv1, B.sc1, "l1")
        ln1(0)
        yield
        ln1(1)
        for _ in range(4):
            yield
        for a in range(4):
            if a + 2 < 4:
                ln1(a + 2)
            for _ in range(3):
                yield
            self.transpose_a(xtm, B.x1T, a, [("x1", par, a)], "x1T")
            yield
        if self.debug and s == 0:
            self.load("sp", self.dbg("x1", [L, 1024])[tsl, :].rearrange("(a p) d -> p a d", p=128), xtm[:],
                      r=[("x1", par, a) for a in range(4)], w=[("dbgx1", ti)])

    def g_ffn(self, s, ti):
        kb, I, ps, B = self.kb, self.I, self.ps, self._B
        par = ti % 2
        xtm = B.xtm[par]
        xT, actT, halo, ft = B.x1T, B.actT, B.halo, B.ftf
        tsl = slice(ti * 512, (ti + 1) * 512)
        cw = self.convw
        for i in range(22):
            wb_, wk = B.s_wup.get()
            ba, bg = (0, 1) if i % 2 == 0 else (2, 3)
            pa, pg = ps[ba], ps[bg]
            pak, pgk = ("ps", ba), ("ps", bg)
            for k in range(8):
                self.mm(pa[:], wb_[:, k * 256:k * 256 + 128], xT[:, k, :], k == 0, k == 7, r=[wk] + [("x1T", a_, g_) for a_ in range(4) for g_ in range(2)], w=[pak])
            for k in range(8):
                self.mm(pg[:], wb_[:, k * 256 + 128:k * 256 + 256], xT[:, k, :], k == 0, k == 7, r=[wk] + [("x1T", a_, g_) for a_ in range(4) for g_ in range(2)], w=[pgk])
            ct, ck = ft[i % 2], ("ftf", i % 2)
            st, sk = ft[2], ("ftf", 2)
            self.act(ct[:], pa[:], AF.Identity, r=[pak, "convw", "convb"], w=[ck], bias=self.convb[:, i:i + 1], scale=cw[:, i, 2:3])
            self.stt(ct[:, 1:512], pa[:, 0:511], cw[:, i, 1:2], ct[:, 1:512], ALU.mult, ALU.add, r=[pak, ck], w=[ck])
            self.stt(ct[:, 2:512], pa[:, 0:510], cw[:, i, 0:1], ct[:, 2:512], ALU.mult, ALU.add, r=[pak, ck], w=[ck])
            self.stt(ct[:, 0:1], halo[:, i, 1:2], cw[:, i, 1:2], ct[:, 0:1], ALU.mult, ALU.add, r=[("halo", i), ck], w=[ck])
            self.stt(ct[:, 0:2], halo[:, i, 0:2], cw[:, i, 0:1], ct[:, 0:2], ALU.mult, ALU.add, r=[("halo", i), ck], w=[ck])
            self.cp("dve", halo[:, i, :], pa[:, 510:512], r=[pak, ck], w=[("halo", i)])
            gs, gk = B.gsb[i % 2], ("gsb", i % 2)
            self.act(gs[:], pg[:], AF.Copy, r=[pgk], w=[gk])
            self.act(st[:], ct[:], AF.Silu, r=[ck], w=[sk])
            self.tt("pool", actT[:, i, :], st[:], gs[:], ALU.mult, r=[sk, gk], w=[("actT", i)])
            yield
        if self.debug and s == 0:
            self.load("sp", self.dbg("act", [128, 22, L], BF16)[:, :, tsl], actT[:], r=[("actT", i) for i in range(22)], w=[("dbgact", ti)])
        for pss in range(2):
            for i in range(22):
                wd, wk = B.s_wdn.get()
                for aa in range(2):
                    a = 2 * pss + aa
                    for nh in range(2):
                        bi = 2 * aa + nh
                        self.kb.op("pe", lambda eng, bi=bi, i=i, a=a, nh=nh, wd=wd: eng.matmul(
                            ps[bi][:], lhsT=actT[:, i, a * 128:(a + 1) * 128], rhs=wd[:, nh * 512:(nh + 1) * 512],
                            start=(i == 0), stop=(i == 21)), reads=[("actT", i), wk], writes=[("ps", bi)], inc=(i == 21 or bi == 3))
                yield
            for aa in range(2):
                a = 2 * pss + aa
                for nh in range(2):
                    bi = 2 * aa + nh
                    xs = xtm[:, a, nh * 512:(nh + 1) * 512]
                    self.stt(xs, xs, ALPHA, ps[bi][:], ALU.mult, ALU.add, r=[("x1", par, a), ("ps", bi)], w=[("x2", par, a, nh)])
                kk = ("x2", par, a)
                kb.reg[kk] = kb.reg[("x2", par, a, 1)]
                self.ln2(xtm[:, a, :], kk, B.lng[:, 1, :], B.lnb[:, 1, :], B.st2, B.mv2, B.sc2, "l2")
                self.load("pool", self.out[s, ti * 512 + a * 128:ti * 512 + (a + 1) * 128, :], xtm[:, a, :], r=[kk, ("xtm", par)], w=[("out", ti, a)])
                yield


def build(**kw):
    return Prog(**kw)


_CACHE = {}


def kernel(**inputs):
    consts = host_consts()
    wts = host_weights(inputs)
    x = np.ascontiguousarray(inputs["x"]).astype(np.float32)
    if "prog" not in _CACHE:
        _CACHE["prog"] = build()
    prog = _CACHE["prog"]
    in_maps = []
    for c in range(8):
        m = dict(consts)
        m.update(wts)
        m["x"] = x[2 * c:2 * c + 2]
        in_maps.append(m)
    res = run_bass_kernel_spmd(prog.nc, in_maps, core_ids=list(range(8)))
    out = np.concatenate([r["out"] for r in res.results], axis=0)
    return out.astype(np.float32)
```
